# Optimizing a Trainium2 kernel written in Bass

```python
import math
import jax, jax.numpy as jnp
from jax import lax
import numpy as np

D_MODEL = 1024
BATCH = 32
SEQ = 256
DEPTH = 4
DEC_BATCH = 2
DEC_SEQ = 2048
PAST_LEN = 512

GRID_W = 64
Q_BLOCK = 128
D_FF = 2816
N_MOD = 9
EPS = 1e-6
ROPE_BASE = 10000.0
A_HEADS = 6
A_KV_HEADS = 2
A_HEAD_DIM = 64
B_HEADS = 4
B_KEY_DIM = 32
B_VAL_DIM = 64
B_GATE_RANK = 16
B_GATE_TAU = 16.0
B_CHUNK = 64
C_HEADS = 4
C_QK_DIM = 48
C_VAL_DIM = 2 * C_QK_DIM

A_WIDTH = A_HEADS * A_HEAD_DIM
B_WIDTH = B_HEADS * B_VAL_DIM
C_WIDTH = C_HEADS * C_VAL_DIM
MIX_WIDTH = A_WIDTH + B_WIDTH + C_WIDTH
PROJ_SPLITS = (A_WIDTH, A_KV_HEADS * A_HEAD_DIM, A_KV_HEADS * A_HEAD_DIM,
               B_HEADS * B_KEY_DIM, B_HEADS * B_KEY_DIM, B_WIDTH, 2 * B_GATE_RANK, B_WIDTH,
               2 * C_HEADS * C_QK_DIM, 2 * C_HEADS * C_QK_DIM, C_WIDTH)
PROJ_WIDTH = sum(PROJ_SPLITS)

kernel_name = 'hybrid_dit_prefix_ctx_step'


def rms_norm(x, g):
    xf = x.astype(jnp.float32)
    y = xf * lax.rsqrt(jnp.mean(xf * xf, axis=-1, keepdims=True) + EPS)
    return (y * g.astype(jnp.float32)).astype(x.dtype)


def modulate(x, shift, scale):
    return x * (1 + scale[:, None, :]) + shift[:, None, :]


def swiglu(h, wg, wu, wd):
    return (jax.nn.silu(h @ wg) * (h @ wu)) @ wd


def rope_1d(x, pos):
    m = x.shape[-1] // 2
    freqs = ROPE_BASE ** (-jnp.arange(m, dtype=jnp.float32) / m)
    ang = pos[:, None] * freqs[None, :]
    cs = jnp.cos(ang)[:, None, :].astype(x.dtype)
    sn = jnp.sin(ang)[:, None, :].astype(x.dtype)
    a, b = x[..., :m], x[..., m:]
    return jnp.concatenate([a * cs - b * sn, a * sn + b * cs], axis=-1)


def rope_2d(x, row, col):
    h = x.shape[-1] // 2
    return jnp.concatenate([rope_1d(x[..., :h], row), rope_1d(x[..., h:], col)], axis=-1)


def map_query_blocks(fn, q):
    B, T = q.shape[:2]
    nb = T // Q_BLOCK
    qb = jnp.moveaxis(q.reshape((B, nb, Q_BLOCK) + q.shape[2:]), 1, 0)
    out = jnp.moveaxis(lax.map(fn, qb), 0, 1)
    return out.reshape((B, T) + out.shape[3:])


def gqa_attention(q, k, v):
    scale = A_HEAD_DIM ** -0.5
    def block(qb):
        s = jnp.einsum('bqhgd,bkhd->bhgqk', qb, k).astype(jnp.float32) * scale
        p = jax.nn.softmax(s, axis=-1).astype(v.dtype)
        return jnp.einsum('bhgqk,bkhd->bqhgd', p, v)
    return map_query_blocks(block, q)


def diff_attention(q, k, v, lam):
    scale = C_QK_DIM ** -0.5
    def block(qb):
        s = jnp.einsum('bqhmd,bkhmd->bhmqk', qb, k).astype(jnp.float32) * scale
        p = jax.nn.softmax(s, axis=-1)
        w = (p[:, :, 0] - lam * p[:, :, 1]).astype(v.dtype)
        return jnp.einsum('bhqk,bkhd->bqhd', w, v)
    return map_query_blocks(block, q)


def gla_scan(q, k, v, log_a, s0):
    B, T, H, dk = q.shape
    dv = v.shape[-1]
    n = T // B_CHUNK
    def chunks(x):
        x = x.astype(jnp.float32).reshape(B, n, B_CHUNK, H, x.shape[-1])
        return jnp.transpose(x, (1, 0, 3, 2, 4))
    qc = chunks(q) * (dk ** -0.5)
    kc, vc, lc = chunks(k), chunks(v), chunks(log_a)
    b = jnp.cumsum(lc, axis=3)
    b_last = b[:, :, :, -1:, :]
    mask = jnp.tril(jnp.ones((B_CHUNK, B_CHUNK), dtype=bool))[:, :, None]
    diff = b[..., :, None, :] - b[..., None, :, :]
    decay = jnp.where(mask, jnp.exp(jnp.where(mask, diff, 0.0)), 0.0)
    scores = jnp.einsum('nbhid,nbhjd,nbhijd->nbhij', qc, kc, decay)
    o_intra = jnp.einsum('nbhij,nbhjv->nbhiv', scores, vc)
    q_dec = qc * jnp.exp(b)
    k_dec = kc * jnp.exp(b_last - b)
    a_last = jnp.exp(b_last[:, :, :, 0, :])
    def step(S, xs):
        qd, kd, vv, al = xs
        o = jnp.einsum('bhid,bhdv->bhiv', qd, S)
        S = al[..., None] * S + jnp.einsum('bhjd,bhjv->bhdv', kd, vv)
        return S, o
    s_fin, o_inter = lax.scan(step, s0.astype(jnp.float32), (q_dec, k_dec, vc, a_last))
    o = jnp.transpose(o_intra + o_inter, (1, 0, 3, 2, 4)).reshape(B, T, H, dv)
    return o.astype(v.dtype), s_fin.astype(v.dtype)


def gla_bidir(q, k, v, log_a_f, log_a_b, s0_f, s0_b):
    o_f, s_f = gla_scan(q, k, v, log_a_f, s0_f)
    flip = lambda x: jnp.flip(x, axis=1)
    o_b, s_b = gla_scan(flip(q), flip(k), flip(v), flip(log_a_b), s0_b)
    return o_f + flip(o_b), s_f, s_b


def token_mix(h, lp, lam_init, ctx, pos):
    B, T, _ = h.shape
    offs = [int(o) for o in np.cumsum(PROJ_SPLITS)[:-1]]
    a_q, a_k, a_v, b_q, b_k, b_v, b_g, b_r, c_q, c_k, c_v = jnp.split(h @ lp['w_in'], offs, axis=-1)
    a_q = rms_norm(a_q.reshape(B, T, A_HEADS, A_HEAD_DIM), lp['g_a_q'])
    a_k = rms_norm(a_k.reshape(B, T, A_KV_HEADS, A_HEAD_DIM), lp['g_a_k'])
    a_v = a_v.reshape(B, T, A_KV_HEADS, A_HEAD_DIM)
    c_q = rms_norm(c_q.reshape(B, T, C_HEADS, 2, C_QK_DIM), lp['g_c_q'])
    c_k = rms_norm(c_k.reshape(B, T, C_HEADS, 2, C_QK_DIM), lp['g_c_k'])
    c_v = c_v.reshape(B, T, C_HEADS, C_VAL_DIM)
    b_q = b_q.reshape(B, T, B_HEADS, B_KEY_DIM)
    b_k = b_k.reshape(B, T, B_HEADS, B_KEY_DIM)
    b_v = b_v.reshape(B, T, B_HEADS, B_VAL_DIM)
    z = jnp.einsum('btsr,srk->btsk', b_g.reshape(B, T, 2, B_GATE_RANK), lp['w_gla_up']) + lp['b_gla']
    log_a = (jax.nn.log_sigmoid(z.astype(jnp.float32)) / B_GATE_TAU).reshape(B, T, 2, B_HEADS, B_KEY_DIM)
    if ctx is None:
        keys_a, vals_a, keys_c, vals_c = a_k, a_v, c_k, c_v
        s0 = jnp.zeros((B, 2, B_HEADS, B_KEY_DIM, B_VAL_DIM), jnp.float32)
    else:
        row, col = pos
        ctx_a_k, ctx_a_v, ctx_c_k, ctx_c_v, s0 = ctx
        L = ctx_a_k.shape[1]
        a_q = rope_2d(a_q, row, col)
        keys_a = jnp.concatenate([rope_2d(a_k, row, col), ctx_a_k], axis=1)
        vals_a = jnp.concatenate([a_v, ctx_a_v], axis=1)
        c_q = rope_2d(c_q.reshape(B, T, 2 * C_HEADS, C_QK_DIM), row, col).reshape(B, T, C_HEADS, 2, C_QK_DIM)
        c_k_lat = rope_2d(c_k.reshape(B, T, 2 * C_HEADS, C_QK_DIM), row, col).reshape(B, T, C_HEADS, 2, C_QK_DIM)
        keys_c = jnp.concatenate([c_k_lat, ctx_c_k.reshape(B, L, C_HEADS, 2, C_QK_DIM)], axis=1)
        vals_c = jnp.concatenate([c_v, ctx_c_v], axis=1)
    out_a = gqa_attention(a_q.reshape(B, T, A_KV_HEADS, A_HEADS // A_KV_HEADS, A_HEAD_DIM),
                          keys_a, vals_a).reshape(B, T, A_WIDTH)
    o_b, s_f, s_b = gla_bidir(b_q, b_k, b_v, log_a[:, :, 0], log_a[:, :, 1], s0[:, 0], s0[:, 1])
    out_b = (rms_norm(o_b, lp['g_gla']) * jax.nn.silu(b_r).reshape(B, T, B_HEADS, B_VAL_DIM)).reshape(B, T, B_WIDTH)
    lq1, lk1, lq2, lk2 = lp['lam_c'].astype(jnp.float32)
    lam = jnp.exp(jnp.sum(lq1 * lk1)) - jnp.exp(jnp.sum(lq2 * lk2)) + lam_init
    out_c = diff_attention(c_q, keys_c, vals_c, lam)
    out_c = (rms_norm(out_c, lp['g_c_out']) * (1.0 - lam_init)).reshape(B, T, C_WIDTH)
    out = jnp.concatenate([out_a, out_b, out_c], axis=-1) @ lp['w_out']
    if ctx is None:
        new_ctx = (a_k, a_v, c_k.reshape(B, T, C_HEADS, 2 * C_QK_DIM), c_v, jnp.stack([s_f, s_b], axis=1))
        return out, new_ctx
    return out, None


def adaln(cond, lp):
    return (jax.nn.silu(cond) @ lp['w_ada'] + lp['b_ada']).reshape(cond.shape[0], N_MOD, D_MODEL)


def trunk_layer(x, mod, lp, lam_init, ctx=None, pos=None):
    sh1, sc1, g1, sh2, sc2, g2, sh3, sc3, g3 = (mod[:, i] for i in range(N_MOD))
    h = modulate(rms_norm(x, lp['g_norm'][0]), sh1, sc1)
    x = x + 0.5 * g1[:, None, :] * swiglu(h, lp['w_ffn_gate'][0], lp['w_ffn_up'][0], lp['w_ffn_down'][0])
    h = modulate(rms_norm(x, lp['g_norm'][1]), sh2, sc2)
    mix, new_ctx = token_mix(h, lp, lam_init, ctx, pos)
    x = x + g2[:, None, :] * mix
    h = modulate(rms_norm(x, lp['g_norm'][2]), sh3, sc3)
    x = x + 0.5 * g3[:, None, :] * swiglu(h, lp['w_ffn_gate'][1], lp['w_ffn_up'][1], lp['w_ffn_down'][1])
    return x, new_ctx


def setup_inputs(seed: int = 0) -> dict:
    key = jax.random.key(seed)
    ks = jax.random.split(key, 32)
    nrm = lambda k, shape, s: jax.random.normal(k, shape, jnp.float32) * s
    gain = lambda k, shape: 1.0 + 0.02 * jax.random.normal(k, shape, jnp.float32)
    return {
        'x_prompt': nrm(ks[0], (BATCH, SEQ, D_MODEL), 1.0),
        'x_sample': nrm(ks[1], (DEC_BATCH, DEC_SEQ, D_MODEL), 1.0),
        'c': nrm(ks[2], (DEC_BATCH, D_MODEL), 1.0),
        'cache_a_k': nrm(ks[3], (DEC_BATCH, DEPTH, PAST_LEN, A_KV_HEADS, A_HEAD_DIM), 1.0),
        'cache_a_v': nrm(ks[4], (DEC_BATCH, DEPTH, PAST_LEN, A_KV_HEADS, A_HEAD_DIM), 1.0),
        'cache_c_k': nrm(ks[5], (DEC_BATCH, DEPTH, PAST_LEN, C_HEADS, 2 * C_QK_DIM), 1.0),
        'cache_c_v': nrm(ks[6], (DEC_BATCH, DEPTH, PAST_LEN, C_HEADS, C_VAL_DIM), 1.0),
        'state_gla': nrm(ks[7], (DEC_BATCH, DEPTH, 2, B_HEADS, B_KEY_DIM, B_VAL_DIM), 2.0),
        'c_ctx': nrm(ks[8], (D_MODEL,), 1.0),
        'w_ada': nrm(ks[9], (DEPTH, D_MODEL, N_MOD * D_MODEL), 0.5 * D_MODEL ** -0.5),
        'b_ada': nrm(ks[10], (DEPTH, N_MOD * D_MODEL), 0.02),
        'g_norm': gain(ks[11], (DEPTH, 3, D_MODEL)),
        'w_ffn_gate': nrm(ks[12], (DEPTH, 2, D_MODEL, D_FF), D_MODEL ** -0.5),
        'w_ffn_up': nrm(ks[13], (DEPTH, 2, D_MODEL, D_FF), D_MODEL ** -0.5),
        'w_ffn_down': nrm(ks[14], (DEPTH, 2, D_FF, D_MODEL), D_FF ** -0.5),
        'w_in': nrm(ks[15], (DEPTH, D_MODEL, PROJ_WIDTH), D_MODEL ** -0.5),
        'g_a_q': gain(ks[16], (DEPTH, A_HEAD_DIM)),
        'g_a_k': gain(ks[17], (DEPTH, A_HEAD_DIM)),
        'w_gla_up': nrm(ks[18], (DEPTH, 2, B_GATE_RANK, B_HEADS * B_KEY_DIM), B_GATE_RANK ** -0.5),
        'b_gla': nrm(ks[19], (DEPTH, 2, B_HEADS * B_KEY_DIM), 0.1),
        'g_gla': gain(ks[20], (DEPTH, B_VAL_DIM)),
        'g_c_q': gain(ks[21], (DEPTH, 2, C_QK_DIM)),
        'g_c_k': gain(ks[22], (DEPTH, 2, C_QK_DIM)),
        'lam_c': nrm(ks[23], (DEPTH, 4, C_QK_DIM), 0.1),
        'g_c_out': gain(ks[24], (DEPTH, C_VAL_DIM)),
        'w_out': nrm(ks[25], (DEPTH, MIX_WIDTH, D_MODEL), MIX_WIDTH ** -0.5),
    }


def reference(x_prompt, x_sample, c, cache_a_k, cache_a_v, cache_c_k, cache_c_v, state_gla, c_ctx,
              w_ada, b_ada, g_norm, w_ffn_gate, w_ffn_up, w_ffn_down, w_in, g_a_q, g_a_k,
              w_gla_up, b_gla, g_gla, g_c_q, g_c_k, lam_c, g_c_out, w_out):
    layers = [dict(w_ada=w_ada[l], b_ada=b_ada[l], g_norm=g_norm[l], w_ffn_gate=w_ffn_gate[l],
                   w_ffn_up=w_ffn_up[l], w_ffn_down=w_ffn_down[l], w_in=w_in[l], g_a_q=g_a_q[l],
                   g_a_k=g_a_k[l], w_gla_up=w_gla_up[l], b_gla=b_gla[l], g_gla=g_gla[l],
                   g_c_q=g_c_q[l], g_c_k=g_c_k[l], lam_c=lam_c[l], g_c_out=g_c_out[l], w_out=w_out[l])
              for l in range(DEPTH)]
    lam_inits = [0.8 - 0.6 * math.exp(-0.3 * l) for l in range(DEPTH)]

    y = x_prompt
    ak, av, ck, cv, sg = [], [], [], [], []
    for l in range(DEPTH):
        mod = adaln(c_ctx[None, :], layers[l])
        y, (k_a, v_a, k_c, v_c, s_g) = trunk_layer(y, mod, layers[l], lam_inits[l])
        ak.append(k_a); av.append(v_a); ck.append(k_c); cv.append(v_c); sg.append(s_g)
    y_prompt = y

    T = x_sample.shape[1]
    ROWS = T // GRID_W
    row = jnp.repeat(jnp.arange(ROWS), GRID_W).astype(jnp.float32)
    col = jnp.tile(jnp.arange(GRID_W), ROWS).astype(jnp.float32)
    z = x_sample
    for l in range(DEPTH):
        mod = adaln(c, layers[l])
        ctx = (cache_a_k[:, l], cache_a_v[:, l], cache_c_k[:, l], cache_c_v[:, l], state_gla[:, l])
        z, _ = trunk_layer(z, mod, layers[l], lam_inits[l], ctx, (row, col))
    y_sample = z

    return (y_prompt, y_sample, jnp.stack(ak, axis=1), jnp.stack(av, axis=1), jnp.stack(ck, axis=1),
            jnp.stack(cv, axis=1), jnp.stack(sg, axis=1))
```

```python
import math
from contextlib import ExitStack

import numpy as np
import concourse.bass as bass
import concourse.mybir as mybir
from concourse.bass_utils import run_bass_kernel_spmd

F32 = mybir.dt.float32
F32R = mybir.dt.float32r
AF = mybir.ActivationFunctionType
ALU = mybir.AluOpType
AX = mybir.AxisListType

D = 1024
L_FULL = 4
DFF = 2816
NTOK = 1536
EPS = 1e-6
NSLOT = 4
SLOTF = 2048
NA = 44
XR = 1281
NV = 104

(C_ONESM, C_BLK64, C_BLK48, C_BLK96, C_SEL32, C_PSWA, C_PSWC, C_TRIU, C_TRIL, C_SL16, C_SU16) = range(11)
NCM = 11


def R(ap):
    return ap if ap.dtype == F32R else ap.bitcast(F32R)


def RO(ap):
    return R(ap) if ap.name == "arena" else ap


class TL:
    def __init__(self, name, step):
        self.name, self.step, self.count, self.sem = name, step, 0, None


class Reg:
    __slots__ = ("name", "w", "r", "excl")

    def __init__(self, name, excl=False):
        self.name, self.w, self.r, self.excl = name, None, {}, excl


class Q:
    def __init__(self, name, no_self=False):
        self.name = name
        self.tl = TL(name, 1)
        self.ops = []
        self.seen = {}
        self.no_self = no_self


class Prog:
    def __init__(self):
        self.pe = Q("pe", no_self=True)
        self.act = Q("act")
        self.dve = Q("dve")
        self.pool = Q("pool")
        self.sp = Q("sp")
        self.tls = [self.pe.tl, self.act.tl, self.dve.tl, self.pool.tl]
        self.dma_tls = []

    def new_dma_tl(self, name):
        t = TL(name, 16)
        self.tls.append(t)
        self.dma_tls.append(t)
        return t

    def emit(self, q, fn, reads=(), writes=(), tl=None):
        dma = tl is not None
        tl = tl or q.tl
        need = {}

        def req(t, v):
            if v > need.get(t, 0):
                need[t] = v

        for r in reads:
            if r.w:
                req(*r.w)
            if r.excl:
                for t, v in r.r.items():
                    if t is not tl:
                        req(t, v)
        for w in writes:
            if w.w:
                req(*w.w)
            for t, v in w.r.items():
                req(t, v)
        if dma and tl.count > 0:
            req(tl, tl.count)
        waits = []
        for t, v in need.items():
            if t is q.tl and q.no_self and not dma:
                continue
            if q.seen.get(t, 0) < v:
                waits.append((t, v))
                q.seen[t] = v
        tl.count += tl.step
        my = tl.count
        q.ops.append((waits, fn, tl))
        for w in writes:
            w.w = (tl, my)
            w.r = {}
        for r in reads:
            r.r[tl] = my

    def replay(self, q, eng):
        for waits, fn, tl in q.ops:
            for t, v in waits:
                eng.wait_ge(t.sem, v)
            ins = fn(eng)
            ins.then_inc(tl.sem, tl.step)


class Rot:
    def __init__(self, items):
        self.items, self.i = list(items), 0

    def next(self):
        it = self.items[self.i % len(self.items)]
        self.i += 1
        return it


class Arena:
    def __init__(self, n):
        self.n = n
        self.free = [True] * n

    def alloc(self, k=1):
        for s in range(self.n - k + 1):
            if all(self.free[s:s + k]):
                for i in range(s, s + k):
                    self.free[i] = False
                return s
        raise RuntimeError(f"arena exhausted (need {k}, free {sum(self.free)})")

    def release(self, s, k=1):
        for i in range(s, s + k):
            assert not self.free[i]
            self.free[i] = True


def build(nl=L_FULL, taps=(), upto=10 ** 9):
    nc = bass.Bass("TRN2", target_bir_lowering=False)
    nc.dge_precook = False
    P = Prog()
    es = ExitStack()

    def din(name, shape):
        return nc.dram_tensor(name, list(shape), F32, kind="ExternalInput").ap()

    def dout(name, shape):
        return nc.dram_tensor(name, list(shape), F32, kind="ExternalOutput").ap()

    NSL = 135
    wst = din("wst", [nl * NSL, 128, SLOTF])
    xin = din("xin", [128, 8, NTOK])
    cmat_d = din("cmat", [128, NCM, 128])
    urow_d = din("urow", [128, 2, 512])
    rope_d = din("rope", [128, 4, 512])
    hmask_d = din("hmask", [128, 4])
    bd_d = din("bdmask", [128, 256])
    pvec_d = din("pvec", [128, nl, NV])
    bgla_d = din("bgla", [128, nl, 256])
    w2_d = din("w2", [32, nl, 256])
    lamc_d = din("lamc", [128, nl * 4 * 48])
    cond_d = din("condT", [128, 8, 2])
    rmask_d = din("rmask", [128, 8])
    oz_d = din("oz", [128, 2, 512])
    cak_d = din("cakT", [nl, 2, 64, 512])
    cav_d = din("cav", [nl, 512, 128])
    cck_d = din("cckT", [nl, 4, 128, 512])
    ccv_d = din("ccv", [nl, 512, 384])
    sg_d = din("sgla", [nl, 2, 128, 64])

    yT_o = dout("yT", [128, 8, NTOK])
    ak_o = dout("akT", [nl, 2, 64, 1024])
    av_o = dout("av", [nl, 1024, 128])
    ck_o = dout("ckT", [nl, 4, 128, 1024])
    cv_o = dout("cv", [nl, 1024, 384])
    st_o = dout("st", [nl, 4, 2, 128, 64])
    tap_o = {name: dout("tap_" + name, shape) for name, shape in taps}

    XB = [512, 512, 264]
    xch_in_t = [nc.dram_tensor(f"xch_in{b}", [XB[b], 512], F32) for b in range(3)]
    xch_out_t = [nc.dram_tensor(f"xch_out{b}", [4 * XB[b], 512], F32) for b in range(3)]

    def xloc(row):
        if row < 512:
            return 0, row
        if row < 640:
            return 1, row - 512
        if row < 768:
            return 2, row - 640
        if row < 1152:
            return 1, row - 768 + 128
        return 2, row - 1152 + 128

    xin_flat = [t_.ap().rearrange("r c -> (r c)") for t_ in xch_in_t]
    xout_flat = [t_.ap().rearrange("r c -> (r c)") for t_ in xch_out_t]

    def xin_rows(row0, nrows):
        b, lr = xloc(row0)
        return xch_in_t[b].ap()[lr:lr + nrows, :], xin_rs[b]

    def xin_v(row0, nrows, c, nfl=None):
        b, lr = xloc(row0)
        n = nrows * 512 if nfl is None else nfl
        return xin_flat[b][lr * 512:lr * 512 + n].rearrange("(t c) -> t c", c=c), xin_rs[b]

    def xo_multi(row0, nrows):
        b, lr = xloc(row0)
        v = xch_out_t[b].ap().rearrange("(r x) c -> r x c", r=4)
        return v[:, lr:lr + nrows, :].rearrange("r p c -> p r c"), xout_rs[b]

    def xout_v(r, row0, nrows, c, nfl=None):
        b, lr = xloc(row0)
        o = (r * XB[b] + lr) * 512
        n = nrows * 512 if nfl is None else nfl
        return xout_flat[b][o:o + n].rearrange("(t c) -> t c", c=c), xout_rs[b]

    def sb(name, shape):
        return es.enter_context(nc.sbuf_tensor(name, list(shape), F32))

    xT = sb("xT", [128, 8, NTOK])
    ring = sb("ring", [128, NSLOT, SLOTF])
    arena = sb("arena", [128, NA, 512])
    cmat = sb("cmat_s", [128, NCM, 128])
    urow = sb("urow_s", [128, 2, 512])
    rope = sb("rope_s", [128, 4, 512])
    hmask = sb("hmask_s", [128, 4])
    bdm = sb("bd_s", [128, 256])
    pvec = sb("pvec_s", [128, nl, NV])
    bgla = sb("bgla_s", [128, nl, 256])
    w2 = sb("w2_s", [32, nl, 256])
    lamc = sb("lamc_s", [128, nl * 4 * 48])
    cond = sb("cond_s", [128, 8, 2])
    scT = sb("scT_s", [128, 8, 2])
    rmask = sb("rmask_s", [128, 8])
    oz = sb("oz_s", [128, 2, 512])
    modT = sb("modT_s", [128, 72, 2])
    gsT = sb("gs_s", [128, 3, 8, 2])
    gtT = sb("gt_s", [128, 3, 8, 2])
    lam = sb("lam_s", [128, 4, 4])
    lamt = sb("lamt_s", [128, nl * 2 * 48])
    gco = sb("gco_s", [128, 4])
    small = sb("small_s", [128, 16])

    psb = [es.enter_context(nc.psum_tensor(f"ps{i}", [128, 512], F32)) for i in range(8)]

    x_r = [[Reg(f"x{c}_{t}") for t in range(3)] for c in range(8)]
    ring_r = [Reg(f"ring{s}") for s in range(NSLOT)]
    ar_r = [Reg(f"ar{i}") for i in range(NA)]
    ps_r = [Reg(f"ps{i}", excl=True) for i in range(8)]
    const_r = Reg("consts")
    mod_r = Reg("mod")
    misc_r = Reg("misc")
    small_r = Reg("small")
    xin_rs = [Reg(f"xch_in{b}") for b in range(3)]
    xout_rs = [Reg(f"xch_out{b}") for b in range(3)]

    ar = Arena(NA)

    class Tile:
        def __init__(self, k=1):
            self.k = k
            self.s = ar.alloc(k)
            self.regs = ar_r[self.s:self.s + k]

        def ap(self):
            return arena[:, self.s:self.s + self.k, :].rearrange("p k f -> p (k f)") if self.k > 1 else arena[:, self.s, :]

        def free(self):
            ar.release(self.s, self.k)

    ps_tmp = Rot(range(0, 5))
    ps_acc = Rot(range(5, 8))

    class PS:
        def __init__(self, kind="tmp"):
            self.i = (ps_tmp if kind == "tmp" else ps_acc).next()
            self.regs = [ps_r[self.i]]

        def ap(self):
            return psb[self.i][:]

    slot_tl = [P.new_dma_tl(f"slot{s}") for s in range(NSLOT)]
    misc_tl = Rot([P.new_dma_tl(f"md{i}") for i in range(8)])
    out_tl = Rot([P.new_dma_tl(f"od{i}") for i in range(4)])
    cc_tls = []
    for i in range(nl * 3):
        t_ = TL(f"cc{i}", 1)
        P.tls.append(t_)
        cc_tls.append(t_)

    E = P.emit

    def dma(q, out, in_, reads, writes, tl):
        E(q, lambda e, out=out, in_=in_: e.dma_start(out=out, in_=in_), reads, writes, tl=tl)

    def pool_ld(out, in_, writes, reads=()):
        dma(P.pool, out, in_, list(reads), list(writes), misc_tl.next())

    def sp_ld(out, in_, writes, reads=()):
        dma(P.sp, out, in_, list(reads), list(writes), misc_tl.next())

    def pool_st(out, in_, reads, writes=()):
        dma(P.pool, out, in_, list(reads), list(writes), out_tl.next())

    ring_n = [0]

    def ring_load(idx, nfl=SLOTF):
        s = ring_n[0] % NSLOT
        ring_n[0] += 1
        dma(P.sp, R(ring[:, s, 0:nfl]), R(wst[idx, :, 0:nfl]), [], [ring_r[s]], slot_tl[s])
        return s

    def mm(ps, out_ap, pairs, reads, start=True, stop=True):
        n = len(pairs)

        def fn(e, pairs=pairs, out_ap=out_ap, start=start, stop=stop):
            ins = None
            for i, (lt, rh) in enumerate(pairs):
                ins = e.matmul(out_ap, R(lt), R(rh), start=(start and i == 0), stop=(stop and i == n - 1))
            return ins

        E(P.pe, fn, list(reads), ps.regs)

    def A(out, in_, func, reads, writes, bias=0.0, scale=1.0):
        out = RO(out)
        E(P.act, lambda e: e.activation(out, in_, func, bias=bias, scale=scale), list(reads), list(writes))

    def V_tt(out, in0, in1, op, reads, writes, q=None):
        out = RO(out)
        E(q or P.dve, lambda e: e.tensor_tensor(out, in0, in1, op), list(reads), list(writes))

    def V_ts(out, in0, s1, s2, op0, op1, reads, writes, q=None):
        out = RO(out)
        if op1 is None:
            E(q or P.dve, lambda e: e.tensor_scalar(out, in0, s1, None, op0), list(reads), list(writes))
        else:
            E(q or P.dve, lambda e: e.tensor_scalar(out, in0, s1, s2, op0, op1), list(reads), list(writes))

    def V_stt(out, in0, sc, in1, op0, op1, reads, writes, q=None):
        out = RO(out)
        E(q or P.dve, lambda e: e.scalar_tensor_tensor(out, in0, sc, in1, op0, op1), list(reads), list(writes))

    def V_rcp(out, in_, reads, writes):
        out = RO(out)
        E(P.dve, lambda e: e.reciprocal(out, in_), list(reads), list(writes))

    def V_cp(out, in_, reads, writes, q=None):
        out = RO(out)
        E(q or P.dve, lambda e: e.tensor_copy(out, in_), list(reads), list(writes))

    def fill(tl_, val, nfl=None):
        n = tl_.k * 512 if nfl is None else nfl
        a = tl_.ap()
        for o in range(0, n, 512):
            w = min(512, n - o)
            V_cp(a[:, o:o + w], oz[:, 1 if val == 1.0 else 0, 0:w], [const_r], [tl_.regs[o // 512]], q=P.pool)

    def CM(i, rows=128, cols=128):
        return cmat[0:rows, i, 0:cols]

    def rsqrt_from(ps_ap, out_ap, reads, writes, tmp_ap, tmp_regs):
        A(tmp_ap, ps_ap, AF.Ln, reads, tmp_regs, bias=EPS)
        A(out_ap, tmp_ap, AF.Exp, tmp_regs, writes, scale=-0.5)

    def tap(name, ap, reads):
        if name in tap_o:
            pool_st(tap_o[name], ap, reads)

    pool_ld(R(cmat[:]), R(cmat_d), [const_r])
    pool_ld(R(urow[:]), R(urow_d), [const_r])
    pool_ld(rope[:], rope_d, [const_r])
    pool_ld(hmask[:], hmask_d, [const_r])
    pool_ld(bdm[:], bd_d, [const_r])
    pool_ld(pvec[:], pvec_d, [const_r])
    pool_ld(bgla[:], bgla_d, [const_r])
    pool_ld(R(w2[:]), R(w2_d), [const_r])
    pool_ld(lamc[:], lamc_d, [const_r])
    pool_ld(cond[:], cond_d, [const_r])
    pool_ld(rmask[:], rmask_d, [const_r])
    pool_ld(oz[:], oz_d, [const_r])
    for t in range(3):
        pool_ld(xT[:, :, t * 512:(t + 1) * 512], xin[:, :, t * 512:(t + 1) * 512], [x_r[c][t] for c in range(8)])

    lc = lamc[:].rearrange("p (l a b d) -> p l a b d", l=nl, a=2, b=2, d=48)
    lt_v = lamt[:].rearrange("p (l a d) -> p l a d", l=nl, a=2, d=48)
    V_tt(lt_v, lc[:, :, :, 0, :], lc[:, :, :, 1, :], ALU.mult, [const_r], [misc_r])
    for l in range(nl):
        E(P.dve, lambda e, l=l: e.tensor_reduce(lam[:, l, 2:4], lt_v[:, l, :, :], AX.X, ALU.add), [misc_r], [misc_r])
        A(lam[:, l, 2:4], lam[:, l, 2:4], AF.Exp, [misc_r], [misc_r])
        V_tt(lam[:, l, 0:1], lam[:, l, 2:3], lam[:, l, 3:4], ALU.subtract, [misc_r], [misc_r])
        li = 0.8 - 0.6 * math.exp(-0.3 * l)
        V_ts(lam[:, l, 0:1], lam[:, l, 0:1], float(li), None, ALU.add, None, [misc_r], [misc_r])
        V_ts(lam[:, l, 1:2], lam[:, l, 0:1], -1.0, None, ALU.mult, None, [misc_r], [misc_r])
        V_ts(gco[:, l:l + 1], pvec[:, l, 101:102], float(1.0 - li), None, ALU.mult, None, [const_r, misc_r], [misc_r])
    A(R(scT[:]), cond[:], AF.Silu, [const_r], [misc_r])

    def adaln(l):
        ps = PS("acc")
        pv = ps.ap()[:, 0:144].rearrange("p (c j) -> p c j", j=2)
        for sl in range(36):
            s = ring_load(l * NSL + 99 + sl)
            sv = ring[:, s, :].rearrange("p (k n) -> p k n", k=8)
            for half in range(2):
                ch = sl * 2 + half
                mm(ps, pv[:, ch, :], [(sv[:, kc, half * 128:(half + 1) * 128], scT[:, kc, :]) for kc in range(8)],
                   [ring_r[s], misc_r])
        for j in range(2):
            V_tt(modT[:, :, j], pv[:, :, j], pvec[:, l, 24:96], ALU.add, ps.regs + [const_r], [mod_r])
        for s3 in range(3):
            for j in range(2):
                V_stt(gsT[:, s3, :, j], modT[:, (3 * s3 + 1) * 8:(3 * s3 + 2) * 8, j], 1.0,
                      pvec[:, l, s3 * 8:(s3 + 1) * 8], ALU.add, ALU.mult, [mod_r, const_r], [mod_r])
                V_ts(gtT[:, s3, :, j], modT[:, (3 * s3 + 2) * 8:(3 * s3 + 3) * 8, j],
                     (1.0 if s3 == 1 else 0.5), None, ALU.mult, None, [mod_r], [mod_r])

    def norm_mod(t, s3, j, h):
        ps = PS()
        sqs = [Tile(), Tile(), Tile()]
        for c in range(8):
            sq = sqs[c % 3]
            xa = xT[:, c, t * 512:(t + 1) * 512]
            V_tt(R(sq.ap()), xa, xa, ALU.mult, [x_r[c][t]], sq.regs, q=P.pool)
            mm(ps, ps.ap(), [(CM(C_ONESM), sq.ap())], sq.regs + [const_r], start=(c == 0), stop=(c == 7))
        tmp = Tile()
        rstd = Tile()
        rsqrt_from(ps.ap(), rstd.ap(), ps.regs, rstd.regs, tmp.ap(), tmp.regs)
        hv = h.ap().rearrange("p (c f) -> p c f", c=8)
        for c in range(8):
            t2 = sqs[c % 3]
            V_stt(t2.ap(), xT[:, c, t * 512:(t + 1) * 512], gsT[:, s3, c, j:j + 1], rstd.ap(), ALU.mult, ALU.mult,
                  [x_r[c][t], mod_r] + rstd.regs, t2.regs)
            A(R(hv[:, c, :]), t2.ap(), AF.Identity, t2.regs + [mod_r], [h.regs[c]], bias=modT[:, 3 * s3 * 8 + c, j:j + 1])
        for tl_ in sqs:
            tl_.free()
        tmp.free()
        rstd.free()

    def ffn_multi(l, tiles, s, hook=None):
        s3 = 0 if s == 0 else 2
        nt = len(tiles)
        js = [1 if t == 2 else 0 for t in tiles]
        hs = [Tile(8) for _ in tiles]
        for ti, t in enumerate(tiles):
            norm_mod(t, s3, js[ti], hs[ti])
        hvs = [h.ap().rearrange("p (c f) -> p c f", c=8) for h in hs]
        acts = [Tile(11) for _ in tiles]
        avs = [a_.ap().rearrange("p (c f) -> p c f", c=11) for a_ in acts]
        base = l * NSL + s * 38
        for half in range(2):
            for fcl in range(11):
                sl = ring_load(base + half * 19 + fcl)
                sv = ring[:, sl, :].rearrange("p (g k n) -> p g k n", g=2, k=8)
                for ti in range(nt):
                    psg, psu = PS(), PS()
                    mm(psg, psg.ap(), [(sv[:, 0, kc, :], hvs[ti][:, kc, :]) for kc in range(8)], [ring_r[sl]] + hs[ti].regs)
                    mm(psu, psu.ap(), [(sv[:, 1, kc, :], hvs[ti][:, kc, :]) for kc in range(8)], [ring_r[sl]] + hs[ti].regs)
                    sg = Tile()
                    A(sg.ap(), psg.ap(), AF.Silu, psg.regs, sg.regs)
                    V_tt(R(avs[ti][:, fcl, :]), sg.ap(), psu.ap(), ALU.mult, sg.regs + psu.regs, [acts[ti].regs[fcl]])
                    sg.free()
                if hook is not None:
                    hook()
            for m in range(8):
                sl = ring_load(base + half * 19 + 11 + m, 1408)
                sv = ring[:, sl, 0:1408].rearrange("p (f n) -> p f n", f=11)
                for ti, t in enumerate(tiles):
                    psy = PS()
                    mm(psy, psy.ap(), [(sv[:, f, :], avs[ti][:, f, :]) for f in range(11)], [ring_r[sl]] + acts[ti].regs)
                    xa = xT[:, m, t * 512:(t + 1) * 512]
                    V_stt(xa, psy.ap(), gtT[:, s3, m, js[ti]:js[ti] + 1], xa, ALU.mult, ALU.add,
                          psy.regs + [mod_r, x_r[m][t]], [x_r[m][t]])
                if hook is not None:
                    hook()
        for tl_ in acts + hs:
            tl_.free()

    def proj_fm(sl_idx, hv, h, nchunks):
        sl = ring_load(sl_idx)
        sv = ring[:, sl, :].rearrange("p (k n) -> p k n", k=8)
        out = []
        for i in range(nchunks):
            ps = PS()
            mm(ps, ps.ap(), [(sv[:, kc, i * 128:(i + 1) * 128], hv[:, kc, :]) for kc in range(8)], [ring_r[sl]] + h.regs)
            out.append(ps)
        return out

    def proj_tm(sl_idx, hv, h, ncols):
        sl = ring_load(sl_idx)
        sv = ring[:, sl, :].rearrange("p (k n) -> p k n", k=8)
        per = 512 // ncols
        res = []
        for g in range(0, 4, per):
            ps = PS()
            pv = ps.ap()[:, 0:per * ncols].rearrange("p (b n) -> p b n", b=per)
            for bi in range(per):
                tb = g + bi
                mm(ps, pv[:, bi, :], [(hv[:, kc, tb * 128:(tb + 1) * 128], sv[:, kc, 0:ncols]) for kc in range(8)],
                   [ring_r[sl]] + h.regs)
            res.append((ps, pv, list(range(g, g + per))))
        return res

    def make_qk_unit(sidx, i, ss, hv, h, blk, gcol_ap, dst, rot):
        st = {}

        def s0():
            if "sl" not in ss:
                ss["sl"] = ring_load(sidx)
            sl = ss["sl"]
            sv = ring[:, sl, :].rearrange("p (k n) -> p k n", k=8)
            ps = PS()
            mm(ps, ps.ap(), [(sv[:, kc, i * 128:(i + 1) * 128], hv[:, kc, :]) for kc in range(8)], [ring_r[sl]] + h.regs)
            st["raw"] = Tile()
            A(st["raw"].ap(), ps.ap(), AF.Copy, ps.regs, st["raw"].regs)

        def s1():
            st["sq"] = Tile()
            V_tt(st["sq"].ap(), st["raw"].ap(), st["raw"].ap(), ALU.mult, st["raw"].regs, st["sq"].regs, q=P.pool)

        def s2():
            st["ps2"] = PS()
            mm(st["ps2"], st["ps2"].ap(), [(CM(blk), st["sq"].ap())], st["sq"].regs + [const_r])

        def s3():
            st["rstd"] = Tile()
            rsqrt_from(st["ps2"].ap(), st["rstd"].ap(), st["ps2"].regs, st["rstd"].regs, st["sq"].ap(), st["sq"].regs)

        def s4():
            raw, rstd = st["raw"], st["rstd"]
            if rot is None:
                V_stt(dst.ap(), raw.ap(), gcol_ap, rstd.ap(), ALU.mult, ALU.mult, raw.regs + rstd.regs + [const_r], dst.regs)
                for k_ in ("raw", "sq", "rstd"):
                    st[k_].free()
            else:
                st["xn"] = Tile()
                V_stt(st["xn"].ap(), raw.ap(), gcol_ap, rstd.ap(), ALU.mult, ALU.mult, raw.regs + rstd.regs + [const_r], st["xn"].regs)

        def s5():
            st["ps3"] = PS()
            mm(st["ps3"], st["ps3"].ap(), [(CM(rot[2]), st["xn"].ap())], st["xn"].regs + [const_r])

        def s6():
            st["t1"] = Tile()
            V_tt(st["t1"].ap(), st["ps3"].ap(), rope[:, rot[1], :], ALU.mult, st["ps3"].regs + [const_r], st["t1"].regs)
            V_tt(st["raw"].ap(), st["xn"].ap(), rope[:, rot[0], :], ALU.mult, st["xn"].regs + [const_r], st["raw"].regs, q=P.pool)

        def s7():
            V_tt(dst.ap(), st["t1"].ap(), st["raw"].ap(), ALU.add, st["t1"].regs + st["raw"].regs, dst.regs)
            for k_ in ("raw", "sq", "rstd", "xn", "t1"):
                st[k_].free()

        return [s0, s1, s2, s3, s4] if rot is None else [s0, s1, s2, s3, s4, s5, s6, s7]

    def qk_project(slot_specs, hv, h, blk, gcols, dsts, rot):
        units = []
        ci = 0
        for (sidx, nch) in slot_specs:
            ss = {}
            for i in range(nch):
                units.append(make_qk_unit(sidx, i, ss, hv, h, blk, gcols[ci], dsts[ci], rot))
                ci += 1
        run_pipeline(units, spacing=2)

    def attn_jobs(jobs, LOOK=2):
        items = []
        for J in jobs:
            nk = len(J["keys"])
            per = 512 // J["ncol"]
            i = 0
            while i < nk:
                g = min(per, nk - i)
                items.append((J, i, g))
                i += g
        pend = []

        def do_pv(ent):
            J, i, g, pT = ent
            nk = len(J["keys"])
            ncol = J["ncol"]
            if i == 0:
                J["acc"] = PS("acc")
            acc = J["acc"]
            for u in range(g):
                vap, vregs = J["vwin"](i + u)
                mm(acc, acc.ap()[0:128, 0:ncol], [(vap, pT.ap()[:, u * ncol:(u + 1) * ncol])], vregs + pT.regs,
                   start=(i + u == 0), stop=(i + u == nk - 1))
            pT.free()
            if i + g == nk:
                J["finish"](acc)

        for (J, i, g) in items:
            ncol, qt, qb, q0 = J["ncol"], J["qt"], J["qbase"], J["q0"]
            ps = PS()
            for u in range(g):
                kap, kregs = J["keys"][i + u]
                mm(ps, ps.ap()[:, u * ncol:(u + 1) * ncol], [(kap, qt.ap()[qb:qb + 64, q0:q0 + ncol])], kregs + qt.regs)
            pT = Tile()
            A(pT.ap()[:, 0:g * ncol], ps.ap()[:, 0:g * ncol], AF.Exp, ps.regs, pT.regs, scale=J["scale"])
            pend.append((J, i, g, pT))
            if len(pend) > LOOK:
                do_pv(pend.pop(0))
        while pend:
            do_pv(pend.pop(0))

    def run_pipeline(units, spacing=2):
        n = len(units)
        ns = max(len(u) for u in units)
        for step in range((n - 1) * spacing + ns):
            for u in range(n):
                st_ = step - u * spacing
                if 0 <= st_ < len(units[u]):
                    units[u][st_]()

    def mixer_A_heads(l, QA, segs, keyf, vwinf, mixA):
        jobs = []
        for (q0, ncol, sid) in segs:
            for hh in range(6):
                ch, par, kv = hh // 2, hh % 2, hh // 3
                base = par * 64

                def finish(acc, ch=ch, par=par, q0=q0, ncol=ncol):
                    nb_, db_ = (0, 64) if par == 0 else (64, 0)
                    rd = Tile()
                    V_rcp(rd.ap()[db_:db_ + 64, 0:ncol], acc.ap()[db_:db_ + 64, 0:ncol], acc.regs, rd.regs)
                    V_tt(R(mixA[ch].ap()[nb_:nb_ + 64, q0:q0 + ncol]), acc.ap()[nb_:nb_ + 64, 0:ncol],
                         rd.ap()[db_:db_ + 64, 0:ncol], ALU.mult, acc.regs + rd.regs, mixA[ch].regs)
                    rd.free()

                jobs.append(dict(qt=QA[ch], qbase=base, ncol=ncol, q0=q0, keys=keyf(sid, kv, base), scale=0.125,
                                 vwin=(lambda i, sid=sid, kv=kv, par=par: vwinf(sid, kv, par, i)), finish=finish))
        attn_jobs(jobs)

    def c_out_norm(l, oc, mixCh):
        sq = Tile()
        V_tt(R(sq.ap()[0:96, :]), oc.ap()[0:96, :], oc.ap()[0:96, :], ALU.mult, oc.regs, sq.regs, q=P.pool)
        ps = PS()
        mm(ps, ps.ap()[0:96, :], [(CM(C_BLK96, 96, 96), sq.ap()[0:96, :])], sq.regs + [const_r])
        rstd = Tile()
        A(sq.ap()[0:96, :], ps.ap()[0:96, :], AF.Ln, ps.regs, sq.regs, bias=EPS)
        A(rstd.ap()[0:96, :], sq.ap()[0:96, :], AF.Exp, sq.regs, rstd.regs, scale=-0.5)
        V_stt(R(mixCh.ap()[0:96, :]), oc.ap()[0:96, :], gco[0:96, l:l + 1], rstd.ap()[0:96, :], ALU.mult, ALU.mult,
              oc.regs + rstd.regs + [misc_r], mixCh.regs)
        sq.free()
        rstd.free()
        oc.free()

    def mixer_C_heads(l, QC, heads, segs, keyf, vwinf, mixC):
        jobs = []
        for hi, hh in enumerate(heads):
            st_h = {}
            for si_, (q0, ncol, sid) in enumerate(segs):
                st_s = {}
                for m in range(2):
                    base = m * 64

                    def finish(acc, m=m, ncol=ncol, q0=q0, st_h=st_h, st_s=st_s, hi=hi, last_seg=(si_ == len(segs) - 1)):
                        if "oc" not in st_h:
                            st_h["oc"] = Tile()
                        oc = st_h["oc"]
                        accs = Tile()
                        A(accs.ap()[:, 0:ncol], acc.ap()[:, 0:ncol], AF.Copy, acc.regs, accs.regs)
                        psd = PS()
                        mm(psd, psd.ap()[0:96, 0:ncol], [(CM(C_SEL32, 128, 96), accs.ap()[:, 0:ncol])], accs.regs + [const_r])
                        rd = Tile()
                        V_rcp(rd.ap()[0:96, 0:ncol], psd.ap()[0:96, 0:ncol], psd.regs, rd.regs)
                        om = Tile()
                        st_s[m] = om
                        V_tt(om.ap()[0:96, 0:ncol], accs.ap()[0:96, 0:ncol], rd.ap()[0:96, 0:ncol], ALU.mult,
                             accs.regs + rd.regs, om.regs)
                        rd.free()
                        accs.free()
                        if m == 1:
                            V_stt(oc.ap()[0:96, q0:q0 + ncol], st_s[1].ap()[0:96, 0:ncol], lam[0:96, l, 1:2],
                                  st_s[0].ap()[0:96, 0:ncol], ALU.mult, ALU.add, st_s[0].regs + st_s[1].regs + [misc_r], oc.regs)
                            st_s[0].free()
                            st_s[1].free()
                            if last_seg:
                                c_out_norm(l, oc, mixC[hi])

                    jobs.append(dict(qt=QC[hi], qbase=base, ncol=ncol, q0=q0, keys=keyf(sid, hh, base), scale=48 ** -0.5,
                                     vwin=(lambda i, sid=sid, hh=hh: vwinf(sid, hh, i)), finish=finish))
        attn_jobs(jobs)

    def gla(l, BQ, BK, BG, KTM, VTM, Vpad, segs, sample, st_dst):
        ktm_v = KTM.ap().rearrange("p (b n) -> p b n", b=4)
        vtm_v = VTM.ap().rearrange("p (b n) -> p b n", b=4)
        vpad_v = Vpad.ap().rearrange("p (b h n) -> p b h n", b=4, h=4)
        Lt = Tile(2)
        Lv = Lt.ap().rearrange("p (b n) -> p b n", b=4)
        for tb in range(4):
            ps = PS()
            mm(ps, ps.ap()[:, 0:256], [(BG.ap()[0:32, tb * 128:(tb + 1) * 128], w2[0:32, l, :])], BG.regs + [const_r])
            zb = Tile()
            V_tt(zb.ap()[:, 0:256], ps.ap()[:, 0:256], bgla[:, l, :], ALU.add, ps.regs + [const_r], zb.regs)
            A(zb.ap()[:, 0:256], zb.ap()[:, 0:256], AF.Exp, zb.regs, zb.regs, scale=-1.0)
            A(R(Lv[:, tb, :]), zb.ap()[:, 0:256], AF.Ln, zb.regs, Lt.regs, bias=1.0)
            zb.free()
        osb = [Tile(), Tile()]
        keep = {}
        for (c0, nb, sid) in segs:
            T = nb * 128
            tb0 = c0 // 128
            mid = T // 2 - 1
            psf, psbk = PS(), PS()
            for jb in range(nb):
                mm(psf, psf.ap()[:, jb * 128:T], [(Lv[:, tb0 + jb, 0:128], urow[:, 0, 0:T - jb * 128])], Lt.regs + [const_r],
                   start=(jb == 0), stop=(jb == nb - 1))
            for idx, jb in enumerate(range(nb - 1, -1, -1)):
                mm(psbk, psbk.ap()[:, 0:(jb + 1) * 128], [(Lv[:, tb0 + jb, 128:256], urow[:, 1, (4 - 1 - jb) * 128:512])],
                   Lt.regs + [const_r], start=(idx == 0), stop=(idx == nb - 1))
            V_cp(small[:, 0:1], psf.ap()[:, mid:mid + 1], psf.regs, [small_r])
            V_ts(small[:, 1:2], psf.ap()[:, mid:mid + 1], -1.0, None, ALU.mult, None, psf.regs, [small_r])
            V_cp(small[:, 2:3], psbk.ap()[:, mid:mid + 1], psbk.regs, [small_r])
            V_ts(small[:, 3:4], psbk.ap()[:, mid:mid + 1], -1.0, None, ALU.mult, None, psbk.regs, [small_r])
            Ef, Enf, Eb, Enb = Tile(), Tile(), Tile(), Tile()
            A(Ef.ap()[:, 0:T], psf.ap()[:, 0:T], AF.Exp, psf.regs + [small_r], Ef.regs, bias=small[:, 1:2], scale=1.0)
            A(Enf.ap()[:, 0:T], psf.ap()[:, 0:T], AF.Exp, psf.regs + [small_r], Enf.regs, bias=small[:, 0:1], scale=-1.0)
            A(Eb.ap()[:, 0:T], psbk.ap()[:, 0:T], AF.Exp, psbk.regs + [small_r], Eb.regs, bias=small[:, 3:4], scale=1.0)
            A(Enb.ap()[:, 0:T], psbk.ap()[:, 0:T], AF.Exp, psbk.regs + [small_r], Enb.regs, bias=small[:, 2:3], scale=-1.0)
            if sample:
                A(small[:, 6:7], psf.ap()[:, T - 1:T], AF.Exp, psf.regs, [small_r])
                A(small[:, 7:8], psbk.ap()[:, 0:1], AF.Exp, psbk.regs, [small_r])
                A(small[:, 4:5], small[:, 0:1], AF.Exp, [small_r], [small_r])
                A(small[:, 5:6], small[:, 2:3], AF.Exp, [small_r], [small_r])
                V_ts(small[:, 4:6], small[:, 4:6], float(32 ** -0.5), None, ALU.mult, None, [small_r], [small_r])
                qinf, qinb = Tile(), Tile()
                V_stt(R(qinf.ap()), Ef.ap(), small[:, 4:5], BQ.ap(), ALU.mult, ALU.mult, Ef.regs + BQ.regs + [small_r], qinf.regs)
                V_stt(R(qinb.ap()), Eb.ap(), small[:, 5:6], BQ.ap(), ALU.mult, ALU.mult, Eb.regs + BQ.regs + [small_r], qinb.regs)
                keep["qinf"], keep["qinb"] = qinf, qinb
                d_ap, d_rg = xin_v(1280, 1, 2, nfl=256)
                pool_st(d_ap, small[:, 6:8], [small_r], [d_rg])
            ktf, ktb = Tile(), Tile()
            V_tt(R(ktf.ap()[:, 0:T]), BK.ap()[:, c0:c0 + T], Enf.ap()[:, 0:T], ALU.mult, BK.regs + Enf.regs, ktf.regs)
            V_tt(R(ktb.ap()[:, 0:T]), BK.ap()[:, c0:c0 + T], Enb.ap()[:, 0:T], ALU.mult, BK.regs + Enb.regs, ktb.regs)
            pso = [PS("acc"), PS("acc")]
            for hh in range(4):
                qf, qb = Tile(), Tile()
                V_stt(R(qf.ap()[:, 0:T]), Ef.ap()[:, 0:T], hmask[:, hh:hh + 1], BQ.ap()[:, c0:c0 + T], ALU.mult, ALU.mult,
                      Ef.regs + BQ.regs + [const_r], qf.regs)
                V_stt(R(qb.ap()[:, 0:T]), Eb.ap()[:, 0:T], hmask[:, hh:hh + 1], BQ.ap()[:, c0:c0 + T], ALU.mult, ALU.mult,
                      Eb.regs + BQ.regs + [const_r], qb.regs)
                MT = Tile(nb * T // 512)
                Mv = MT.ap().rearrange("p (b n) -> p b n", b=nb)
                for jb in range(nb):
                    pf, pb = PS(), PS()
                    mm(pf, pf.ap()[:, 0:T - jb * 128], [(ktf.ap()[:, jb * 128:(jb + 1) * 128], qf.ap()[:, jb * 128:T])],
                       ktf.regs + qf.regs)
                    mm(pb, pb.ap()[:, 0:(jb + 1) * 128], [(ktb.ap()[:, jb * 128:(jb + 1) * 128], qb.ap()[:, 0:(jb + 1) * 128])],
                       ktb.regs + qb.regs)
                    if jb > 0:
                        A(R(Mv[:, jb, 0:jb * 128]), pb.ap()[:, 0:jb * 128], AF.Copy, pb.regs, MT.regs)
                    if jb < nb - 1:
                        A(R(Mv[:, jb, (jb + 1) * 128:T]), pf.ap()[:, 128:T - jb * 128], AF.Copy, pf.regs, MT.regs)
                    t1 = Tile()
                    V_tt(t1.ap()[:, 0:128], pf.ap()[:, 0:128], CM(C_TRIU), ALU.mult, pf.regs + [const_r], t1.regs)
                    V_tt(t1.ap()[:, 128:256], pb.ap()[:, jb * 128:(jb + 1) * 128], CM(C_TRIL), ALU.mult, pb.regs + [const_r], t1.regs)
                    V_tt(R(Mv[:, jb, jb * 128:(jb + 1) * 128]), t1.ap()[:, 0:128], t1.ap()[:, 128:256], ALU.add, t1.regs, MT.regs,
                         q=P.pool)
                    t1.free()
                for jb in range(nb):
                    first = (hh % 2 == 0 and jb == 0)
                    last = (hh % 2 == 1 and jb == nb - 1)
                    mm(pso[hh // 2], pso[hh // 2].ap()[:, 0:T], [(vpad_v[:, tb0 + jb, hh, :], Mv[:, jb, :])], Vpad.regs + MT.regs,
                       start=first, stop=last)
                MT.free()
                qf.free()
                qb.free()
            for c2 in range(2):
                if not sample:
                    V_cp(osb[c2].ap()[:, c0:c0 + T], pso[c2].ap()[:, 0:T], pso[c2].regs, osb[c2].regs)
                else:
                    V_cp(osb[c2].ap()[:, 0:T], pso[c2].ap()[:, 0:T], pso[c2].regs, osb[c2].regs)
            for d_ in range(2):
                kd = Tile()
                kdv = kd.ap().rearrange("p (b n) -> p b n", b=4)
                for jb in range(nb):
                    ps = PS()
                    prs = []
                    if d_ == 0:
                        for j2 in range(jb, nb):
                            lt = CM(C_SL16) if j2 == jb else urow[:, 0, 128:256]
                            prs.append((lt, Lv[:, tb0 + j2, 0:128]))
                    else:
                        for j2 in range(0, jb + 1):
                            lt = CM(C_SU16) if j2 == jb else urow[:, 0, 128:256]
                            prs.append((lt, Lv[:, tb0 + j2, 128:256]))
                    mm(ps, ps.ap()[:, 0:128], prs, Lt.regs + [const_r])
                    ed = Tile()
                    A(ed.ap()[:, 0:128], ps.ap()[:, 0:128], AF.Exp, ps.regs, ed.regs)
                    V_tt(R(kdv[:, jb, :]), ktm_v[:, tb0 + jb, :], ed.ap()[:, 0:128], ALU.mult, KTM.regs + ed.regs, kd.regs)
                    ed.free()
                pst = PS()
                mm(pst, pst.ap()[:, 0:256], [(kdv[:, jb, :], vtm_v[:, tb0 + jb, :]) for jb in range(nb)], kd.regs + VTM.regs)
                kd.free()
                stt = Tile()
                if not sample:
                    for hh in range(4):
                        A(stt.ap()[32 * hh:32 * hh + 32, 0:64], pst.ap()[32 * hh:32 * hh + 32, 64 * hh:64 * hh + 64], AF.Copy,
                          pst.regs, stt.regs)
                    pool_st(st_dst(sid, d_), stt.ap()[:, 0:64], stt.regs)
                else:
                    A(stt.ap()[:, 0:256], pst.ap()[:, 0:256], AF.Copy, pst.regs, stt.regs)
                    r0 = 1152 + 64 * d_
                    d_ap, d_rg = xin_v(r0, 64, 256)
                    pool_st(d_ap, stt.ap()[:, 0:256], stt.regs, [d_rg])
                stt.free()
            for tl_ in (Ef, Enf, Eb, Enb, ktf, ktb):
                tl_.free()
        Lt.free()
        return osb, keep

    def gla_out(l, osb, GR, mixB):
        for c2 in range(2):
            sq = Tile()
            V_tt(R(sq.ap()), osb[c2].ap(), osb[c2].ap(), ALU.mult, osb[c2].regs, sq.regs, q=P.pool)
            ps = PS()
            mm(ps, ps.ap(), [(CM(C_BLK64), sq.ap())], sq.regs + [const_r])
            rstd = Tile()
            rsqrt_from(ps.ap(), rstd.ap(), ps.regs, rstd.regs, sq.ap(), sq.regs)
            V_stt(sq.ap(), osb[c2].ap(), pvec[:, l, 100:101], rstd.ap(), ALU.mult, ALU.mult, osb[c2].regs + rstd.regs + [const_r], sq.regs)
            V_tt(R(mixB[c2].ap()), sq.ap(), GR[c2].ap(), ALU.mult, sq.regs + GR[c2].regs, mixB[c2].regs)
            sq.free()
            rstd.free()

    def w_out(l, t, j, mix):
        base = l * NSL + 91
        for m in range(8):
            sl = ring_load(base + m, 1152)
            sv = ring[:, sl, 0:1152].rearrange("p (c n) -> p c n", c=9)
            psy = PS()
            prs, rds = [], [ring_r[sl]]
            for c in range(9):
                rows = 128 if c < 5 else 96
                prs.append((sv[0:rows, c, :], mix[c].ap()[0:rows, :]))
                rds += mix[c].regs
            mm(psy, psy.ap(), prs, rds)
            xa = xT[:, m, t * 512:(t + 1) * 512]
            V_stt(xa, psy.ap(), gtT[:, 1, m, j:j + 1], xa, ALU.mult, ALU.add, psy.regs + [mod_r, x_r[m][t]], [x_r[m][t]])

    import os as _os
    stopat = _os.environ.get("STOPAT", "")

    class _Stop(Exception):
        pass

    def chk(name):
        if stopat == name:
            raise _Stop()

    def mixer(l, t):
        try:
            mixer_(l, t)
        except _Stop:
            pass

    def mixer_(l, t):
        sample = (t == 2)
        j = 1 if sample else 0
        wb = l * NSL + 76
        h = Tile(8)
        norm_mod(t, 1, j, h)
        hv = h.ap().rearrange("p (c f) -> p c f", c=8)
        mix = [Tile() for _ in range(9)] if not sample else None
        tcol = t * 512
        QA = [Tile() for _ in range(3)]
        KA = [Tile() for _ in range(2)]
        dsts = QA + KA
        rotA = (0, 1, C_PSWA) if sample else None
        qk_project([(wb + 0, 2), (wb + 1, 2), (wb + 2, 1)], hv, h, C_BLK64,
                   [pvec[:, l, 96:97]] * 3 + [pvec[:, l, 97:98]] * 2, dsts, rotA)
        (ps, pv, tbs), = proj_tm(wb + 3, hv, h, 128)
        avt = Tile()
        V_cp(avt.ap(), ps.ap(), ps.regs, avt.regs)
        if not sample:
            pool_st(av_o[l, tcol:tcol + 512, :].rearrange("(b p) c -> p b c", p=128), avt.ap().rearrange("p (b c) -> p b c", b=4), avt.regs)
            for kv in range(2):
                pool_st(ak_o[l, kv, :, tcol:tcol + 512], KA[kv].ap()[0:64, :], KA[kv].regs)
        else:
            for kv in range(2):
                d_ap, d_rg = xin_rows(kv * 64, 64)
                pool_st(d_ap, KA[kv].ap()[0:64, :], KA[kv].regs, [d_rg])
            d_ap, d_rg = xin_v(640, 128, 128)
            pool_st(d_ap.rearrange("(b p) c -> p b c", p=128),
                    avt.ap().rearrange("p (b c) -> p b c", b=4), avt.regs, [d_rg])
            for tl_ in KA:
                tl_.free()
        chk("Aproj")
        VAkv = []
        if not sample:
            VAo = Tile(2)
            vao_v = VAo.ap()[:, 0:768].rearrange("p (b n) -> p b n", b=4)
            fill(VAo, 1.0)
            VAo2 = Tile(2)
            fill(VAo2, 1.0)
            vao2_v = VAo2.ap()[:, 0:768].rearrange("p (b n) -> p b n", b=4)
            avv = avt.ap().rearrange("p (b c) -> p b c", b=4)
            V_cp(R(vao_v[:, :, 64:128]), avv[:, :, 0:64], avt.regs, VAo.regs, q=P.pool)
            V_cp(R(vao2_v[:, :, 64:128]), avv[:, :, 64:128], avt.regs, VAo2.regs, q=P.pool)
            VAkv = [(VAo, vao_v), (VAo2, vao2_v)]

            def keyfA(sid, kv, base):
                return [(KA[kv].ap()[base:base + 64, sid * 256 + kc * 128: sid * 256 + (kc + 1) * 128], KA[kv].regs) for kc in range(2)]

            def vwinA(sid, kv, par, i):
                tl_, vv = VAkv[kv]
                off = 64 if par == 0 else 0
                return (vv[:, sid * 2 + i, off:off + 128], tl_.regs)

            mixer_A_heads(l, QA, [(0, 256, 0), (256, 256, 1)], keyfA, vwinA, mix[0:3])
            VAo2.free()
            VAo.free()
            for tl_ in QA + KA:
                tl_.free()
        avt.free()
        chk("A")
        QC = [Tile() for _ in range(4)]
        KC = [Tile() for _ in range(4)]
        dsts = QC + KC
        rotC = (2, 3, C_PSWC) if sample else None
        qk_project([(wb + 4 + si, 2) for si in range(4)], hv, h, C_BLK48,
                   [pvec[:, l, 98:99]] * 4 + [pvec[:, l, 99:100]] * 4, dsts, rotC)
        cvt = Tile(3)
        cvv = cvt.ap().rearrange("p (b c) -> p b c", b=4)
        for (ps, pv, tbs) in proj_tm(wb + 8, hv, h, 256):
            V_cp(cvv[:, tbs[0]:tbs[0] + 2, 0:256], pv, ps.regs, cvt.regs)
        for (ps, pv, tbs) in proj_tm(wb + 9, hv, h, 128):
            V_cp(cvv[:, :, 256:384], pv, ps.regs, cvt.regs)
        if not sample:
            pool_st(cv_o[l, tcol:tcol + 512, :].rearrange("(b p) c -> p b c", p=128), cvv, cvt.regs)
            for hh in range(4):
                pool_st(ck_o[l, hh, :, tcol:tcol + 512], KC[hh].ap(), KC[hh].regs)
            VCo = [Tile() for _ in range(4)]
            for hh in range(4):
                vv = VCo[hh].ap().rearrange("p (b n) -> p b n", b=4)
                fill(VCo[hh], 1.0)
                V_cp(R(vv[:, :, 0:96]), cvv[:, :, hh * 96:(hh + 1) * 96], cvt.regs, VCo[hh].regs, q=P.pool)

            def keyfC(sid, hh, base):
                return [(KC[hh].ap()[base:base + 64, sid * 256 + kc * 128: sid * 256 + (kc + 1) * 128], KC[hh].regs) for kc in range(2)]

            def vwinC(sid, hh, i):
                vv = VCo[hh].ap().rearrange("p (b n) -> p b n", b=4)
                return (vv[:, sid * 2 + i, :], VCo[hh].regs)

            mixer_C_heads(l, QC, [0, 1, 2, 3], [(0, 256, 0), (256, 256, 1)], keyfC, vwinC, mix[5:9])
            for tl_ in VCo + QC + KC:
                tl_.free()
        else:
            for hh in range(4):
                d_ap, d_rg = xin_rows(128 + hh * 128, 128)
                pool_st(d_ap, KC[hh].ap(), KC[hh].regs, [d_rg])
            d_ap, d_rg = xin_v(768, 384, 384)
            pool_st(d_ap.rearrange("(b p) c -> p b c", p=128), cvv, cvt.regs, [d_rg])
            for b_ in range(2):
                E(P.pool, lambda e, b_=b_: e.collective_compute("AllGather", ALU.bypass, replica_groups=[[0, 1, 2, 3], [4, 5, 6, 7]],
                                                               ins=[xch_in_t[b_].ap().opt()], outs=[xch_out_t[b_].ap().opt()]),
                  [xin_rs[b_]], [xout_rs[b_]], tl=cc_tls[l * 3 + b_])
            for tl_ in KC:
                tl_.free()
        cvt.free()
        chk("C")
        BQ, BK, BR0, BR1, BG = Tile(), Tile(), Tile(), Tile(), Tile()
        raws = [BQ, BK, BR0, BR1, BG]
        ci = 0
        for si, nch in enumerate((2, 2, 1)):
            pss = proj_fm(wb + 10 + si, hv, h, nch)
            for ps in pss:
                dst = raws[ci]
                if ci in (2, 3):
                    A(dst.ap(), ps.ap(), AF.Silu, ps.regs, dst.regs)
                elif ci == 4:
                    A(R(dst.ap()[0:32, :]), ps.ap()[0:32, :], AF.Copy, ps.regs, dst.regs)
                else:
                    A(dst.ap(), ps.ap(), AF.Copy, ps.regs, dst.regs)
                ci += 1
        chk("Braw")
        KTM, VTM, Vpad = Tile(), Tile(2), Tile(4)
        ktm_v = KTM.ap().rearrange("p (b n) -> p b n", b=4)
        vtm_v = VTM.ap().rearrange("p (b n) -> p b n", b=4)
        vpad_v = Vpad.ap().rearrange("p (b h n) -> p b h n", b=4, h=4)
        if _os.environ.get("SKIP", "") != "fill":
            fill(Vpad, 0.0)
        for (ps, pv, tbs) in proj_tm(wb + 13, hv, h, 256):
            b0 = tbs[0]
            for bi in range(2):
                A(R(ktm_v[:, b0 + bi, :]), pv[:, bi, 0:128], AF.Copy, ps.regs, KTM.regs)
                A(R(vtm_v[:, b0 + bi, 0:128]), pv[:, bi, 128:256], AF.Copy, ps.regs, VTM.regs)

        chk("Btm1")
        for (ps, pv, tbs) in proj_tm(wb + 14, hv, h, 128):
            for bi in range(4):
                A(R(vtm_v[:, bi, 128:256]), pv[:, bi, :], AF.Copy, ps.regs, VTM.regs)
        for hh in range(4):
            o_ = (hh % 2) * 64
            V_cp(R(vpad_v[:, :, hh, o_:o_ + 64]), vtm_v[:, :, hh * 64:(hh + 1) * 64], VTM.regs, Vpad.regs, q=P.pool)
        h.free()
        chk("Bproj")
        if not sample:
            segs = [(0, 2, 0), (256, 2, 1)]
            osb, _ = gla(l, BQ, BK, BG, KTM, VTM, Vpad, segs, False,
                         lambda sid, d_: st_o[l, t * 2 + sid, d_, :, :])
            for tl_ in (BQ, BK, BG, KTM, VTM, Vpad):
                tl_.free()
            chk("gla")
            gla_out(l, osb, [BR0, BR1], mix[3:5])
            for tl_ in osb + [BR0, BR1]:
                tl_.free()
            chk("glaout")
            w_out(l, t, j, mix)
            for tl_ in mix:
                tl_.free()
            return
        osb, keep = gla(l, BQ, BK, BG, KTM, VTM, Vpad, [(0, 4, 0)], True, None)
        for tl_ in (BQ, BK, BG, KTM, VTM, Vpad):
            tl_.free()
        for b_ in range(2, 3):
            E(P.pool, lambda e, b_=b_: e.collective_compute("AllGather", ALU.bypass, replica_groups=[[0, 1, 2, 3], [4, 5, 6, 7]],
                                                           ins=[xch_in_t[b_].ap().opt()], outs=[xch_out_t[b_].ap().opt()]),
              [xin_rs[b_]], [xout_rs[b_]], tl=cc_tls[l * 3 + b_])
        mix = [Tile() for _ in range(9)]
        def gla_post():
            Sin = []
            for d_ in range(2):
                S = Tile()
                fill(S, 0.0, 256)
                for hh in range(4):
                    pool_ld(R(S.ap()[32 * hh:32 * hh + 32, 64 * hh:64 * hh + 64]), R(sg_d[l, d_, 32 * hh:32 * hh + 32, :]), S.regs)
                SL = Tile(2)
                slv = SL.ap().rearrange("p (r c) -> p r c", r=4)
                r0 = 1152 + 64 * d_
                for r in range(4):
                    s_ap, s_rg = xout_v(r, r0, 64, 256)
                    pool_ld(R(slv[:, r, :]), R(s_ap), SL.regs, [s_rg])
                AD = Tile()
                adv = AD.ap()[:, 0:8].rearrange("p (r c) -> p r c", r=4)
                for r in range(4):
                    s_ap, s_rg = xout_v(r, 1280, 1, 2, nfl=256)
                    pool_ld(R(adv[:, r, :]), R(s_ap), AD.regs, [s_rg])
                V_ts(AD.ap()[:, 8:16], AD.ap()[:, 0:8], -1.0, None, ALU.add, None, AD.regs, AD.regs)
                am1 = AD.ap()[:, 8:16].rearrange("p (r c) -> p r c", r=4)
                order = range(4) if d_ == 0 else range(3, -1, -1)
                tmp = Tile()
                for r in order:
                    V_stt(tmp.ap()[:, 0:256], S.ap()[:, 0:256], am1[:, r, d_:d_ + 1], slv[:, r, :], ALU.mult, ALU.add,
                          S.regs + AD.regs + SL.regs, tmp.regs)
                    V_stt(S.ap()[:, 0:256], tmp.ap()[:, 0:256], rmask[:, d_ * 4 + r:d_ * 4 + r + 1], S.ap()[:, 0:256], ALU.mult, ALU.add,
                          tmp.regs + S.regs + [const_r], S.regs)
                V_tt(R(tmp.ap()[:, 0:256]), S.ap()[:, 0:256], bdm[:], ALU.mult, S.regs + [const_r], tmp.regs)
                Sin.append(tmp)
                S.free()
                SL.free()
                AD.free()
            qin = [keep["qinf"], keep["qinb"]]
            for c2 in range(2):
                ps = PS("acc")
                mm(ps, ps.ap(), [(Sin[d_].ap()[:, c2 * 128:(c2 + 1) * 128], qin[d_].ap()) for d_ in range(2)],
                   Sin[0].regs + Sin[1].regs + qin[0].regs + qin[1].regs)
                V_tt(osb[c2].ap(), osb[c2].ap(), ps.ap(), ALU.add, osb[c2].regs + ps.regs, osb[c2].regs)
            for tl_ in Sin + qin:
                tl_.free()
            gla_out(l, osb, [BR0, BR1], mix[3:5])
            for tl_ in osb + [BR0, BR1]:
                tl_.free()

        VAll = Tile(8)
        vav = VAll.ap()[:, 0:3840].rearrange("p (k n) -> p k n", k=20)
        fill(VAll, 1.0)
        for kv in range(2):
            KAll = Tile(5)
            for half in range(2):
                s_ap, s_rg = xo_multi(kv * 64, 64)
                sp_ld(R(KAll.ap()[half * 64:(half + 1) * 64, 0:2048].rearrange("p (r c) -> p r c", r=4)),
                      R(s_ap), KAll.regs, [s_rg])
                sp_ld(R(KAll.ap()[half * 64:(half + 1) * 64, 2048:2560]), R(cak_d[l, kv, :, :]), KAll.regs)
            for r in range(4):
                s_ap, s_rg = xout_v(r, 640, 128, 128)
                src = s_ap.rearrange("(b p) c -> p b c", p=128)
                sp_ld(R(vav[:, r * 4:(r + 1) * 4, 64:128]), R(src[:, :, kv * 64:(kv + 1) * 64]), VAll.regs, [s_rg])
            sp_ld(R(vav[:, 16:20, 64:128]), R(cav_d[l, :, kv * 64:(kv + 1) * 64].rearrange("(b p) c -> p b c", p=128)), VAll.regs)
            jobs = []
            for g in range(3):
                hh = kv * 3 + g
                ch, par = hh // 2, hh % 2
                base = par * 64
                keys = [(KAll.ap()[base:base + 64, kc * 128:(kc + 1) * 128], KAll.regs) for kc in range(20)]

                def finish(acc, ch=ch, par=par):
                    nb_, db_ = (0, 64) if par == 0 else (64, 0)
                    rd = Tile()
                    V_rcp(rd.ap()[db_:db_ + 64, :], acc.ap()[db_:db_ + 64, :], acc.regs, rd.regs)
                    V_tt(R(mix[ch].ap()[nb_:nb_ + 64, :]), acc.ap()[nb_:nb_ + 64, :], rd.ap()[db_:db_ + 64, :], ALU.mult,
                         acc.regs + rd.regs, mix[ch].regs)
                    rd.free()

                off = 64 if par == 0 else 0
                jobs.append(dict(qt=QA[ch], qbase=base, ncol=512, q0=0, keys=keys, scale=0.125,
                                 vwin=(lambda i, off=off: (vav[:, i, off:off + 128], VAll.regs)), finish=finish))
            attn_jobs(jobs)
            KAll.free()
        VAll.free()
        for tl_ in QA:
            tl_.free()
        gla_post()
        KCh = Tile(5)
        VCh = Tile(5)
        vcv = VCh.ap().rearrange("p (k n) -> p k n", k=20)

        def keyfC2(sid, hh, base):
            return [(KCh.ap()[base:base + 64, kc * 128:(kc + 1) * 128], KCh.regs) for kc in range(20)]

        def vwinC2(sid, hh, i):
            return (vcv[:, i, :], VCh.regs)

        fill(VCh, 1.0)
        for hh in range(4):
            s_ap, s_rg = xo_multi(128 + hh * 128, 128)
            sp_ld(R(KCh.ap()[:, 0:2048].rearrange("p (r c) -> p r c", r=4)), R(s_ap), KCh.regs, [s_rg])
            sp_ld(R(KCh.ap()[:, 2048:2560]), R(cck_d[l, hh, :, :]), KCh.regs)
            for r in range(4):
                s_ap, s_rg = xout_v(r, 768, 384, 384)
                src = s_ap.rearrange("(b p) c -> p b c", p=128)
                sp_ld(R(vcv[:, r * 4:(r + 1) * 4, 0:96]), R(src[:, :, hh * 96:(hh + 1) * 96]), VCh.regs, [s_rg])
            sp_ld(R(vcv[:, 16:20, 0:96]), R(ccv_d[l, :, hh * 96:(hh + 1) * 96].rearrange("(b p) c -> p b c", p=128)), VCh.regs)
            mixer_C_heads(l, [QC[hh]], [hh], [(0, 512, 0)], keyfC2, vwinC2, [mix[5 + hh]])
        KCh.free()
        VCh.free()
        for tl_ in QC:
            tl_.free()
        w_out(l, t, j, mix)
        for tl_ in mix:
            tl_.free()

    step = [0]

    def go():
        step[0] += 1
        return step[0] <= upto

    for l in range(nl):
        if go():
            adaln(l)
        if go():
            ffn_multi(l, [0, 1], 0)
        if go():
            mixer(l, 0)
        if go():
            mixer(l, 1)
        if go():
            ffn_multi(l, [0, 1], 1)
        if go():
            ffn_multi(l, [2], 0)
        if go():
            mixer(l, 2)
        if go():
            ffn_multi(l, [2], 1)
    for t in range(3):
        pool_st(yT_o[:, :, t * 512:(t + 1) * 512], xT[:, :, t * 512:(t + 1) * 512], [x_r[c][t] for c in range(8)])

    for t in P.tls:
        t.sem = es.enter_context(nc.semaphore(t.name))
    final_waits = [(t, t.count) for t in P.dma_tls if t.count > 0]
    with nc.allow_low_precision("float32r PE operands"), nc.Block() as block:
        @block.sync
        def _(e):
            P.replay(P.sp, e)

        @block.tensor
        def _(e):
            P.replay(P.pe, e)

        @block.scalar
        def _(e):
            P.replay(P.act, e)

        @block.vector
        def _(e):
            P.replay(P.dve, e)

        @block.gpsimd
        def _(e):
            P.replay(P.pool, e)
            for t, v in final_waits:
                e.wait_ge(t.sem, v)
    es.close()
    return nc


OFF = dict(a_q=0, a_k=384, a_v=512, b_q=640, b_k=768, b_v=896, b_g=1152, b_r=1184, c_q=1440, c_k=1824, c_v=2208)


def _w_in_cols():
    def pad(lst, n=256):
        return list(lst) + [-1] * (n - len(lst))

    def rng(a, n):
        return list(range(a, a + n))

    def cmap(base, hh):
        out = []
        for m in range(2):
            out += rng(base + hh * 96 + m * 48, 48) + [-1] * 16
        return out

    slots = []
    slots.append(rng(OFF["a_q"], 256))
    slots.append(rng(OFF["a_q"] + 256, 128) + rng(OFF["a_k"], 64) * 2)
    slots.append(pad(rng(OFF["a_k"] + 64, 64) * 2))
    slots.append(pad(rng(OFF["a_v"], 128)))
    for base in (OFF["c_q"], OFF["c_k"]):
        slots.append(cmap(base, 0) + cmap(base, 1))
        slots.append(cmap(base, 2) + cmap(base, 3))
    slots.append(rng(OFF["c_v"], 256))
    slots.append(pad(rng(OFF["c_v"] + 256, 128)))
    slots.append(rng(OFF["b_q"], 128) + rng(OFF["b_k"], 128))
    slots.append(rng(OFF["b_r"], 256))
    slots.append(pad(rng(OFF["b_g"], 32)))
    slots.append(rng(OFF["b_k"], 128) + rng(OFF["b_v"], 128))
    slots.append(pad(rng(OFF["b_v"] + 128, 128)))
    assert len(slots) == 15
    return np.array(slots, dtype=np.int64)


def pack_weights(inp, nl):
    NSL = 135
    wst = np.zeros((nl * NSL, 128, SLOTF), np.float32)
    cols = _w_in_cols()
    for l in range(nl):
        b = l * NSL
        for s in range(2):
            wg = inp["w_ffn_gate"][l, s].reshape(8, 128, 22, 128).transpose(2, 1, 0, 3)
            wu = inp["w_ffn_up"][l, s].reshape(8, 128, 22, 128).transpose(2, 1, 0, 3)
            gu = np.stack([wg, wu], axis=2).reshape(22, 128, SLOTF)
            wd = inp["w_ffn_down"][l, s].reshape(2, 11, 128, 8, 128).transpose(0, 3, 2, 1, 4).reshape(2, 8, 128, 1408)
            for half in range(2):
                o = b + s * 38 + half * 19
                wst[o:o + 11] = gu[half * 11:(half + 1) * 11]
                wst[o + 11:o + 19, :, 0:1408] = wd[half]
        wi = np.concatenate([inp["w_in"][l], np.zeros((D, 1), np.float32)], axis=1)
        for si in range(15):
            wc = wi[:, cols[si]]
            wst[b + 76 + si] = wc.reshape(8, 128, 256).transpose(1, 0, 2).reshape(128, SLOTF)
        wo = inp["w_out"][l]
        wpad = np.zeros((9, 128, D), np.float32)
        for c in range(5):
            wpad[c] = wo[c * 128:(c + 1) * 128]
        for c in range(4):
            wpad[5 + c, 0:96] = wo[640 + c * 96:640 + (c + 1) * 96]
        wst[b + 91:b + 99, :, 0:1152] = wpad.reshape(9, 128, 8, 128).transpose(2, 1, 0, 3).reshape(8, 128, 1152)
        wa = inp["w_ada"][l].reshape(8, 128, 36, 256).transpose(2, 1, 0, 3).reshape(36, 128, SLOTF)
        wst[b + 99:b + 135] = wa
    return wst


def make_consts():
    cm = np.zeros((128, NCM, 128), np.float32)
    p = np.arange(128)
    cm[:, C_ONESM, :] = 1.0 / D
    cm[:, C_BLK64, :] = (p[:, None] // 64 == p[None, :] // 64) / 64.0
    real48 = (p % 64) < 48
    cm[:, C_BLK48, :] = ((p[:, None] // 64 == p[None, :] // 64) & real48[:, None] & real48[None, :]) / 48.0
    cm[0:96, C_BLK96, 0:96] = 1.0 / 96.0
    cm[96:128, C_SEL32, 0:96] = 1.0 / 32.0
    permA = np.zeros(128, np.int64)
    for m in range(128):
        d = m % 64
        permA[m] = m + 16 if (d % 32) < 16 else m - 16
    cm[permA, C_PSWA, p] = 1.0
    permC = np.arange(128)
    for m in range(128):
        d = m % 64
        if d < 48:
            permC[m] = m + 12 if (d % 24) < 12 else m - 12
    for m in range(128):
        if (m % 64) < 48:
            cm[permC[m], C_PSWC, m] = 1.0
    cm[:, C_TRIU, :] = (p[:, None] <= p[None, :])
    cm[:, C_TRIL, :] = (p[:, None] >= p[None, :])
    cm[:, C_SL16, :] = (p[:, None] > p[None, :]).astype(np.float32) / -16.0
    cm[:, C_SU16, :] = (p[:, None] < p[None, :]).astype(np.float32) / -16.0
    ur = np.zeros((128, 2, 512), np.float32)
    ur[:, 0, :] = -1.0 / 16.0
    ur[:, 0, 0:128] = (p[:, None] <= p[None, :]).astype(np.float32) / -16.0
    ur[:, 1, :] = -1.0 / 16.0
    ur[:, 1, 384:512] = (p[:, None] >= p[None, :]).astype(np.float32) / -16.0
    hm = np.zeros((128, 4), np.float32)
    for hh in range(4):
        hm[32 * hh:32 * hh + 32, hh] = 32 ** -0.5
    bd = np.zeros((128, 256), np.float32)
    for hh in range(4):
        bd[32 * hh:32 * hh + 32, 64 * hh:64 * hh + 64] = 1.0
    return cm, ur, hm, bd


def rope_tables(tok0):
    t = np.arange(tok0, tok0 + 512)
    row = (t // 64).astype(np.float32)
    col = (t % 64).astype(np.float32)
    out = np.zeros((128, 4, 512), np.float32)
    out[:, 0, :] = 1.0
    out[:, 2, :] = 1.0
    for p in range(128):
        d = p % 64
        half, dd = d // 32, d % 32
        i = dd % 16
        f = np.float32(10000.0) ** (-np.float32(i) / np.float32(16))
        ang = (row if half == 0 else col) * np.float32(f)
        out[p, 0] = np.cos(ang)
        out[p, 1] = (-np.sin(ang)) if dd < 16 else np.sin(ang)
        if d < 48:
            half, dd = d // 24, d % 24
            i = dd % 12
            f = np.float32(10000.0) ** (-np.float32(i) / np.float32(12))
            ang = (row if half == 0 else col) * np.float32(f)
            out[p, 2] = np.cos(ang)
            out[p, 3] = (-np.sin(ang)) if dd < 12 else np.sin(ang)
        else:
            out[p, 2] = 1.0
            out[p, 3] = 0.0
    return out


def pack_pvec(inp, nl):
    pv = np.zeros((128, nl, NV), np.float32)
    p = np.arange(128)
    for l in range(nl):
        pv[:, l, 0:24] = inp["g_norm"][l].reshape(3, 8, 128).transpose(2, 0, 1).reshape(128, 24)
        pv[:, l, 24:96] = inp["b_ada"][l].reshape(72, 128).T
        pv[:, l, 96] = inp["g_a_q"][l][p % 64]
        pv[:, l, 97] = inp["g_a_k"][l][p % 64]
        for col, key in ((98, "g_c_q"), (99, "g_c_k")):
            g = np.zeros((2, 64), np.float32)
            g[:, 0:48] = inp[key][l]
            pv[:, l, col] = g.reshape(128)
        pv[:, l, 100] = inp["g_gla"][l][p % 64]
        pv[0:96, l, 101] = inp["g_c_out"][l]
    return pv


_NC_CACHE = {}


def kernel(**inp):
    return run(inp, L_FULL)


def run(inp, nl, trace=False):
    inp = {k: np.asarray(v) for k, v in inp.items()}
    if nl not in _NC_CACHE:
        _NC_CACHE[nl] = build(nl)
    nc = _NC_CACHE[nl]
    wst = pack_weights(inp, nl)
    cm, ur, hm, bd = make_consts()
    pv = pack_pvec(inp, nl)
    bgla = np.broadcast_to(inp["b_gla"][:nl].reshape(1, nl, 256), (128, nl, 256)).copy()
    w2 = np.zeros((32, nl, 256), np.float32)
    for l in range(nl):
        w2[0:16, l, 0:128] = inp["w_gla_up"][l, 0]
        w2[16:32, l, 128:256] = inp["w_gla_up"][l, 1]
    lamc = np.broadcast_to(inp["lam_c"][:nl].reshape(1, nl * 4 * 48), (128, nl * 4 * 48)).copy()
    ozc = np.zeros((128, 2, 512), np.float32)
    ozc[:, 1, :] = 1.0
    in_maps = []
    for c in range(8):
        b, r = c // 4, c % 4
        xp = inp["x_prompt"][4 * c:4 * c + 4].reshape(1024, D)
        xs = inp["x_sample"][b, r * 512:(r + 1) * 512]
        xt = np.concatenate([xp, xs], axis=0)
        xin = xt.reshape(NTOK, 8, 128).transpose(2, 1, 0).copy()
        cond = np.stack([inp["c_ctx"], inp["c"][b]], axis=1).reshape(8, 128, 2).transpose(1, 0, 2).copy()
        rm = np.zeros((128, 8), np.float32)
        for rr in range(4):
            rm[:, rr] = 1.0 if rr < r else 0.0
            rm[:, 4 + rr] = 1.0 if rr > r else 0.0
        cak = inp["cache_a_k"][b, :nl].transpose(0, 2, 3, 1).copy()
        cav = inp["cache_a_v"][b, :nl].reshape(nl, 512, 128).copy()
        ck = inp["cache_c_k"][b, :nl].reshape(nl, 512, 4, 2, 48)
        cck = np.zeros((nl, 4, 2, 64, 512), np.float32)
        cck[:, :, :, 0:48, :] = ck.transpose(0, 2, 3, 4, 1)
        cck = cck.reshape(nl, 4, 128, 512)
        ccv = inp["cache_c_v"][b, :nl].reshape(nl, 512, 384).copy()
        sg = inp["state_gla"][b, :nl].reshape(nl, 2, 128, 64).copy()
        in_maps.append(dict(wst=wst, xin=xin, cmat=cm, urow=ur, rope=rope_tables(r * 512), hmask=hm, bdmask=bd, pvec=pv, oz=ozc,
                            bgla=bgla, w2=w2, lamc=lamc, condT=cond, rmask=rm, cakT=cak, cav=cav, cckT=cck, ccv=ccv, sgla=sg))
    if trace:
        res = run_bass_kernel_spmd(nc, in_maps, core_ids=list(range(8)), trace=True)
        print("exec_time_ns", res.exec_time_ns)
    else:
        res = run_bass_kernel_spmd(nc, in_maps, core_ids=list(range(8)))
    return assemble(res.results, nl)


def assemble(results, nl):
    y_prompt = np.zeros((32, 256, D), np.float32)
    y_sample = np.zeros((2, 2048, D), np.float32)
    n_ak = np.zeros((32, nl, 256, 2, 64), np.float32)
    n_av = np.zeros((32, nl, 256, 2, 64), np.float32)
    n_ck = np.zeros((32, nl, 256, 4, 96), np.float32)
    n_cv = np.zeros((32, nl, 256, 4, 96), np.float32)
    n_st = np.zeros((32, nl, 2, 4, 32, 64), np.float32)
    for c in range(8):
        r = results[c]
        b, rk = c // 4, c % 4
        y = np.asarray(r["yT"]).transpose(2, 1, 0).reshape(NTOK, D)
        y_prompt[4 * c:4 * c + 4] = y[0:1024].reshape(4, 256, D)
        y_sample[b, rk * 512:(rk + 1) * 512] = y[1024:]
        ak = np.asarray(r["akT"])
        n_ak[4 * c:4 * c + 4] = ak.reshape(nl, 2, 64, 4, 256).transpose(3, 0, 4, 1, 2)
        av = np.asarray(r["av"])
        n_av[4 * c:4 * c + 4] = av.reshape(nl, 4, 256, 2, 64).transpose(1, 0, 2, 3, 4)
        ck = np.asarray(r["ckT"]).reshape(nl, 4, 2, 64, 4, 256)[:, :, :, 0:48]
        n_ck[4 * c:4 * c + 4] = ck.transpose(4, 0, 5, 1, 2, 3).reshape(4, nl, 256, 4, 96)
        cv = np.asarray(r["cv"])
        n_cv[4 * c:4 * c + 4] = cv.reshape(nl, 4, 256, 4, 96).transpose(1, 0, 2, 3, 4)
        st = np.asarray(r["st"])
        n_st[4 * c:4 * c + 4] = st.reshape(nl, 4, 2, 4, 32, 64).transpose(1, 0, 2, 3, 4, 5)
    return (y_prompt, y_sample, n_ak, n_av, n_ck, n_cv, n_st)
```

```python
import math
from contextlib import ExitStack

import numpy as np
import concourse.bass as bass
import concourse.mybir as mybir
from concourse.bass_utils import run_bass_kernel_spmd

F32 = mybir.dt.float32
F32R = mybir.dt.float32r
AF = mybir.ActivationFunctionType
ALU = mybir.AluOpType
AX = mybir.AxisListType

D = 1024
L_FULL = 4
DFF = 2816
NTOK = 1536
EPS = 1e-6
NSLOT = 4
SLOTF = 2048
NA = 44
XR = 1281
NV = 104

(C_ONESM, C_BLK64, C_BLK48, C_BLK96, C_SEL32, C_PSWA, C_PSWC, C_TRIU, C_TRIL, C_SL16, C_SU16) = range(11)
NCM = 11


def R(ap):
    return ap if ap.dtype == F32R else ap.bitcast(F32R)


def RO(ap):
    return R(ap) if ap.name == "arena" else ap


class TL:
    def __init__(self, name, step):
        self.name, self.step, self.count, self.sem = name, step, 0, None


class Reg:
    __slots__ = ("name", "w", "r", "excl")

    def __init__(self, name, excl=False):
        self.name, self.w, self.r, self.excl = name, None, {}, excl


class Q:
    def __init__(self, name, no_self=False):
        self.name = name
        self.tl = TL(name, 1)
        self.ops = []
        self.seen = {}
        self.no_self = no_self


class Prog:
    def __init__(self):
        self.pe = Q("pe", no_self=True)
        self.act = Q("act")
        self.dve = Q("dve")
        self.pool = Q("pool")
        self.sp = Q("sp")
        self.tls = [self.pe.tl, self.act.tl, self.dve.tl, self.pool.tl]
        self.dma_tls = []

    def new_dma_tl(self, name):
        t = TL(name, 16)
        self.tls.append(t)
        self.dma_tls.append(t)
        return t

    def emit(self, q, fn, reads=(), writes=(), tl=None):
        dma = tl is not None
        tl = tl or q.tl
        need = {}

        def req(t, v):
            if v > need.get(t, 0):
                need[t] = v

        for r in reads:
            if r.w:
                req(*r.w)
            if r.excl:
                for t, v in r.r.items():
                    if t is not tl:
                        req(t, v)
        for w in writes:
            if w.w:
                req(*w.w)
            for t, v in w.r.items():
                req(t, v)
        if dma and tl.count > 0:
            req(tl, tl.count)
        waits = []
        for t, v in need.items():
            if t is q.tl and q.no_self and not dma:
                continue
            if q.seen.get(t, 0) < v:
                waits.append((t, v))
                q.seen[t] = v
        tl.count += tl.step
        my = tl.count
        q.ops.append((waits, fn, tl))
        for w in writes:
            w.w = (tl, my)
            w.r = {}
        for r in reads:
            r.r[tl] = my

    def replay(self, q, eng):
        for waits, fn, tl in q.ops:
            for t, v in waits:
                eng.wait_ge(t.sem, v)
            ins = fn(eng)
            ins.then_inc(tl.sem, tl.step)


class Rot:
    def __init__(self, items):
        self.items, self.i = list(items), 0

    def next(self):
        it = self.items[self.i % len(self.items)]
        self.i += 1
        return it


class Arena:
    def __init__(self, n):
        self.n = n
        self.free = [True] * n

    def alloc(self, k=1):
        for s in range(self.n - k + 1):
            if all(self.free[s:s + k]):
                for i in range(s, s + k):
                    self.free[i] = False
                return s
        raise RuntimeError(f"arena exhausted (need {k}, free {sum(self.free)})")

    def release(self, s, k=1):
        for i in range(s, s + k):
            assert not self.free[i]
            self.free[i] = True


def build(nl=L_FULL, taps=(), upto=10 ** 9):
    nc = bass.Bass("TRN2", target_bir_lowering=False)
    nc.dge_precook = False
    P = Prog()
    es = ExitStack()

    def din(name, shape):
        return nc.dram_tensor(name, list(shape), F32, kind="ExternalInput").ap()

    def dout(name, shape):
        return nc.dram_tensor(name, list(shape), F32, kind="ExternalOutput").ap()

    NSL = 135
    wst = din("wst", [nl * NSL, 128, SLOTF])
    xin = din("xin", [128, 8, NTOK])
    cmat_d = din("cmat", [128, NCM, 128])
    urow_d = din("urow", [128, 2, 512])
    rope_d = din("rope", [128, 4, 512])
    hmask_d = din("hmask", [128, 4])
    bd_d = din("bdmask", [128, 256])
    pvec_d = din("pvec", [128, nl, NV])
    bgla_d = din("bgla", [128, nl, 256])
    w2_d = din("w2", [32, nl, 256])
    lamc_d = din("lamc", [128, nl * 4 * 48])
    cond_d = din("condT", [128, 8, 2])
    rmask_d = din("rmask", [128, 8])
    oz_d = din("oz", [128, 2, 512])
    cak_d = din("cakT", [nl, 2, 64, 512])
    cav_d = din("cav", [nl, 512, 128])
    cck_d = din("cckT", [nl, 4, 128, 512])
    ccv_d = din("ccv", [nl, 512, 384])
    sg_d = din("sgla", [nl, 2, 128, 64])

    yT_o = dout("yT", [128, 8, NTOK])
    ak_o = dout("akT", [nl, 2, 64, 1024])
    av_o = dout("av", [nl, 1024, 128])
    ck_o = dout("ckT", [nl, 4, 128, 1024])
    cv_o = dout("cv", [nl, 1024, 384])
    st_o = dout("st", [nl, 4, 2, 128, 64])
    tap_o = {name: dout("tap_" + name, shape) for name, shape in taps}

    XB = [512, 512, 264]
    xch_in_t = [nc.dram_tensor(f"xch_in{b}", [XB[b], 512], F32) for b in range(3)]
    xch_out_t = [nc.dram_tensor(f"xch_out{b}", [4 * XB[b], 512], F32) for b in range(3)]

    def xloc(row):
        if row < 512:
            return 0, row
        if row < 640:
            return 1, row - 512
        if row < 768:
            return 2, row - 640
        if row < 1152:
            return 1, row - 768 + 128
        return 2, row - 1152 + 128

    xin_flat = [t_.ap().rearrange("r c -> (r c)") for t_ in xch_in_t]
    xout_flat = [t_.ap().rearrange("r c -> (r c)") for t_ in xch_out_t]

    def xin_rows(row0, nrows):
        b, lr = xloc(row0)
        return xch_in_t[b].ap()[lr:lr + nrows, :], xin_rs[b]

    def xin_v(row0, nrows, c, nfl=None):
        b, lr = xloc(row0)
        n = nrows * 512 if nfl is None else nfl
        return xin_flat[b][lr * 512:lr * 512 + n].rearrange("(t c) -> t c", c=c), xin_rs[b]

    def xo_multi(row0, nrows):
        b, lr = xloc(row0)
        v = xch_out_t[b].ap().rearrange("(r x) c -> r x c", r=4)
        return v[:, lr:lr + nrows, :].rearrange("r p c -> p r c"), xout_rs[b]

    def xout_v(r, row0, nrows, c, nfl=None):
        b, lr = xloc(row0)
        o = (r * XB[b] + lr) * 512
        n = nrows * 512 if nfl is None else nfl
        return xout_flat[b][o:o + n].rearrange("(t c) -> t c", c=c), xout_rs[b]

    def sb(name, shape):
        return es.enter_context(nc.sbuf_tensor(name, list(shape), F32))

    xT = sb("xT", [128, 8, NTOK])
    ring = sb("ring", [128, NSLOT, SLOTF])
    arena = sb("arena", [128, NA, 512])
    cmat = sb("cmat_s", [128, NCM, 128])
    urow = sb("urow_s", [128, 2, 512])
    rope = sb("rope_s", [128, 4, 512])
    hmask = sb("hmask_s", [128, 4])
    bdm = sb("bd_s", [128, 256])
    pvec = sb("pvec_s", [128, nl, NV])
    bgla = sb("bgla_s", [128, nl, 256])
    w2 = sb("w2_s", [32, nl, 256])
    lamc = sb("lamc_s", [128, nl * 4 * 48])
    cond = sb("cond_s", [128, 8, 2])
    scT = sb("scT_s", [128, 8, 2])
    rmask = sb("rmask_s", [128, 8])
    oz = sb("oz_s", [128, 2, 512])
    modT2 = sb("modT_s", [128, 2, 72, 2])
    gsT2 = sb("gs_s", [128, 2, 3, 8, 2])
    gtT2 = sb("gt_s", [128, 2, 3, 8, 2])
    lam = sb("lam_s", [128, 4, 4])
    lamt = sb("lamt_s", [128, nl * 2 * 48])
    gco = sb("gco_s", [128, 4])
    small = sb("small_s", [128, 16])

    psb = [es.enter_context(nc.psum_tensor(f"ps{i}", [128, 512], F32)) for i in range(8)]

    x_r = [[Reg(f"x{c}_{t}") for t in range(3)] for c in range(8)]
    ring_r = [Reg(f"ring{s}") for s in range(NSLOT)]
    ar_r = [Reg(f"ar{i}") for i in range(NA)]
    ps_r = [Reg(f"ps{i}", excl=True) for i in range(8)]
    const_r = Reg("consts")
    mod_rs = [Reg("mod0"), Reg("mod1")]
    misc_r = Reg("misc")
    small_r = Reg("small")
    xin_rs = [Reg(f"xch_in{b}") for b in range(3)]
    xout_rs = [Reg(f"xch_out{b}") for b in range(3)]

    ar = Arena(NA)

    class Tile:
        def __init__(self, k=1):
            self.k = k
            self.s = ar.alloc(k)
            self.regs = ar_r[self.s:self.s + k]

        def ap(self):
            return arena[:, self.s:self.s + self.k, :].rearrange("p k f -> p (k f)") if self.k > 1 else arena[:, self.s, :]

        def free(self):
            ar.release(self.s, self.k)

    ps_tmp = Rot(range(0, 5))
    ps_acc = Rot(range(5, 8))

    class PS:
        def __init__(self, kind="tmp"):
            self.i = (ps_tmp if kind == "tmp" else ps_acc).next()
            self.regs = [ps_r[self.i]]

        def ap(self):
            return psb[self.i][:]

    slot_tl = [P.new_dma_tl(f"slot{s}") for s in range(NSLOT)]
    misc_tl = Rot([P.new_dma_tl(f"md{i}") for i in range(8)])
    out_tl = Rot([P.new_dma_tl(f"od{i}") for i in range(4)])
    cc_tls = []
    for i in range(nl * 3):
        t_ = TL(f"cc{i}", 1)
        P.tls.append(t_)
        cc_tls.append(t_)

    E = P.emit

    def dma(q, out, in_, reads, writes, tl):
        E(q, lambda e, out=out, in_=in_: e.dma_start(out=out, in_=in_), reads, writes, tl=tl)

    def pool_ld(out, in_, writes, reads=()):
        dma(P.pool, out, in_, list(reads), list(writes), misc_tl.next())

    def sp_ld(out, in_, writes, reads=()):
        dma(P.sp, out, in_, list(reads), list(writes), misc_tl.next())

    def pool_st(out, in_, reads, writes=()):
        dma(P.pool, out, in_, list(reads), list(writes), out_tl.next())

    ring_n = [0]

    def ring_load(idx, nfl=SLOTF):
        s = ring_n[0] % NSLOT
        ring_n[0] += 1
        dma(P.sp, R(ring[:, s, 0:nfl]), R(wst[idx, :, 0:nfl]), [], [ring_r[s]], slot_tl[s])
        return s

    def mm(ps, out_ap, pairs, reads, start=True, stop=True):
        n = len(pairs)

        def fn(e, pairs=pairs, out_ap=out_ap, start=start, stop=stop):
            ins = None
            for i, (lt, rh) in enumerate(pairs):
                ins = e.matmul(out_ap, R(lt), R(rh), start=(start and i == 0), stop=(stop and i == n - 1))
            return ins

        E(P.pe, fn, list(reads), ps.regs)

    def A(out, in_, func, reads, writes, bias=0.0, scale=1.0):
        out = RO(out)
        E(P.act, lambda e: e.activation(out, in_, func, bias=bias, scale=scale), list(reads), list(writes))

    def V_tt(out, in0, in1, op, reads, writes, q=None):
        out = RO(out)
        E(q or P.dve, lambda e: e.tensor_tensor(out, in0, in1, op), list(reads), list(writes))

    def V_ts(out, in0, s1, s2, op0, op1, reads, writes, q=None):
        out = RO(out)
        if op1 is None:
            E(q or P.dve, lambda e: e.tensor_scalar(out, in0, s1, None, op0), list(reads), list(writes))
        else:
            E(q or P.dve, lambda e: e.tensor_scalar(out, in0, s1, s2, op0, op1), list(reads), list(writes))

    def V_stt(out, in0, sc, in1, op0, op1, reads, writes, q=None):
        out = RO(out)
        E(q or P.dve, lambda e: e.scalar_tensor_tensor(out, in0, sc, in1, op0, op1), list(reads), list(writes))

    def V_rcp(out, in_, reads, writes):
        out = RO(out)
        E(P.dve, lambda e: e.reciprocal(out, in_), list(reads), list(writes))

    def V_cp(out, in_, reads, writes, q=None):
        out = RO(out)
        E(q or P.dve, lambda e: e.tensor_copy(out, in_), list(reads), list(writes))

    def fill(tl_, val, nfl=None, q=None):
        n = tl_.k * 512 if nfl is None else nfl
        a = tl_.ap()
        for o in range(0, n, 512):
            w = min(512, n - o)
            V_cp(a[:, o:o + w], oz[:, 1 if val == 1.0 else 0, 0:w], [const_r], [tl_.regs[o // 512]], q=(q or P.pool))

    def CM(i, rows=128, cols=128):
        return cmat[0:rows, i, 0:cols]

    def rsqrt_from(ps_ap, out_ap, reads, writes, tmp_ap, tmp_regs):
        A(tmp_ap, ps_ap, AF.Ln, reads, tmp_regs, bias=EPS)
        A(out_ap, tmp_ap, AF.Exp, tmp_regs, writes, scale=-0.5)

    def tap(name, ap, reads):
        if name in tap_o:
            pool_st(tap_o[name], ap, reads)

    pool_ld(R(cmat[:]), R(cmat_d), [const_r])
    pool_ld(R(urow[:]), R(urow_d), [const_r])
    pool_ld(rope[:], rope_d, [const_r])
    pool_ld(hmask[:], hmask_d, [const_r])
    pool_ld(bdm[:], bd_d, [const_r])
    pool_ld(pvec[:], pvec_d, [const_r])
    pool_ld(bgla[:], bgla_d, [const_r])
    pool_ld(R(w2[:]), R(w2_d), [const_r])
    pool_ld(lamc[:], lamc_d, [const_r])
    pool_ld(cond[:], cond_d, [const_r])
    pool_ld(rmask[:], rmask_d, [const_r])
    pool_ld(oz[:], oz_d, [const_r])
    for t in range(3):
        pool_ld(xT[:, :, t * 512:(t + 1) * 512], xin[:, :, t * 512:(t + 1) * 512], [x_r[c][t] for c in range(8)])

    lc = lamc[:].rearrange("p (l a b d) -> p l a b d", l=nl, a=2, b=2, d=48)
    lt_v = lamt[:].rearrange("p (l a d) -> p l a d", l=nl, a=2, d=48)
    V_tt(lt_v, lc[:, :, :, 0, :], lc[:, :, :, 1, :], ALU.mult, [const_r], [misc_r])
    for l in range(nl):
        E(P.dve, lambda e, l=l: e.tensor_reduce(lam[:, l, 2:4], lt_v[:, l, :, :], AX.X, ALU.add), [misc_r], [misc_r])
        A(lam[:, l, 2:4], lam[:, l, 2:4], AF.Exp, [misc_r], [misc_r])
        V_tt(lam[:, l, 0:1], lam[:, l, 2:3], lam[:, l, 3:4], ALU.subtract, [misc_r], [misc_r])
        li = 0.8 - 0.6 * math.exp(-0.3 * l)
        V_ts(lam[:, l, 0:1], lam[:, l, 0:1], float(li), None, ALU.add, None, [misc_r], [misc_r])
        V_ts(lam[:, l, 1:2], lam[:, l, 0:1], -1.0, None, ALU.mult, None, [misc_r], [misc_r])
        V_ts(gco[:, l:l + 1], pvec[:, l, 101:102], float(1.0 - li), None, ALU.mult, None, [const_r, misc_r], [misc_r])
    A(R(scT[:]), cond[:], AF.Silu, [const_r], [misc_r])

    ada_state = {}

    def ada_begin(l):
        ps = PS("acc")
        ada_state[l] = dict(ps=ps, pv=ps.ap()[:, 0:144].rearrange("p (c j) -> p c j", j=2), k=0)

    def ada_slot(l):
        st_ = ada_state[l]
        sl = st_["k"]
        if sl >= 36:
            return
        st_["k"] += 1
        ps, pv = st_["ps"], st_["pv"]
        s = ring_load(l * NSL + 99 + sl)
        sv = ring[:, s, :].rearrange("p (k n) -> p k n", k=8)
        for half in range(2):
            ch = sl * 2 + half
            mm(ps, pv[:, ch, :], [(sv[:, kc, half * 128:(half + 1) * 128], scT[:, kc, :]) for kc in range(8)],
               [ring_r[s], misc_r])

    def ada_end(l):
        st_ = ada_state[l]
        while st_["k"] < 36:
            ada_slot(l)
        ps, pv = st_["ps"], st_["pv"]
        par = l % 2
        modT, gsT, gtT, mod_r = modT2[:, par], gsT2[:, par], gtT2[:, par], mod_rs[par]
        for j in range(2):
            V_tt(modT[:, :, j], pv[:, :, j], pvec[:, l, 24:96], ALU.add, ps.regs + [const_r], [mod_r])
        for s3 in range(3):
            for j in range(2):
                V_stt(gsT[:, s3, :, j], modT[:, (3 * s3 + 1) * 8:(3 * s3 + 2) * 8, j], 1.0,
                      pvec[:, l, s3 * 8:(s3 + 1) * 8], ALU.add, ALU.mult, [mod_r, const_r], [mod_r])
                V_ts(gtT[:, s3, :, j], modT[:, (3 * s3 + 2) * 8:(3 * s3 + 3) * 8, j],
                     (1.0 if s3 == 1 else 0.5), None, ALU.mult, None, [mod_r], [mod_r])

    def norm_mod(l, t, s3, j, h):
        par = l % 2
        modT, gsT, mod_r = modT2[:, par], gsT2[:, par], mod_rs[par]
        ps = PS()
        sqs = [Tile(), Tile(), Tile()]
        for c in range(8):
            sq = sqs[c % 3]
            xa = xT[:, c, t * 512:(t + 1) * 512]
            V_tt(R(sq.ap()), xa, xa, ALU.mult, [x_r[c][t]], sq.regs, q=P.pool)
            mm(ps, ps.ap(), [(CM(C_ONESM), sq.ap())], sq.regs + [const_r], start=(c == 0), stop=(c == 7))
        tmp = Tile()
        rstd = Tile()
        rsqrt_from(ps.ap(), rstd.ap(), ps.regs, rstd.regs, tmp.ap(), tmp.regs)
        hv = h.ap().rearrange("p (c f) -> p c f", c=8)
        for c in range(8):
            t2 = sqs[c % 3]
            V_stt(t2.ap(), xT[:, c, t * 512:(t + 1) * 512], gsT[:, s3, c, j:j + 1], rstd.ap(), ALU.mult, ALU.mult,
                  [x_r[c][t], mod_r] + rstd.regs, t2.regs)
            A(R(hv[:, c, :]), t2.ap(), AF.Identity, t2.regs + [mod_r], [h.regs[c]], bias=modT[:, 3 * s3 * 8 + c, j:j + 1])
        for tl_ in sqs:
            tl_.free()
        tmp.free()
        rstd.free()

    def ffn_multi(l, tiles, s, hook=None):
        s3 = 0 if s == 0 else 2
        gtT, mod_r = gtT2[:, l % 2], mod_rs[l % 2]
        nt = len(tiles)
        js = [1 if t == 2 else 0 for t in tiles]
        hs = [Tile(8) for _ in tiles]
        for ti, t in enumerate(tiles):
            norm_mod(l, t, s3, js[ti], hs[ti])
        hvs = [h.ap().rearrange("p (c f) -> p c f", c=8) for h in hs]
        acts = [Tile(11) for _ in tiles]
        avs = [a_.ap().rearrange("p (c f) -> p c f", c=11) for a_ in acts]
        base = l * NSL + s * 38
        for half in range(2):
            for fcl in range(11):
                sl = ring_load(base + half * 19 + fcl)
                sv = ring[:, sl, :].rearrange("p (g k n) -> p g k n", g=2, k=8)
                for ti in range(nt):
                    psg, psu = PS(), PS()
                    mm(psg, psg.ap(), [(sv[:, 0, kc, :], hvs[ti][:, kc, :]) for kc in range(8)], [ring_r[sl]] + hs[ti].regs)
                    mm(psu, psu.ap(), [(sv[:, 1, kc, :], hvs[ti][:, kc, :]) for kc in range(8)], [ring_r[sl]] + hs[ti].regs)
                    sg = Tile()
                    A(sg.ap(), psg.ap(), AF.Silu, psg.regs, sg.regs)
                    V_tt(R(avs[ti][:, fcl, :]), sg.ap(), psu.ap(), ALU.mult, sg.regs + psu.regs, [acts[ti].regs[fcl]])
                    sg.free()
                if hook is not None:
                    hook()
            for m in range(8):
                sl = ring_load(base + half * 19 + 11 + m, 1408)
                sv = ring[:, sl, 0:1408].rearrange("p (f n) -> p f n", f=11)
                for ti, t in enumerate(tiles):
                    psy = PS()
                    mm(psy, psy.ap(), [(sv[:, f, :], avs[ti][:, f, :]) for f in range(11)], [ring_r[sl]] + acts[ti].regs)
                    xa = xT[:, m, t * 512:(t + 1) * 512]
                    V_stt(xa, psy.ap(), gtT[:, s3, m, js[ti]:js[ti] + 1], xa, ALU.mult, ALU.add,
                          psy.regs + [mod_r, x_r[m][t]], [x_r[m][t]])
                if hook is not None:
                    hook()
        for tl_ in acts + hs:
            tl_.free()

    def proj_fm(sl_idx, hv, h, nchunks):
        sl = ring_load(sl_idx)
        sv = ring[:, sl, :].rearrange("p (k n) -> p k n", k=8)
        out = []
        for i in range(nchunks):
            ps = PS()
            mm(ps, ps.ap(), [(sv[:, kc, i * 128:(i + 1) * 128], hv[:, kc, :]) for kc in range(8)], [ring_r[sl]] + h.regs)
            out.append(ps)
        return out

    def proj_tm(sl_idx, hv, h, ncols):
        sl = ring_load(sl_idx)
        sv = ring[:, sl, :].rearrange("p (k n) -> p k n", k=8)
        per = 512 // ncols
        res = []
        for g in range(0, 4, per):
            ps = PS()
            pv = ps.ap()[:, 0:per * ncols].rearrange("p (b n) -> p b n", b=per)
            for bi in range(per):
                tb = g + bi
                mm(ps, pv[:, bi, :], [(hv[:, kc, tb * 128:(tb + 1) * 128], sv[:, kc, 0:ncols]) for kc in range(8)],
                   [ring_r[sl]] + h.regs)
            res.append((ps, pv, list(range(g, g + per))))
        return res

    def make_qk_unit(sidx, i, ss, hv, h, blk, gcol_ap, dst, rot):
        st = {}

        def s0():
            if "sl" not in ss:
                ss["sl"] = ring_load(sidx)
            sl = ss["sl"]
            sv = ring[:, sl, :].rearrange("p (k n) -> p k n", k=8)
            ps = PS()
            mm(ps, ps.ap(), [(sv[:, kc, i * 128:(i + 1) * 128], hv[:, kc, :]) for kc in range(8)], [ring_r[sl]] + h.regs)
            st["raw"] = Tile()
            A(st["raw"].ap(), ps.ap(), AF.Copy, ps.regs, st["raw"].regs)

        def s1():
            st["sq"] = Tile()
            V_tt(st["sq"].ap(), st["raw"].ap(), st["raw"].ap(), ALU.mult, st["raw"].regs, st["sq"].regs, q=P.pool)

        def s2():
            st["ps2"] = PS()
            mm(st["ps2"], st["ps2"].ap(), [(CM(blk), st["sq"].ap())], st["sq"].regs + [const_r])

        def s3():
            st["rstd"] = Tile()
            rsqrt_from(st["ps2"].ap(), st["rstd"].ap(), st["ps2"].regs, st["rstd"].regs, st["sq"].ap(), st["sq"].regs)

        def s4():
            raw, rstd = st["raw"], st["rstd"]
            if rot is None:
                V_stt(dst.ap(), raw.ap(), gcol_ap, rstd.ap(), ALU.mult, ALU.mult, raw.regs + rstd.regs + [const_r], dst.regs)
                for k_ in ("raw", "sq", "rstd"):
                    st[k_].free()
            else:
                st["xn"] = Tile()
                V_stt(st["xn"].ap(), raw.ap(), gcol_ap, rstd.ap(), ALU.mult, ALU.mult, raw.regs + rstd.regs + [const_r], st["xn"].regs)

        def s5():
            st["ps3"] = PS()
            mm(st["ps3"], st["ps3"].ap(), [(CM(rot[2]), st["xn"].ap())], st["xn"].regs + [const_r])

        def s6():
            st["t1"] = Tile()
            V_tt(st["t1"].ap(), st["ps3"].ap(), rope[:, rot[1], :], ALU.mult, st["ps3"].regs + [const_r], st["t1"].regs)
            V_tt(st["raw"].ap(), st["xn"].ap(), rope[:, rot[0], :], ALU.mult, st["xn"].regs + [const_r], st["raw"].regs, q=P.pool)

        def s7():
            V_tt(dst.ap(), st["t1"].ap(), st["raw"].ap(), ALU.add, st["t1"].regs + st["raw"].regs, dst.regs)
            for k_ in ("raw", "sq", "rstd", "xn", "t1"):
                st[k_].free()

        return [s0, s1, s2, s3, s4] if rot is None else [s0, s1, s2, s3, s4, s5, s6, s7]

    def qk_project(slot_specs, hv, h, blk, gcols, dsts, rot):
        units = []
        ci = 0
        for (sidx, nch) in slot_specs:
            ss = {}
            for i in range(nch):
                units.append(make_qk_unit(sidx, i, ss, hv, h, blk, gcols[ci], dsts[ci], rot))
                ci += 1
        run_pipeline(units, spacing=2)

    def attn_jobs(jobs, LOOK=2):
        items = []
        for J in jobs:
            nk = len(J["keys"])
            per = 512 // J["ncol"]
            i = 0
            while i < nk:
                g = min(per, nk - i)
                items.append((J, i, g))
                i += g
        pend = []

        def do_pv(ent):
            J, i, g, pT = ent
            nk = len(J["keys"])
            ncol = J["ncol"]
            if i == 0:
                J["acc"] = PS("acc")
            acc = J["acc"]
            for u in range(g):
                vap, vregs = J["vwin"](i + u)
                mm(acc, acc.ap()[0:128, 0:ncol], [(vap, pT.ap()[:, u * ncol:(u + 1) * ncol])], vregs + pT.regs,
                   start=(i + u == 0), stop=(i + u == nk - 1))
            pT.free()
            if i + g == nk:
                J["finish"](acc)

        for (J, i, g) in items:
            ncol, qt, qb, q0 = J["ncol"], J["qt"], J["qbase"], J["q0"]
            ps = PS()
            for u in range(g):
                kap, kregs = J["keys"][i + u]
                mm(ps, ps.ap()[:, u * ncol:(u + 1) * ncol], [(kap, qt.ap()[qb:qb + 64, q0:q0 + ncol])], kregs + qt.regs)
            pT = Tile()
            A(pT.ap()[:, 0:g * ncol], ps.ap()[:, 0:g * ncol], AF.Exp, ps.regs, pT.regs, scale=J["scale"])
            pend.append((J, i, g, pT))
            if len(pend) > LOOK:
                do_pv(pend.pop(0))
        while pend:
            do_pv(pend.pop(0))

    def run_pipeline(units, spacing=2):
        n = len(units)
        ns = max(len(u) for u in units)
        for step in range((n - 1) * spacing + ns):
            for u in range(n):
                st_ = step - u * spacing
                if 0 <= st_ < len(units[u]):
                    units[u][st_]()

    def mixer_A_heads(l, QA, segs, keyf, vwinf, mixA):
        jobs = []
        for (q0, ncol, sid) in segs:
            for hh in range(6):
                ch, par, kv = hh // 2, hh % 2, hh // 3
                base = par * 64

                def finish(acc, ch=ch, par=par, q0=q0, ncol=ncol):
                    nb_, db_ = (0, 64) if par == 0 else (64, 0)
                    rd = Tile()
                    V_rcp(rd.ap()[db_:db_ + 64, 0:ncol], acc.ap()[db_:db_ + 64, 0:ncol], acc.regs, rd.regs)
                    V_tt(R(mixA[ch].ap()[nb_:nb_ + 64, q0:q0 + ncol]), acc.ap()[nb_:nb_ + 64, 0:ncol],
                         rd.ap()[db_:db_ + 64, 0:ncol], ALU.mult, acc.regs + rd.regs, mixA[ch].regs)
                    rd.free()

                jobs.append(dict(qt=QA[ch], qbase=base, ncol=ncol, q0=q0, keys=keyf(sid, kv, base), scale=0.125,
                                 vwin=(lambda i, sid=sid, kv=kv, par=par: vwinf(sid, kv, par, i)), finish=finish))
        attn_jobs(jobs)

    def c_out_norm(l, oc, mixCh):
        sq = Tile()
        V_tt(R(sq.ap()[0:96, :]), oc.ap()[0:96, :], oc.ap()[0:96, :], ALU.mult, oc.regs, sq.regs, q=P.pool)
        ps = PS()
        mm(ps, ps.ap()[0:96, :], [(CM(C_BLK96, 96, 96), sq.ap()[0:96, :])], sq.regs + [const_r])
        rstd = Tile()
        A(sq.ap()[0:96, :], ps.ap()[0:96, :], AF.Ln, ps.regs, sq.regs, bias=EPS)
        A(rstd.ap()[0:96, :], sq.ap()[0:96, :], AF.Exp, sq.regs, rstd.regs, scale=-0.5)
        V_stt(R(mixCh.ap()[0:96, :]), oc.ap()[0:96, :], gco[0:96, l:l + 1], rstd.ap()[0:96, :], ALU.mult, ALU.mult,
              oc.regs + rstd.regs + [misc_r], mixCh.regs)
        sq.free()
        rstd.free()
        oc.free()

    def mixer_C_heads(l, QC, heads, segs, keyf, vwinf, mixC):
        jobs = []
        for hi, hh in enumerate(heads):
            st_h = {}
            for si_, (q0, ncol, sid) in enumerate(segs):
                st_s = {}
                for m in range(2):
                    base = m * 64

                    def finish(acc, m=m, ncol=ncol, q0=q0, st_h=st_h, st_s=st_s, hi=hi, last_seg=(si_ == len(segs) - 1)):
                        if "oc" not in st_h:
                            st_h["oc"] = Tile()
                        oc = st_h["oc"]
                        accs = Tile()
                        A(accs.ap()[:, 0:ncol], acc.ap()[:, 0:ncol], AF.Copy, acc.regs, accs.regs)
                        psd = PS()
                        mm(psd, psd.ap()[0:96, 0:ncol], [(CM(C_SEL32, 128, 96), accs.ap()[:, 0:ncol])], accs.regs + [const_r])
                        rd = Tile()
                        V_rcp(rd.ap()[0:96, 0:ncol], psd.ap()[0:96, 0:ncol], psd.regs, rd.regs)
                        om = Tile()
                        st_s[m] = om
                        V_tt(om.ap()[0:96, 0:ncol], accs.ap()[0:96, 0:ncol], rd.ap()[0:96, 0:ncol], ALU.mult,
                             accs.regs + rd.regs, om.regs)
                        rd.free()
                        accs.free()
                        if m == 1:
                            V_stt(oc.ap()[0:96, q0:q0 + ncol], st_s[1].ap()[0:96, 0:ncol], lam[0:96, l, 1:2],
                                  st_s[0].ap()[0:96, 0:ncol], ALU.mult, ALU.add, st_s[0].regs + st_s[1].regs + [misc_r], oc.regs)
                            st_s[0].free()
                            st_s[1].free()
                            if last_seg:
                                c_out_norm(l, oc, mixC[hi])

                    jobs.append(dict(qt=QC[hi], qbase=base, ncol=ncol, q0=q0, keys=keyf(sid, hh, base), scale=48 ** -0.5,
                                     vwin=(lambda i, sid=sid, hh=hh: vwinf(sid, hh, i)), finish=finish))
        attn_jobs(jobs)

    def gla(l, BQ, BK, BG, KTM, VTM, Vpad, segs, sample, st_dst):
        ktm_v = KTM.ap().rearrange("p (b n) -> p b n", b=4)
        vtm_v = VTM.ap().rearrange("p (b n) -> p b n", b=4)
        vpad_v = Vpad.ap().rearrange("p (b h n) -> p b h n", b=4, h=4)
        Lt = Tile(2)
        Lv = Lt.ap().rearrange("p (b n) -> p b n", b=4)
        for tb in range(4):
            ps = PS()
            mm(ps, ps.ap()[:, 0:256], [(BG.ap()[0:32, tb * 128:(tb + 1) * 128], w2[0:32, l, :])], BG.regs + [const_r])
            zb = Tile()
            V_tt(zb.ap()[:, 0:256], ps.ap()[:, 0:256], bgla[:, l, :], ALU.add, ps.regs + [const_r], zb.regs)
            A(zb.ap()[:, 0:256], zb.ap()[:, 0:256], AF.Exp, zb.regs, zb.regs, scale=-1.0)
            A(R(Lv[:, tb, :]), zb.ap()[:, 0:256], AF.Ln, zb.regs, Lt.regs, bias=1.0)
            zb.free()
        osb = [Tile(), Tile()]
        keep = {}
        for (c0, nb, sid) in segs:
            T = nb * 128
            tb0 = c0 // 128
            mid = T // 2 - 1
            psf, psbk = PS(), PS()
            for jb in range(nb):
                mm(psf, psf.ap()[:, jb * 128:T], [(Lv[:, tb0 + jb, 0:128], urow[:, 0, 0:T - jb * 128])], Lt.regs + [const_r],
                   start=(jb == 0), stop=(jb == nb - 1))
            for idx, jb in enumerate(range(nb - 1, -1, -1)):
                mm(psbk, psbk.ap()[:, 0:(jb + 1) * 128], [(Lv[:, tb0 + jb, 128:256], urow[:, 1, (4 - 1 - jb) * 128:512])],
                   Lt.regs + [const_r], start=(idx == 0), stop=(idx == nb - 1))
            V_cp(small[:, 0:1], psf.ap()[:, mid:mid + 1], psf.regs, [small_r])
            V_ts(small[:, 1:2], psf.ap()[:, mid:mid + 1], -1.0, None, ALU.mult, None, psf.regs, [small_r])
            V_cp(small[:, 2:3], psbk.ap()[:, mid:mid + 1], psbk.regs, [small_r])
            V_ts(small[:, 3:4], psbk.ap()[:, mid:mid + 1], -1.0, None, ALU.mult, None, psbk.regs, [small_r])
            Ef, Enf, Eb, Enb = Tile(), Tile(), Tile(), Tile()
            A(Ef.ap()[:, 0:T], psf.ap()[:, 0:T], AF.Exp, psf.regs + [small_r], Ef.regs, bias=small[:, 1:2], scale=1.0)
            A(Enf.ap()[:, 0:T], psf.ap()[:, 0:T], AF.Exp, psf.regs + [small_r], Enf.regs, bias=small[:, 0:1], scale=-1.0)
            A(Eb.ap()[:, 0:T], psbk.ap()[:, 0:T], AF.Exp, psbk.regs + [small_r], Eb.regs, bias=small[:, 3:4], scale=1.0)
            A(Enb.ap()[:, 0:T], psbk.ap()[:, 0:T], AF.Exp, psbk.regs + [small_r], Enb.regs, bias=small[:, 2:3], scale=-1.0)
            if sample:
                A(small[:, 6:7], psf.ap()[:, T - 1:T], AF.Exp, psf.regs, [small_r])
                A(small[:, 7:8], psbk.ap()[:, 0:1], AF.Exp, psbk.regs, [small_r])
                A(small[:, 4:5], small[:, 0:1], AF.Exp, [small_r], [small_r])
                A(small[:, 5:6], small[:, 2:3], AF.Exp, [small_r], [small_r])
                V_ts(small[:, 4:6], small[:, 4:6], float(32 ** -0.5), None, ALU.mult, None, [small_r], [small_r])
                qinf, qinb = Tile(), Tile()
                V_stt(R(qinf.ap()), Ef.ap(), small[:, 4:5], BQ.ap(), ALU.mult, ALU.mult, Ef.regs + BQ.regs + [small_r], qinf.regs)
                V_stt(R(qinb.ap()), Eb.ap(), small[:, 5:6], BQ.ap(), ALU.mult, ALU.mult, Eb.regs + BQ.regs + [small_r], qinb.regs)
                keep["qinf"], keep["qinb"] = qinf, qinb
                d_ap, d_rg = xin_v(1280, 1, 2, nfl=256)
                pool_st(d_ap, small[:, 6:8], [small_r], [d_rg])
            ktf, ktb = Tile(), Tile()
            V_tt(R(ktf.ap()[:, 0:T]), BK.ap()[:, c0:c0 + T], Enf.ap()[:, 0:T], ALU.mult, BK.regs + Enf.regs, ktf.regs)
            V_tt(R(ktb.ap()[:, 0:T]), BK.ap()[:, c0:c0 + T], Enb.ap()[:, 0:T], ALU.mult, BK.regs + Enb.regs, ktb.regs)
            pso = [PS("acc"), PS("acc")]
            for hh in range(4):
                qf, qb = Tile(), Tile()
                V_stt(R(qf.ap()[:, 0:T]), Ef.ap()[:, 0:T], hmask[:, hh:hh + 1], BQ.ap()[:, c0:c0 + T], ALU.mult, ALU.mult,
                      Ef.regs + BQ.regs + [const_r], qf.regs)
                V_stt(R(qb.ap()[:, 0:T]), Eb.ap()[:, 0:T], hmask[:, hh:hh + 1], BQ.ap()[:, c0:c0 + T], ALU.mult, ALU.mult,
                      Eb.regs + BQ.regs + [const_r], qb.regs)
                MT = Tile(nb * T // 512)
                Mv = MT.ap().rearrange("p (b n) -> p b n", b=nb)
                for jb in range(nb):
                    pf, pb = PS(), PS()
                    mm(pf, pf.ap()[:, 0:T - jb * 128], [(ktf.ap()[:, jb * 128:(jb + 1) * 128], qf.ap()[:, jb * 128:T])],
                       ktf.regs + qf.regs)
                    mm(pb, pb.ap()[:, 0:(jb + 1) * 128], [(ktb.ap()[:, jb * 128:(jb + 1) * 128], qb.ap()[:, 0:(jb + 1) * 128])],
                       ktb.regs + qb.regs)
                    if jb > 0:
                        A(R(Mv[:, jb, 0:jb * 128]), pb.ap()[:, 0:jb * 128], AF.Copy, pb.regs, MT.regs)
                    if jb < nb - 1:
                        A(R(Mv[:, jb, (jb + 1) * 128:T]), pf.ap()[:, 128:T - jb * 128], AF.Copy, pf.regs, MT.regs)
                    t1 = Tile()
                    V_tt(t1.ap()[:, 0:128], pf.ap()[:, 0:128], CM(C_TRIU), ALU.mult, pf.regs + [const_r], t1.regs)
                    V_tt(t1.ap()[:, 128:256], pb.ap()[:, jb * 128:(jb + 1) * 128], CM(C_TRIL), ALU.mult, pb.regs + [const_r], t1.regs)
                    V_tt(R(Mv[:, jb, jb * 128:(jb + 1) * 128]), t1.ap()[:, 0:128], t1.ap()[:, 128:256], ALU.add, t1.regs, MT.regs,
                         q=P.pool)
                    t1.free()
                for jb in range(nb):
                    first = (hh % 2 == 0 and jb == 0)
                    last = (hh % 2 == 1 and jb == nb - 1)
                    mm(pso[hh // 2], pso[hh // 2].ap()[:, 0:T], [(vpad_v[:, tb0 + jb, hh, :], Mv[:, jb, :])], Vpad.regs + MT.regs,
                       start=first, stop=last)
                MT.free()
                qf.free()
                qb.free()
            for c2 in range(2):
                if not sample:
                    V_cp(osb[c2].ap()[:, c0:c0 + T], pso[c2].ap()[:, 0:T], pso[c2].regs, osb[c2].regs)
                else:
                    V_cp(osb[c2].ap()[:, 0:T], pso[c2].ap()[:, 0:T], pso[c2].regs, osb[c2].regs)
            for d_ in range(2):
                kd = Tile()
                kdv = kd.ap().rearrange("p (b n) -> p b n", b=4)
                for jb in range(nb):
                    ps = PS()
                    prs = []
                    if d_ == 0:
                        for j2 in range(jb, nb):
                            lt = CM(C_SL16) if j2 == jb else urow[:, 0, 128:256]
                            prs.append((lt, Lv[:, tb0 + j2, 0:128]))
                    else:
                        for j2 in range(0, jb + 1):
                            lt = CM(C_SU16) if j2 == jb else urow[:, 0, 128:256]
                            prs.append((lt, Lv[:, tb0 + j2, 128:256]))
                    mm(ps, ps.ap()[:, 0:128], prs, Lt.regs + [const_r])
                    ed = Tile()
                    A(ed.ap()[:, 0:128], ps.ap()[:, 0:128], AF.Exp, ps.regs, ed.regs)
                    V_tt(R(kdv[:, jb, :]), ktm_v[:, tb0 + jb, :], ed.ap()[:, 0:128], ALU.mult, KTM.regs + ed.regs, kd.regs)
                    ed.free()
                pst = PS()
                mm(pst, pst.ap()[:, 0:256], [(kdv[:, jb, :], vtm_v[:, tb0 + jb, :]) for jb in range(nb)], kd.regs + VTM.regs)
                kd.free()
                stt = Tile()
                if not sample:
                    for hh in range(4):
                        A(stt.ap()[32 * hh:32 * hh + 32, 0:64], pst.ap()[32 * hh:32 * hh + 32, 64 * hh:64 * hh + 64], AF.Copy,
                          pst.regs, stt.regs)
                    pool_st(st_dst(sid, d_), stt.ap()[:, 0:64], stt.regs)
                else:
                    A(stt.ap()[:, 0:256], pst.ap()[:, 0:256], AF.Copy, pst.regs, stt.regs)
                    r0 = 1152 + 64 * d_
                    d_ap, d_rg = xin_v(r0, 64, 256)
                    pool_st(d_ap, stt.ap()[:, 0:256], stt.regs, [d_rg])
                stt.free()
            for tl_ in (Ef, Enf, Eb, Enb, ktf, ktb):
                tl_.free()
        Lt.free()
        return osb, keep

    def gla_out(l, osb, GR, mixB):
        for c2 in range(2):
            sq = Tile()
            V_tt(R(sq.ap()), osb[c2].ap(), osb[c2].ap(), ALU.mult, osb[c2].regs, sq.regs, q=P.pool)
            ps = PS()
            mm(ps, ps.ap(), [(CM(C_BLK64), sq.ap())], sq.regs + [const_r])
            rstd = Tile()
            rsqrt_from(ps.ap(), rstd.ap(), ps.regs, rstd.regs, sq.ap(), sq.regs)
            V_stt(sq.ap(), osb[c2].ap(), pvec[:, l, 100:101], rstd.ap(), ALU.mult, ALU.mult, osb[c2].regs + rstd.regs + [const_r], sq.regs)
            V_tt(R(mixB[c2].ap()), sq.ap(), GR[c2].ap(), ALU.mult, sq.regs + GR[c2].regs, mixB[c2].regs)
            sq.free()
            rstd.free()

    def w_out(l, t, j, mix):
        gtT, mod_r = gtT2[:, l % 2], mod_rs[l % 2]
        base = l * NSL + 91
        for m in range(8):
            sl = ring_load(base + m, 1152)
            sv = ring[:, sl, 0:1152].rearrange("p (c n) -> p c n", c=9)
            psy = PS()
            prs, rds = [], [ring_r[sl]]
            for c in range(9):
                rows = 128 if c < 5 else 96
                prs.append((sv[0:rows, c, :], mix[c].ap()[0:rows, :]))
                rds += mix[c].regs
            mm(psy, psy.ap(), prs, rds)
            xa = xT[:, m, t * 512:(t + 1) * 512]
            V_stt(xa, psy.ap(), gtT[:, 1, m, j:j + 1], xa, ALU.mult, ALU.add, psy.regs + [mod_r, x_r[m][t]], [x_r[m][t]])

    import os as _os
    stopat = _os.environ.get("STOPAT", "")

    class _Stop(Exception):
        pass

    def chk(name):
        if stopat == name:
            raise _Stop()

    def mixer(l, t):
        try:
            mixer_(l, t)
        except _Stop:
            pass

    def mixer_(l, t):
        sample = (t == 2)
        j = 1 if sample else 0
        wb = l * NSL + 76
        h = Tile(8)
        norm_mod(l, t, 1, j, h)
        hv = h.ap().rearrange("p (c f) -> p c f", c=8)
        mix = [Tile() for _ in range(9)] if not sample else None
        tcol = t * 512
        QA = [Tile() for _ in range(3)]
        KA = [Tile() for _ in range(2)]
        dsts = QA + KA
        rotA = (0, 1, C_PSWA) if sample else None
        qk_project([(wb + 0, 2), (wb + 1, 2), (wb + 2, 1)], hv, h, C_BLK64,
                   [pvec[:, l, 96:97]] * 3 + [pvec[:, l, 97:98]] * 2, dsts, rotA)
        (ps, pv, tbs), = proj_tm(wb + 3, hv, h, 128)
        avt = Tile()
        V_cp(avt.ap(), ps.ap(), ps.regs, avt.regs)
        if not sample:
            pool_st(av_o[l, tcol:tcol + 512, :].rearrange("(b p) c -> p b c", p=128), avt.ap().rearrange("p (b c) -> p b c", b=4), avt.regs)
            for kv in range(2):
                pool_st(ak_o[l, kv, :, tcol:tcol + 512], KA[kv].ap()[0:64, :], KA[kv].regs)
        else:
            for kv in range(2):
                d_ap, d_rg = xin_rows(kv * 64, 64)
                pool_st(d_ap, KA[kv].ap()[0:64, :], KA[kv].regs, [d_rg])
            d_ap, d_rg = xin_v(640, 128, 128)
            pool_st(d_ap.rearrange("(b p) c -> p b c", p=128),
                    avt.ap().rearrange("p (b c) -> p b c", b=4), avt.regs, [d_rg])
            for tl_ in KA:
                tl_.free()
        chk("Aproj")
        VAkv = []
        if not sample:
            VAo = Tile(2)
            vao_v = VAo.ap()[:, 0:768].rearrange("p (b n) -> p b n", b=4)
            fill(VAo, 1.0)
            VAo2 = Tile(2)
            fill(VAo2, 1.0)
            vao2_v = VAo2.ap()[:, 0:768].rearrange("p (b n) -> p b n", b=4)
            avv = avt.ap().rearrange("p (b c) -> p b c", b=4)
            V_cp(R(vao_v[:, :, 64:128]), avv[:, :, 0:64], avt.regs, VAo.regs, q=P.pool)
            V_cp(R(vao2_v[:, :, 64:128]), avv[:, :, 64:128], avt.regs, VAo2.regs, q=P.pool)
            VAkv = [(VAo, vao_v), (VAo2, vao2_v)]

            def keyfA(sid, kv, base):
                return [(KA[kv].ap()[base:base + 64, sid * 256 + kc * 128: sid * 256 + (kc + 1) * 128], KA[kv].regs) for kc in range(2)]

            def vwinA(sid, kv, par, i):
                tl_, vv = VAkv[kv]
                off = 64 if par == 0 else 0
                return (vv[:, sid * 2 + i, off:off + 128], tl_.regs)

            mixer_A_heads(l, QA, [(0, 256, 0), (256, 256, 1)], keyfA, vwinA, mix[0:3])
            VAo2.free()
            VAo.free()
            for tl_ in QA + KA:
                tl_.free()
        avt.free()
        chk("A")
        QC = [Tile() for _ in range(4)]
        KC = [Tile() for _ in range(4)]
        dsts = QC + KC
        rotC = (2, 3, C_PSWC) if sample else None
        qk_project([(wb + 4 + si, 2) for si in range(4)], hv, h, C_BLK48,
                   [pvec[:, l, 98:99]] * 4 + [pvec[:, l, 99:100]] * 4, dsts, rotC)
        cvt = Tile(3)
        cvv = cvt.ap().rearrange("p (b c) -> p b c", b=4)
        for (ps, pv, tbs) in proj_tm(wb + 8, hv, h, 256):
            V_cp(cvv[:, tbs[0]:tbs[0] + 2, 0:256], pv, ps.regs, cvt.regs)
        for (ps, pv, tbs) in proj_tm(wb + 9, hv, h, 128):
            V_cp(cvv[:, :, 256:384], pv, ps.regs, cvt.regs)
        if not sample:
            pool_st(cv_o[l, tcol:tcol + 512, :].rearrange("(b p) c -> p b c", p=128), cvv, cvt.regs)
            for hh in range(4):
                pool_st(ck_o[l, hh, :, tcol:tcol + 512], KC[hh].ap(), KC[hh].regs)
            VCo = [Tile() for _ in range(4)]
            for hh in range(4):
                vv = VCo[hh].ap().rearrange("p (b n) -> p b n", b=4)
                fill(VCo[hh], 1.0)
                V_cp(R(vv[:, :, 0:96]), cvv[:, :, hh * 96:(hh + 1) * 96], cvt.regs, VCo[hh].regs, q=P.pool)

            def keyfC(sid, hh, base):
                return [(KC[hh].ap()[base:base + 64, sid * 256 + kc * 128: sid * 256 + (kc + 1) * 128], KC[hh].regs) for kc in range(2)]

            def vwinC(sid, hh, i):
                vv = VCo[hh].ap().rearrange("p (b n) -> p b n", b=4)
                return (vv[:, sid * 2 + i, :], VCo[hh].regs)

            mixer_C_heads(l, QC, [0, 1, 2, 3], [(0, 256, 0), (256, 256, 1)], keyfC, vwinC, mix[5:9])
            for tl_ in VCo + QC + KC:
                tl_.free()
        else:
            for hh in range(4):
                d_ap, d_rg = xin_rows(128 + hh * 128, 128)
                pool_st(d_ap, KC[hh].ap(), KC[hh].regs, [d_rg])
            d_ap, d_rg = xin_v(768, 384, 384)
            pool_st(d_ap.rearrange("(b p) c -> p b c", p=128), cvv, cvt.regs, [d_rg])
            for b_ in range(2):
                E(P.pool, lambda e, b_=b_: e.collective_compute("AllGather", ALU.bypass, replica_groups=[[0, 1, 2, 3], [4, 5, 6, 7]],
                                                               ins=[xch_in_t[b_].ap().opt()], outs=[xch_out_t[b_].ap().opt()]),
                  [xin_rs[b_]], [xout_rs[b_]], tl=cc_tls[l * 3 + b_])
            for tl_ in KC:
                tl_.free()
        cvt.free()
        chk("C")
        BQ, BK, BR0, BR1, BG = Tile(), Tile(), Tile(), Tile(), Tile()
        raws = [BQ, BK, BR0, BR1, BG]
        ci = 0
        for si, nch in enumerate((2, 2, 1)):
            pss = proj_fm(wb + 10 + si, hv, h, nch)
            for ps in pss:
                dst = raws[ci]
                if ci in (2, 3):
                    A(dst.ap(), ps.ap(), AF.Silu, ps.regs, dst.regs)
                elif ci == 4:
                    A(R(dst.ap()[0:32, :]), ps.ap()[0:32, :], AF.Copy, ps.regs, dst.regs)
                else:
                    A(dst.ap(), ps.ap(), AF.Copy, ps.regs, dst.regs)
                ci += 1
        chk("Braw")
        KTM, VTM, Vpad = Tile(), Tile(2), Tile(4)
        ktm_v = KTM.ap().rearrange("p (b n) -> p b n", b=4)
        vtm_v = VTM.ap().rearrange("p (b n) -> p b n", b=4)
        vpad_v = Vpad.ap().rearrange("p (b h n) -> p b h n", b=4, h=4)
        if _os.environ.get("SKIP", "") != "fill":
            fill(Vpad, 0.0)
        for (ps, pv, tbs) in proj_tm(wb + 13, hv, h, 256):
            b0 = tbs[0]
            for bi in range(2):
                A(R(ktm_v[:, b0 + bi, :]), pv[:, bi, 0:128], AF.Copy, ps.regs, KTM.regs)
                A(R(vtm_v[:, b0 + bi, 0:128]), pv[:, bi, 128:256], AF.Copy, ps.regs, VTM.regs)

        chk("Btm1")
        for (ps, pv, tbs) in proj_tm(wb + 14, hv, h, 128):
            for bi in range(4):
                A(R(vtm_v[:, bi, 128:256]), pv[:, bi, :], AF.Copy, ps.regs, VTM.regs)
        for hh in range(4):
            o_ = (hh % 2) * 64
            V_cp(R(vpad_v[:, :, hh, o_:o_ + 64]), vtm_v[:, :, hh * 64:(hh + 1) * 64], VTM.regs, Vpad.regs, q=P.pool)
        h.free()
        chk("Bproj")
        if not sample:
            segs = [(0, 2, 0), (256, 2, 1)]
            osb, _ = gla(l, BQ, BK, BG, KTM, VTM, Vpad, segs, False,
                         lambda sid, d_: st_o[l, t * 2 + sid, d_, :, :])
            for tl_ in (BQ, BK, BG, KTM, VTM, Vpad):
                tl_.free()
            chk("gla")
            gla_out(l, osb, [BR0, BR1], mix[3:5])
            for tl_ in osb + [BR0, BR1]:
                tl_.free()
            chk("glaout")
            w_out(l, t, j, mix)
            for tl_ in mix:
                tl_.free()
            return
        osb, keep = gla(l, BQ, BK, BG, KTM, VTM, Vpad, [(0, 4, 0)], True, None)
        for tl_ in (BQ, BK, BG, KTM, VTM, Vpad):
            tl_.free()
        for b_ in range(2, 3):
            E(P.pool, lambda e, b_=b_: e.collective_compute("AllGather", ALU.bypass, replica_groups=[[0, 1, 2, 3], [4, 5, 6, 7]],
                                                           ins=[xch_in_t[b_].ap().opt()], outs=[xch_out_t[b_].ap().opt()]),
              [xin_rs[b_]], [xout_rs[b_]], tl=cc_tls[l * 3 + b_])
        mix = [Tile() for _ in range(9)]
        def gla_post():
            Sin = []
            for d_ in range(2):
                S = Tile()
                fill(S, 0.0, 256)
                for hh in range(4):
                    pool_ld(R(S.ap()[32 * hh:32 * hh + 32, 64 * hh:64 * hh + 64]), R(sg_d[l, d_, 32 * hh:32 * hh + 32, :]), S.regs)
                SL = Tile(2)
                slv = SL.ap().rearrange("p (r c) -> p r c", r=4)
                r0 = 1152 + 64 * d_
                for r in range(4):
                    s_ap, s_rg = xout_v(r, r0, 64, 256)
                    pool_ld(R(slv[:, r, :]), R(s_ap), SL.regs, [s_rg])
                AD = Tile()
                adv = AD.ap()[:, 0:8].rearrange("p (r c) -> p r c", r=4)
                for r in range(4):
                    s_ap, s_rg = xout_v(r, 1280, 1, 2, nfl=256)
                    pool_ld(R(adv[:, r, :]), R(s_ap), AD.regs, [s_rg])
                V_ts(AD.ap()[:, 8:16], AD.ap()[:, 0:8], -1.0, None, ALU.add, None, AD.regs, AD.regs)
                am1 = AD.ap()[:, 8:16].rearrange("p (r c) -> p r c", r=4)
                order = range(4) if d_ == 0 else range(3, -1, -1)
                tmp = Tile()
                for r in order:
                    V_stt(tmp.ap()[:, 0:256], S.ap()[:, 0:256], am1[:, r, d_:d_ + 1], slv[:, r, :], ALU.mult, ALU.add,
                          S.regs + AD.regs + SL.regs, tmp.regs)
                    V_stt(S.ap()[:, 0:256], tmp.ap()[:, 0:256], rmask[:, d_ * 4 + r:d_ * 4 + r + 1], S.ap()[:, 0:256], ALU.mult, ALU.add,
                          tmp.regs + S.regs + [const_r], S.regs)
                V_tt(R(tmp.ap()[:, 0:256]), S.ap()[:, 0:256], bdm[:], ALU.mult, S.regs + [const_r], tmp.regs)
                Sin.append(tmp)
                S.free()
                SL.free()
                AD.free()
            qin = [keep["qinf"], keep["qinb"]]
            for c2 in range(2):
                ps = PS("acc")
                mm(ps, ps.ap(), [(Sin[d_].ap()[:, c2 * 128:(c2 + 1) * 128], qin[d_].ap()) for d_ in range(2)],
                   Sin[0].regs + Sin[1].regs + qin[0].regs + qin[1].regs)
                V_tt(osb[c2].ap(), osb[c2].ap(), ps.ap(), ALU.add, osb[c2].regs + ps.regs, osb[c2].regs)
            for tl_ in Sin + qin:
                tl_.free()
            gla_out(l, osb, [BR0, BR1], mix[3:5])
            for tl_ in osb + [BR0, BR1]:
                tl_.free()

        VAll = Tile(8)
        vav = VAll.ap()[:, 0:3840].rearrange("p (k n) -> p k n", k=20)
        fill(VAll, 1.0, q=P.dve)
        for kv in range(2):
            KAll = Tile(5)
            for half in range(2):
                s_ap, s_rg = xo_multi(kv * 64, 64)
                sp_ld(R(KAll.ap()[half * 64:(half + 1) * 64, 0:2048].rearrange("p (r c) -> p r c", r=4)),
                      R(s_ap), KAll.regs, [s_rg])
                sp_ld(R(KAll.ap()[half * 64:(half + 1) * 64, 2048:2560]), R(cak_d[l, kv, :, :]), KAll.regs)
            for r in range(4):
                s_ap, s_rg = xout_v(r, 640, 128, 128)
                src = s_ap.rearrange("(b p) c -> p b c", p=128)
                sp_ld(R(vav[:, r * 4:(r + 1) * 4, 64:128]), R(src[:, :, kv * 64:(kv + 1) * 64]), VAll.regs, [s_rg])
            sp_ld(R(vav[:, 16:20, 64:128]), R(cav_d[l, :, kv * 64:(kv + 1) * 64].rearrange("(b p) c -> p b c", p=128)), VAll.regs)
            jobs = []
            for g in range(3):
                hh = kv * 3 + g
                ch, par = hh // 2, hh % 2
                base = par * 64
                keys = [(KAll.ap()[base:base + 64, kc * 128:(kc + 1) * 128], KAll.regs) for kc in range(20)]

                def finish(acc, ch=ch, par=par):
                    nb_, db_ = (0, 64) if par == 0 else (64, 0)
                    rd = Tile()
                    V_rcp(rd.ap()[db_:db_ + 64, :], acc.ap()[db_:db_ + 64, :], acc.regs, rd.regs)
                    V_tt(R(mix[ch].ap()[nb_:nb_ + 64, :]), acc.ap()[nb_:nb_ + 64, :], rd.ap()[db_:db_ + 64, :], ALU.mult,
                         acc.regs + rd.regs, mix[ch].regs)
                    rd.free()

                off = 64 if par == 0 else 0
                jobs.append(dict(qt=QA[ch], qbase=base, ncol=512, q0=0, keys=keys, scale=0.125,
                                 vwin=(lambda i, off=off: (vav[:, i, off:off + 128], VAll.regs)), finish=finish))
            attn_jobs(jobs)
            KAll.free()
        VAll.free()
        for tl_ in QA:
            tl_.free()
        gla_post()
        KCh = Tile(5)
        VCh = Tile(5)
        vcv = VCh.ap().rearrange("p (k n) -> p k n", k=20)

        def keyfC2(sid, hh, base):
            return [(KCh.ap()[base:base + 64, kc * 128:(kc + 1) * 128], KCh.regs) for kc in range(20)]

        def vwinC2(sid, hh, i):
            return (vcv[:, i, :], VCh.regs)

        fill(VCh, 1.0, q=P.dve)
        for hh in range(4):
            s_ap, s_rg = xo_multi(128 + hh * 128, 128)
            sp_ld(R(KCh.ap()[:, 0:2048].rearrange("p (r c) -> p r c", r=4)), R(s_ap), KCh.regs, [s_rg])
            sp_ld(R(KCh.ap()[:, 2048:2560]), R(cck_d[l, hh, :, :]), KCh.regs)
            for r in range(4):
                s_ap, s_rg = xout_v(r, 768, 384, 384)
                src = s_ap.rearrange("(b p) c -> p b c", p=128)
                sp_ld(R(vcv[:, r * 4:(r + 1) * 4, 0:96]), R(src[:, :, hh * 96:(hh + 1) * 96]), VCh.regs, [s_rg])
            sp_ld(R(vcv[:, 16:20, 0:96]), R(ccv_d[l, :, hh * 96:(hh + 1) * 96].rearrange("(b p) c -> p b c", p=128)), VCh.regs)
            mixer_C_heads(l, [QC[hh]], [hh], [(0, 512, 0)], keyfC2, vwinC2, [mix[5 + hh]])
        KCh.free()
        VCh.free()
        for tl_ in QC:
            tl_.free()
        w_out(l, t, j, mix)
        for tl_ in mix:
            tl_.free()

    step = [0]

    def go():
        step[0] += 1
        return step[0] <= upto

    for l in range(nl):
        if l == 0:
            ada_begin(0)
            ada_end(0)
        if go():
            ffn_multi(l, [0, 1], 0)
        if go():
            mixer(l, 0)
        if go():
            mixer(l, 1)
        if go():
            if l + 1 < nl:
                ada_begin(l + 1)
                ffn_multi(l, [0, 1], 1, hook=lambda l=l: ada_slot(l + 1))
                ada_end(l + 1)
            else:
                ffn_multi(l, [0, 1], 1)
        if go():
            ffn_multi(l, [2], 0)
        if go():
            mixer(l, 2)
        if go():
            ffn_multi(l, [2], 1)
    for t in range(3):
        pool_st(yT_o[:, :, t * 512:(t + 1) * 512], xT[:, :, t * 512:(t + 1) * 512], [x_r[c][t] for c in range(8)])

    for t in P.tls:
        t.sem = es.enter_context(nc.semaphore(t.name))
    final_waits = [(t, t.count) for t in P.dma_tls if t.count > 0]
    with nc.allow_low_precision("float32r PE operands"), nc.Block() as block:
        @block.sync
        def _(e):
            P.replay(P.sp, e)

        @block.tensor
        def _(e):
            P.replay(P.pe, e)

        @block.scalar
        def _(e):
            P.replay(P.act, e)

        @block.vector
        def _(e):
            P.replay(P.dve, e)

        @block.gpsimd
        def _(e):
            P.replay(P.pool, e)
            for t, v in final_waits:
                e.wait_ge(t.sem, v)
    es.close()
    return nc


OFF = dict(a_q=0, a_k=384, a_v=512, b_q=640, b_k=768, b_v=896, b_g=1152, b_r=1184, c_q=1440, c_k=1824, c_v=2208)


def _w_in_cols():
    def pad(lst, n=256):
        return list(lst) + [-1] * (n - len(lst))

    def rng(a, n):
        return list(range(a, a + n))

    def cmap(base, hh):
        out = []
        for m in range(2):
            out += rng(base + hh * 96 + m * 48, 48) + [-1] * 16
        return out

    slots = []
    slots.append(rng(OFF["a_q"], 256))
    slots.append(rng(OFF["a_q"] + 256, 128) + rng(OFF["a_k"], 64) * 2)
    slots.append(pad(rng(OFF["a_k"] + 64, 64) * 2))
    slots.append(pad(rng(OFF["a_v"], 128)))
    for base in (OFF["c_q"], OFF["c_k"]):
        slots.append(cmap(base, 0) + cmap(base, 1))
        slots.append(cmap(base, 2) + cmap(base, 3))
    slots.append(rng(OFF["c_v"], 256))
    slots.append(pad(rng(OFF["c_v"] + 256, 128)))
    slots.append(rng(OFF["b_q"], 128) + rng(OFF["b_k"], 128))
    slots.append(rng(OFF["b_r"], 256))
    slots.append(pad(rng(OFF["b_g"], 32)))
    slots.append(rng(OFF["b_k"], 128) + rng(OFF["b_v"], 128))
    slots.append(pad(rng(OFF["b_v"] + 128, 128)))
    assert len(slots) == 15
    return np.array(slots, dtype=np.int64)


def pack_weights(inp, nl):
    NSL = 135
    wst = np.zeros((nl * NSL, 128, SLOTF), np.float32)
    cols = _w_in_cols()
    for l in range(nl):
        b = l * NSL
        for s in range(2):
            wg = inp["w_ffn_gate"][l, s].reshape(8, 128, 22, 128).transpose(2, 1, 0, 3)
            wu = inp["w_ffn_up"][l, s].reshape(8, 128, 22, 128).transpose(2, 1, 0, 3)
            gu = np.stack([wg, wu], axis=2).reshape(22, 128, SLOTF)
            wd = inp["w_ffn_down"][l, s].reshape(2, 11, 128, 8, 128).transpose(0, 3, 2, 1, 4).reshape(2, 8, 128, 1408)
            for half in range(2):
                o = b + s * 38 + half * 19
                wst[o:o + 11] = gu[half * 11:(half + 1) * 11]
                wst[o + 11:o + 19, :, 0:1408] = wd[half]
        wi = np.concatenate([inp["w_in"][l], np.zeros((D, 1), np.float32)], axis=1)
        for si in range(15):
            wc = wi[:, cols[si]]
            wst[b + 76 + si] = wc.reshape(8, 128, 256).transpose(1, 0, 2).reshape(128, SLOTF)
        wo = inp["w_out"][l]
        wpad = np.zeros((9, 128, D), np.float32)
        for c in range(5):
            wpad[c] = wo[c * 128:(c + 1) * 128]
        for c in range(4):
            wpad[5 + c, 0:96] = wo[640 + c * 96:640 + (c + 1) * 96]
        wst[b + 91:b + 99, :, 0:1152] = wpad.reshape(9, 128, 8, 128).transpose(2, 1, 0, 3).reshape(8, 128, 1152)
        wa = inp["w_ada"][l].reshape(8, 128, 36, 256).transpose(2, 1, 0, 3).reshape(36, 128, SLOTF)
        wst[b + 99:b + 135] = wa
    return wst


def make_consts():
    cm = np.zeros((128, NCM, 128), np.float32)
    p = np.arange(128)
    cm[:, C_ONESM, :] = 1.0 / D
    cm[:, C_BLK64, :] = (p[:, None] // 64 == p[None, :] // 64) / 64.0
    real48 = (p % 64) < 48
    cm[:, C_BLK48, :] = ((p[:, None] // 64 == p[None, :] // 64) & real48[:, None] & real48[None, :]) / 48.0
    cm[0:96, C_BLK96, 0:96] = 1.0 / 96.0
    cm[96:128, C_SEL32, 0:96] = 1.0 / 32.0
    permA = np.zeros(128, np.int64)
    for m in range(128):
        d = m % 64
        permA[m] = m + 16 if (d % 32) < 16 else m - 16
    cm[permA, C_PSWA, p] = 1.0
    permC = np.arange(128)
    for m in range(128):
        d = m % 64
        if d < 48:
            permC[m] = m + 12 if (d % 24) < 12 else m - 12
    for m in range(128):
        if (m % 64) < 48:
            cm[permC[m], C_PSWC, m] = 1.0
    cm[:, C_TRIU, :] = (p[:, None] <= p[None, :])
    cm[:, C_TRIL, :] = (p[:, None] >= p[None, :])
    cm[:, C_SL16, :] = (p[:, None] > p[None, :]).astype(np.float32) / -16.0
    cm[:, C_SU16, :] = (p[:, None] < p[None, :]).astype(np.float32) / -16.0
    ur = np.zeros((128, 2, 512), np.float32)
    ur[:, 0, :] = -1.0 / 16.0
    ur[:, 0, 0:128] = (p[:, None] <= p[None, :]).astype(np.float32) / -16.0
    ur[:, 1, :] = -1.0 / 16.0
    ur[:, 1, 384:512] = (p[:, None] >= p[None, :]).astype(np.float32) / -16.0
    hm = np.zeros((128, 4), np.float32)
    for hh in range(4):
        hm[32 * hh:32 * hh + 32, hh] = 32 ** -0.5
    bd = np.zeros((128, 256), np.float32)
    for hh in range(4):
        bd[32 * hh:32 * hh + 32, 64 * hh:64 * hh + 64] = 1.0
    return cm, ur, hm, bd


def rope_tables(tok0):
    t = np.arange(tok0, tok0 + 512)
    row = (t // 64).astype(np.float32)
    col = (t % 64).astype(np.float32)
    out = np.zeros((128, 4, 512), np.float32)
    out[:, 0, :] = 1.0
    out[:, 2, :] = 1.0
    for p in range(128):
        d = p % 64
        half, dd = d // 32, d % 32
        i = dd % 16
        f = np.float32(10000.0) ** (-np.float32(i) / np.float32(16))
        ang = (row if half == 0 else col) * np.float32(f)
        out[p, 0] = np.cos(ang)
        out[p, 1] = (-np.sin(ang)) if dd < 16 else np.sin(ang)
        if d < 48:
            half, dd = d // 24, d % 24
            i = dd % 12
            f = np.float32(10000.0) ** (-np.float32(i) / np.float32(12))
            ang = (row if half == 0 else col) * np.float32(f)
            out[p, 2] = np.cos(ang)
            out[p, 3] = (-np.sin(ang)) if dd < 12 else np.sin(ang)
        else:
            out[p, 2] = 1.0
            out[p, 3] = 0.0
    return out


def pack_pvec(inp, nl):
    pv = np.zeros((128, nl, NV), np.float32)
    p = np.arange(128)
    for l in range(nl):
        pv[:, l, 0:24] = inp["g_norm"][l].reshape(3, 8, 128).transpose(2, 0, 1).reshape(128, 24)
        pv[:, l, 24:96] = inp["b_ada"][l].reshape(72, 128).T
        pv[:, l, 96] = inp["g_a_q"][l][p % 64]
        pv[:, l, 97] = inp["g_a_k"][l][p % 64]
        for col, key in ((98, "g_c_q"), (99, "g_c_k")):
            g = np.zeros((2, 64), np.float32)
            g[:, 0:48] = inp[key][l]
            pv[:, l, col] = g.reshape(128)
        pv[:, l, 100] = inp["g_gla"][l][p % 64]
        pv[0:96, l, 101] = inp["g_c_out"][l]
    return pv


_NC_CACHE = {}


def kernel(**inp):
    return run(inp, L_FULL)


def run(inp, nl, trace=False):
    inp = {k: np.asarray(v) for k, v in inp.items()}
    if nl not in _NC_CACHE:
        _NC_CACHE[nl] = build(nl)
    nc = _NC_CACHE[nl]
    wst = pack_weights(inp, nl)
    cm, ur, hm, bd = make_consts()
    pv = pack_pvec(inp, nl)
    bgla = np.broadcast_to(inp["b_gla"][:nl].reshape(1, nl, 256), (128, nl, 256)).copy()
    w2 = np.zeros((32, nl, 256), np.float32)
    for l in range(nl):
        w2[0:16, l, 0:128] = inp["w_gla_up"][l, 0]
        w2[16:32, l, 128:256] = inp["w_gla_up"][l, 1]
    lamc = np.broadcast_to(inp["lam_c"][:nl].reshape(1, nl * 4 * 48), (128, nl * 4 * 48)).copy()
    ozc = np.zeros((128, 2, 512), np.float32)
    ozc[:, 1, :] = 1.0
    in_maps = []
    for c in range(8):
        b, r = c // 4, c % 4
        xp = inp["x_prompt"][4 * c:4 * c + 4].reshape(1024, D)
        xs = inp["x_sample"][b, r * 512:(r + 1) * 512]
        xt = np.concatenate([xp, xs], axis=0)
        xin = xt.reshape(NTOK, 8, 128).transpose(2, 1, 0).copy()
        cond = np.stack([inp["c_ctx"], inp["c"][b]], axis=1).reshape(8, 128, 2).transpose(1, 0, 2).copy()
        rm = np.zeros((128, 8), np.float32)
        for rr in range(4):
            rm[:, rr] = 1.0 if rr < r else 0.0
            rm[:, 4 + rr] = 1.0 if rr > r else 0.0
        cak = inp["cache_a_k"][b, :nl].transpose(0, 2, 3, 1).copy()
        cav = inp["cache_a_v"][b, :nl].reshape(nl, 512, 128).copy()
        ck = inp["cache_c_k"][b, :nl].reshape(nl, 512, 4, 2, 48)
        cck = np.zeros((nl, 4, 2, 64, 512), np.float32)
        cck[:, :, :, 0:48, :] = ck.transpose(0, 2, 3, 4, 1)
        cck = cck.reshape(nl, 4, 128, 512)
        ccv = inp["cache_c_v"][b, :nl].reshape(nl, 512, 384).copy()
        sg = inp["state_gla"][b, :nl].reshape(nl, 2, 128, 64).copy()
        in_maps.append(dict(wst=wst, xin=xin, cmat=cm, urow=ur, rope=rope_tables(r * 512), hmask=hm, bdmask=bd, pvec=pv, oz=ozc,
                            bgla=bgla, w2=w2, lamc=lamc, condT=cond, rmask=rm, cakT=cak, cav=cav, cckT=cck, ccv=ccv, sgla=sg))
    if trace:
        res = run_bass_kernel_spmd(nc, in_maps, core_ids=list(range(8)), trace=True)
        print("exec_time_ns", res.exec_time_ns)
    else:
        res = run_bass_kernel_spmd(nc, in_maps, core_ids=list(range(8)))
    return assemble(res.results, nl)


def assemble(results, nl):
    y_prompt = np.zeros((32, 256, D), np.float32)
    y_sample = np.zeros((2, 2048, D), np.float32)
    n_ak = np.zeros((32, nl, 256, 2, 64), np.float32)
    n_av = np.zeros((32, nl, 256, 2, 64), np.float32)
    n_ck = np.zeros((32, nl, 256, 4, 96), np.float32)
    n_cv = np.zeros((32, nl, 256, 4, 96), np.float32)
    n_st = np.zeros((32, nl, 2, 4, 32, 64), np.float32)
    for c in range(8):
        r = results[c]
        b, rk = c // 4, c % 4
        y = np.asarray(r["yT"]).transpose(2, 1, 0).reshape(NTOK, D)
        y_prompt[4 * c:4 * c + 4] = y[0:1024].reshape(4, 256, D)
        y_sample[b, rk * 512:(rk + 1) * 512] = y[1024:]
        ak = np.asarray(r["akT"])
        n_ak[4 * c:4 * c + 4] = ak.reshape(nl, 2, 64, 4, 256).transpose(3, 0, 4, 1, 2)
        av = np.asarray(r["av"])
        n_av[4 * c:4 * c + 4] = av.reshape(nl, 4, 256, 2, 64).transpose(1, 0, 2, 3, 4)
        ck = np.asarray(r["ckT"]).reshape(nl, 4, 2, 64, 4, 256)[:, :, :, 0:48]
        n_ck[4 * c:4 * c + 4] = ck.transpose(4, 0, 5, 1, 2, 3).reshape(4, nl, 256, 4, 96)
        cv = np.asarray(r["cv"])
        n_cv[4 * c:4 * c + 4] = cv.reshape(nl, 4, 256, 4, 96).transpose(1, 0, 2, 3, 4)
        st = np.asarray(r["st"])
        n_st[4 * c:4 * c + 4] = st.reshape(nl, 4, 2, 4, 32, 64).transpose(1, 0, 2, 3, 4, 5)
    return (y_prompt, y_sample, n_ak, n_av, n_ck, n_cv, n_st)
```

```python
import math
from contextlib import ExitStack

import numpy as np
import concourse.bass as bass
import concourse.mybir as mybir
from concourse.bass_utils import run_bass_kernel_spmd

F32 = mybir.dt.float32
F32R = mybir.dt.float32r
AF = mybir.ActivationFunctionType
ALU = mybir.AluOpType
AX = mybir.AxisListType

D = 1024
L_FULL = 4
DFF = 2816
NTOK = 1536
EPS = 1e-6
NSLOT = 4
SLOTF = 2048
NA = 44
XR = 1281
NV = 104

(C_ONESM, C_BLK64, C_BLK48, C_BLK96, C_SEL32, C_PSWA, C_PSWC, C_TRIU, C_TRIL, C_SL16, C_SU16) = range(11)
NCM = 11


def R(ap):
    return ap if ap.dtype == F32R else ap.bitcast(F32R)


def RO(ap):
    return R(ap) if ap.name == "arena" else ap


class TL:
    def __init__(self, name, step):
        self.name, self.step, self.count, self.sem = name, step, 0, None


class Reg:
    __slots__ = ("name", "w", "r", "excl")

    def __init__(self, name, excl=False):
        self.name, self.w, self.r, self.excl = name, None, {}, excl


class Q:
    def __init__(self, name, no_self=False):
        self.name = name
        self.tl = TL(name, 1)
        self.ops = []
        self.seen = {}
        self.no_self = no_self


class Prog:
    def __init__(self):
        self.pe = Q("pe", no_self=True)
        self.act = Q("act")
        self.dve = Q("dve")
        self.pool = Q("pool")
        self.sp = Q("sp")
        self.tls = [self.pe.tl, self.act.tl, self.dve.tl, self.pool.tl]
        self.dma_tls = []

    def new_dma_tl(self, name):
        t = TL(name, 16)
        self.tls.append(t)
        self.dma_tls.append(t)
        return t

    def emit(self, q, fn, reads=(), writes=(), tl=None):
        dma = tl is not None
        tl = tl or q.tl
        need = {}

        def req(t, v):
            if v > need.get(t, 0):
                need[t] = v

        for r in reads:
            if r.w:
                req(*r.w)
            if r.excl:
                for t, v in r.r.items():
                    if t is not tl:
                        req(t, v)
        for w in writes:
            if w.w:
                req(*w.w)
            for t, v in w.r.items():
                req(t, v)
        if dma and tl.count > 0:
            req(tl, tl.count)
        waits = []
        for t, v in need.items():
            if t is q.tl and q.no_self and not dma:
                continue
            if q.seen.get(t, 0) < v:
                waits.append((t, v))
                q.seen[t] = v
        tl.count += tl.step
        my = tl.count
        q.ops.append((waits, fn, tl))
        for w in writes:
            w.w = (tl, my)
            w.r = {}
        for r in reads:
            r.r[tl] = my

    def replay(self, q, eng):
        for waits, fn, tl in q.ops:
            for t, v in waits:
                eng.wait_ge(t.sem, v)
            ins = fn(eng)
            ins.then_inc(tl.sem, tl.step)


class Rot:
    def __init__(self, items):
        self.items, self.i = list(items), 0

    def next(self):
        it = self.items[self.i % len(self.items)]
        self.i += 1
        return it


class Arena:
    def __init__(self, n):
        self.n = n
        self.free = [True] * n

    def alloc(self, k=1):
        for s in range(self.n - k + 1):
            if all(self.free[s:s + k]):
                for i in range(s, s + k):
                    self.free[i] = False
                return s
        raise RuntimeError(f"arena exhausted (need {k}, free {sum(self.free)})")

    def release(self, s, k=1):
        for i in range(s, s + k):
            assert not self.free[i]
            self.free[i] = True


def build(nl=L_FULL, taps=(), upto=10 ** 9):
    nc = bass.Bass("TRN2", target_bir_lowering=False)
    nc.dge_precook = False
    P = Prog()
    es = ExitStack()

    def din(name, shape):
        return nc.dram_tensor(name, list(shape), F32, kind="ExternalInput").ap()

    def dout(name, shape):
        return nc.dram_tensor(name, list(shape), F32, kind="ExternalOutput").ap()

    NSL = 135
    wst = din("wst", [nl * NSL, 128, SLOTF])
    xin = din("xin", [128, 8, NTOK])
    cmat_d = din("cmat", [128, NCM, 128])
    urow_d = din("urow", [128, 2, 512])
    rope_d = din("rope", [128, 4, 512])
    hmask_d = din("hmask", [128, 4])
    bd_d = din("bdmask", [128, 256])
    pvec_d = din("pvec", [128, nl, NV])
    bgla_d = din("bgla", [128, nl, 256])
    w2_d = din("w2", [32, nl, 256])
    lamc_d = din("lamc", [128, nl * 4 * 48])
    cond_d = din("condT", [128, 8, 2])
    rmask_d = din("rmask", [128, 8])
    oz_d = din("oz", [128, 2, 512])
    cak_d = din("cakT", [nl, 2, 64, 512])
    cav_d = din("cav", [nl, 512, 128])
    cck_d = din("cckT", [nl, 4, 128, 512])
    ccv_d = din("ccv", [nl, 512, 384])
    sg_d = din("sgla", [nl, 2, 128, 64])

    yT_o = dout("yT", [128, 8, NTOK])
    ak_o = dout("akT", [nl, 2, 64, 1024])
    av_o = dout("av", [nl, 1024, 128])
    ck_o = dout("ckT", [nl, 4, 128, 1024])
    cv_o = dout("cv", [nl, 1024, 384])
    st_o = dout("st", [nl, 4, 2, 128, 64])
    tap_o = {name: dout("tap_" + name, shape) for name, shape in taps}

    XB = [512, 512, 264]
    xch_in_t = [nc.dram_tensor(f"xch_in{b}", [XB[b], 512], F32) for b in range(3)]
    xch_out_t = [nc.dram_tensor(f"xch_out{b}", [4 * XB[b], 512], F32) for b in range(3)]

    def xloc(row):
        if row < 512:
            return 0, row
        if row < 640:
            return 1, row - 512
        if row < 768:
            return 2, row - 640
        if row < 1152:
            return 1, row - 768 + 128
        return 2, row - 1152 + 128

    xin_flat = [t_.ap().rearrange("r c -> (r c)") for t_ in xch_in_t]
    xout_flat = [t_.ap().rearrange("r c -> (r c)") for t_ in xch_out_t]

    def xin_rows(row0, nrows):
        b, lr = xloc(row0)
        return xch_in_t[b].ap()[lr:lr + nrows, :], xin_rs[b]

    def xin_v(row0, nrows, c, nfl=None):
        b, lr = xloc(row0)
        n = nrows * 512 if nfl is None else nfl
        return xin_flat[b][lr * 512:lr * 512 + n].rearrange("(t c) -> t c", c=c), xin_rs[b]

    def xo_multi(row0, nrows):
        b, lr = xloc(row0)
        v = xch_out_t[b].ap().rearrange("(r x) c -> r x c", r=4)
        return v[:, lr:lr + nrows, :].rearrange("r p c -> p r c"), xout_rs[b]

    def xout_v(r, row0, nrows, c, nfl=None):
        b, lr = xloc(row0)
        o = (r * XB[b] + lr) * 512
        n = nrows * 512 if nfl is None else nfl
        return xout_flat[b][o:o + n].rearrange("(t c) -> t c", c=c), xout_rs[b]

    def sb(name, shape):
        return es.enter_context(nc.sbuf_tensor(name, list(shape), F32))

    xT = sb("xT", [128, 8, NTOK])
    ring = sb("ring", [128, NSLOT, SLOTF])
    arena = sb("arena", [128, NA, 512])
    cmat = sb("cmat_s", [128, NCM, 128])
    urow = sb("urow_s", [128, 2, 512])
    rope = sb("rope_s", [128, 4, 512])
    hmask = sb("hmask_s", [128, 4])
    bdm = sb("bd_s", [128, 256])
    pvec = sb("pvec_s", [128, nl, NV])
    bgla = sb("bgla_s", [128, nl, 256])
    w2 = sb("w2_s", [32, nl, 256])
    lamc = sb("lamc_s", [128, nl * 4 * 48])
    cond = sb("cond_s", [128, 8, 2])
    scT = sb("scT_s", [128, 8, 2])
    rmask = sb("rmask_s", [128, 8])
    oz = sb("oz_s", [128, 2, 512])
    modT2 = sb("modT_s", [128, 2, 72, 2])
    gsT2 = sb("gs_s", [128, 2, 3, 8, 2])
    gtT2 = sb("gt_s", [128, 2, 3, 8, 2])
    lam = sb("lam_s", [128, 4, 4])
    lamt = sb("lamt_s", [128, nl * 2 * 48])
    gco = sb("gco_s", [128, 4])
    small = sb("small_s", [128, 16])

    psb = [es.enter_context(nc.psum_tensor(f"ps{i}", [128, 512], F32)) for i in range(8)]

    x_r = [[Reg(f"x{c}_{t}") for t in range(3)] for c in range(8)]
    ring_r = [Reg(f"ring{s}") for s in range(NSLOT)]
    ar_r = [Reg(f"ar{i}") for i in range(NA)]
    ps_r = [Reg(f"ps{i}", excl=True) for i in range(8)]
    const_r = Reg("consts")
    mod_rs = [Reg("mod0"), Reg("mod1")]
    misc_r = Reg("misc")
    small_r = Reg("small")
    xin_rs = [Reg(f"xch_in{b}") for b in range(3)]
    xout_rs = [Reg(f"xch_out{b}") for b in range(3)]

    ar = Arena(NA)

    class Tile:
        def __init__(self, k=1):
            self.k = k
            self.s = ar.alloc(k)
            self.regs = ar_r[self.s:self.s + k]

        def ap(self):
            return arena[:, self.s:self.s + self.k, :].rearrange("p k f -> p (k f)") if self.k > 1 else arena[:, self.s, :]

        def free(self):
            ar.release(self.s, self.k)

    ps_tmp = Rot(range(0, 5))
    ps_acc = Rot(range(5, 8))

    class PS:
        def __init__(self, kind="tmp"):
            self.i = (ps_tmp if kind == "tmp" else ps_acc).next()
            self.regs = [ps_r[self.i]]

        def ap(self):
            return psb[self.i][:]

    slot_tl = [P.new_dma_tl(f"slot{s}") for s in range(NSLOT)]
    misc_tl = Rot([P.new_dma_tl(f"md{i}") for i in range(8)])
    out_tl = Rot([P.new_dma_tl(f"od{i}") for i in range(4)])
    cc_tls = []
    for i in range(nl * 3):
        t_ = TL(f"cc{i}", 1)
        P.tls.append(t_)
        cc_tls.append(t_)

    E = P.emit

    def dma(q, out, in_, reads, writes, tl):
        E(q, lambda e, out=out, in_=in_: e.dma_start(out=out, in_=in_), reads, writes, tl=tl)

    def pool_ld(out, in_, writes, reads=()):
        dma(P.pool, out, in_, list(reads), list(writes), misc_tl.next())

    def sp_ld(out, in_, writes, reads=()):
        dma(P.sp, out, in_, list(reads), list(writes), misc_tl.next())

    def pool_st(out, in_, reads, writes=()):
        dma(P.pool, out, in_, list(reads), list(writes), out_tl.next())

    ring_n = [0]

    def ring_load(idx, nfl=SLOTF):
        s = ring_n[0] % NSLOT
        ring_n[0] += 1
        dma(P.sp, R(ring[:, s, 0:nfl]), R(wst[idx, :, 0:nfl]), [], [ring_r[s]], slot_tl[s])
        return s

    def mm(ps, out_ap, pairs, reads, start=True, stop=True):
        n = len(pairs)

        def fn(e, pairs=pairs, out_ap=out_ap, start=start, stop=stop):
            ins = None
            for i, (lt, rh) in enumerate(pairs):
                ins = e.matmul(out_ap, R(lt), R(rh), start=(start and i == 0), stop=(stop and i == n - 1))
            return ins

        E(P.pe, fn, list(reads), ps.regs)

    def A(out, in_, func, reads, writes, bias=0.0, scale=1.0):
        out = RO(out)
        E(P.act, lambda e: e.activation(out, in_, func, bias=bias, scale=scale), list(reads), list(writes))

    def V_tt(out, in0, in1, op, reads, writes, q=None):
        out = RO(out)
        E(q or P.dve, lambda e: e.tensor_tensor(out, in0, in1, op), list(reads), list(writes))

    def V_ts(out, in0, s1, s2, op0, op1, reads, writes, q=None):
        out = RO(out)
        if op1 is None:
            E(q or P.dve, lambda e: e.tensor_scalar(out, in0, s1, None, op0), list(reads), list(writes))
        else:
            E(q or P.dve, lambda e: e.tensor_scalar(out, in0, s1, s2, op0, op1), list(reads), list(writes))

    def V_stt(out, in0, sc, in1, op0, op1, reads, writes, q=None):
        out = RO(out)
        E(q or P.dve, lambda e: e.scalar_tensor_tensor(out, in0, sc, in1, op0, op1), list(reads), list(writes))

    def V_rcp(out, in_, reads, writes):
        out = RO(out)
        E(P.dve, lambda e: e.reciprocal(out, in_), list(reads), list(writes))

    def V_cp(out, in_, reads, writes, q=None):
        out = RO(out)
        E(q or P.dve, lambda e: e.tensor_copy(out, in_), list(reads), list(writes))

    def fill(tl_, val, nfl=None, q=None):
        n = tl_.k * 512 if nfl is None else nfl
        a = tl_.ap()
        for o in range(0, n, 512):
            w = min(512, n - o)
            V_cp(a[:, o:o + w], oz[:, 1 if val == 1.0 else 0, 0:w], [const_r], [tl_.regs[o // 512]], q=(q or P.pool))

    def CM(i, rows=128, cols=128):
        return cmat[0:rows, i, 0:cols]

    def rsqrt_from(ps_ap, out_ap, reads, writes, tmp_ap, tmp_regs):
        A(tmp_ap, ps_ap, AF.Ln, reads, tmp_regs, bias=EPS)
        A(out_ap, tmp_ap, AF.Exp, tmp_regs, writes, scale=-0.5)

    def tap(name, ap, reads):
        if name in tap_o:
            pool_st(tap_o[name], ap, reads)

    pool_ld(R(cmat[:]), R(cmat_d), [const_r])
    pool_ld(R(urow[:]), R(urow_d), [const_r])
    pool_ld(rope[:], rope_d, [const_r])
    pool_ld(hmask[:], hmask_d, [const_r])
    pool_ld(bdm[:], bd_d, [const_r])
    pool_ld(pvec[:], pvec_d, [const_r])
    pool_ld(bgla[:], bgla_d, [const_r])
    pool_ld(R(w2[:]), R(w2_d), [const_r])
    pool_ld(lamc[:], lamc_d, [const_r])
    pool_ld(cond[:], cond_d, [const_r])
    pool_ld(rmask[:], rmask_d, [const_r])
    pool_ld(oz[:], oz_d, [const_r])
    for t in range(3):
        pool_ld(xT[:, :, t * 512:(t + 1) * 512], xin[:, :, t * 512:(t + 1) * 512], [x_r[c][t] for c in range(8)])

    lc = lamc[:].rearrange("p (l a b d) -> p l a b d", l=nl, a=2, b=2, d=48)
    lt_v = lamt[:].rearrange("p (l a d) -> p l a d", l=nl, a=2, d=48)
    V_tt(lt_v, lc[:, :, :, 0, :], lc[:, :, :, 1, :], ALU.mult, [const_r], [misc_r])
    for l in range(nl):
        E(P.dve, lambda e, l=l: e.tensor_reduce(lam[:, l, 2:4], lt_v[:, l, :, :], AX.X, ALU.add), [misc_r], [misc_r])
        A(lam[:, l, 2:4], lam[:, l, 2:4], AF.Exp, [misc_r], [misc_r])
        V_tt(lam[:, l, 0:1], lam[:, l, 2:3], lam[:, l, 3:4], ALU.subtract, [misc_r], [misc_r])
        li = 0.8 - 0.6 * math.exp(-0.3 * l)
        V_ts(lam[:, l, 0:1], lam[:, l, 0:1], float(li), None, ALU.add, None, [misc_r], [misc_r])
        V_ts(lam[:, l, 1:2], lam[:, l, 0:1], -1.0, None, ALU.mult, None, [misc_r], [misc_r])
        V_ts(gco[:, l:l + 1], pvec[:, l, 101:102], float(1.0 - li), None, ALU.mult, None, [const_r, misc_r], [misc_r])
    A(R(scT[:]), cond[:], AF.Silu, [const_r], [misc_r])

    ada_state = {}

    def ada_begin(l):
        ps = PS("acc")
        ada_state[l] = dict(ps=ps, pv=ps.ap()[:, 0:144].rearrange("p (c j) -> p c j", j=2), k=0)

    def ada_slot(l):
        st_ = ada_state[l]
        sl = st_["k"]
        if sl >= 36:
            return
        st_["k"] += 1
        ps, pv = st_["ps"], st_["pv"]
        s = ring_load(l * NSL + 99 + sl)
        sv = ring[:, s, :].rearrange("p (k n) -> p k n", k=8)
        for half in range(2):
            ch = sl * 2 + half
            mm(ps, pv[:, ch, :], [(sv[:, kc, half * 128:(half + 1) * 128], scT[:, kc, :]) for kc in range(8)],
               [ring_r[s], misc_r])

    def ada_end(l):
        st_ = ada_state[l]
        while st_["k"] < 36:
            ada_slot(l)
        ps, pv = st_["ps"], st_["pv"]
        par = l % 2
        modT, gsT, gtT, mod_r = modT2[:, par], gsT2[:, par], gtT2[:, par], mod_rs[par]
        for j in range(2):
            V_tt(modT[:, :, j], pv[:, :, j], pvec[:, l, 24:96], ALU.add, ps.regs + [const_r], [mod_r])
        for s3 in range(3):
            for j in range(2):
                V_stt(gsT[:, s3, :, j], modT[:, (3 * s3 + 1) * 8:(3 * s3 + 2) * 8, j], 1.0,
                      pvec[:, l, s3 * 8:(s3 + 1) * 8], ALU.add, ALU.mult, [mod_r, const_r], [mod_r])
                V_ts(gtT[:, s3, :, j], modT[:, (3 * s3 + 2) * 8:(3 * s3 + 3) * 8, j],
                     (1.0 if s3 == 1 else 0.5), None, ALU.mult, None, [mod_r], [mod_r])

    def norm_mod(l, t, s3, j, h):
        par = l % 2
        modT, gsT, mod_r = modT2[:, par], gsT2[:, par], mod_rs[par]
        ps = PS()
        sqs = [Tile(), Tile(), Tile()]
        for c in range(8):
            sq = sqs[c % 3]
            xa = xT[:, c, t * 512:(t + 1) * 512]
            V_tt(R(sq.ap()), xa, xa, ALU.mult, [x_r[c][t]], sq.regs, q=P.pool)
            mm(ps, ps.ap(), [(CM(C_ONESM), sq.ap())], sq.regs + [const_r], start=(c == 0), stop=(c == 7))
        tmp = Tile()
        rstd = Tile()
        rsqrt_from(ps.ap(), rstd.ap(), ps.regs, rstd.regs, tmp.ap(), tmp.regs)
        hv = h.ap().rearrange("p (c f) -> p c f", c=8)
        for c in range(8):
            t2 = sqs[c % 3]
            V_stt(t2.ap(), xT[:, c, t * 512:(t + 1) * 512], gsT[:, s3, c, j:j + 1], rstd.ap(), ALU.mult, ALU.mult,
                  [x_r[c][t], mod_r] + rstd.regs, t2.regs)
            A(R(hv[:, c, :]), t2.ap(), AF.Identity, t2.regs + [mod_r], [h.regs[c]], bias=modT[:, 3 * s3 * 8 + c, j:j + 1])
        for tl_ in sqs:
            tl_.free()
        tmp.free()
        rstd.free()

    def ffn_multi(l, tiles, s, hook=None):
        s3 = 0 if s == 0 else 2
        gtT, mod_r = gtT2[:, l % 2], mod_rs[l % 2]
        nt = len(tiles)
        js = [1 if t == 2 else 0 for t in tiles]
        hs = [Tile(8) for _ in tiles]
        for ti, t in enumerate(tiles):
            norm_mod(l, t, s3, js[ti], hs[ti])
        hvs = [h.ap().rearrange("p (c f) -> p c f", c=8) for h in hs]
        acts = [Tile(11) for _ in tiles]
        avs = [a_.ap().rearrange("p (c f) -> p c f", c=11) for a_ in acts]
        base = l * NSL + s * 38
        for half in range(2):
            for fcl in range(11):
                sl = ring_load(base + half * 19 + fcl)
                sv = ring[:, sl, :].rearrange("p (g k n) -> p g k n", g=2, k=8)
                for ti in range(nt):
                    psg, psu = PS(), PS()
                    mm(psg, psg.ap(), [(sv[:, 0, kc, :], hvs[ti][:, kc, :]) for kc in range(8)], [ring_r[sl]] + hs[ti].regs)
                    mm(psu, psu.ap(), [(sv[:, 1, kc, :], hvs[ti][:, kc, :]) for kc in range(8)], [ring_r[sl]] + hs[ti].regs)
                    sg = Tile()
                    A(sg.ap(), psg.ap(), AF.Silu, psg.regs, sg.regs)
                    V_tt(R(avs[ti][:, fcl, :]), sg.ap(), psu.ap(), ALU.mult, sg.regs + psu.regs, [acts[ti].regs[fcl]])
                    sg.free()
                if hook is not None:
                    hook()
            for m in range(8):
                sl = ring_load(base + half * 19 + 11 + m, 1408)
                sv = ring[:, sl, 0:1408].rearrange("p (f n) -> p f n", f=11)
                for ti, t in enumerate(tiles):
                    psy = PS()
                    mm(psy, psy.ap(), [(sv[:, f, :], avs[ti][:, f, :]) for f in range(11)], [ring_r[sl]] + acts[ti].regs)
                    xa = xT[:, m, t * 512:(t + 1) * 512]
                    V_stt(xa, psy.ap(), gtT[:, s3, m, js[ti]:js[ti] + 1], xa, ALU.mult, ALU.add,
                          psy.regs + [mod_r, x_r[m][t]], [x_r[m][t]])
                if hook is not None:
                    hook()
        for tl_ in acts + hs:
            tl_.free()

    def proj_fm(sl_idx, hv, h, nchunks):
        sl = ring_load(sl_idx)
        sv = ring[:, sl, :].rearrange("p (k n) -> p k n", k=8)
        out = []
        for i in range(nchunks):
            ps = PS()
            mm(ps, ps.ap(), [(sv[:, kc, i * 128:(i + 1) * 128], hv[:, kc, :]) for kc in range(8)], [ring_r[sl]] + h.regs)
            out.append(ps)
        return out

    def proj_tm(sl_idx, hv, h, ncols):
        sl = ring_load(sl_idx)
        sv = ring[:, sl, :].rearrange("p (k n) -> p k n", k=8)
        per = 512 // ncols
        res = []
        for g in range(0, 4, per):
            ps = PS()
            pv = ps.ap()[:, 0:per * ncols].rearrange("p (b n) -> p b n", b=per)
            for bi in range(per):
                tb = g + bi
                mm(ps, pv[:, bi, :], [(hv[:, kc, tb * 128:(tb + 1) * 128], sv[:, kc, 0:ncols]) for kc in range(8)],
                   [ring_r[sl]] + h.regs)
            res.append((ps, pv, list(range(g, g + per))))
        return res

    def make_qk_unit(sidx, i, ss, hv, h, blk, gcol_ap, dst, rot):
        st = {}

        def s0():
            if "sl" not in ss:
                ss["sl"] = ring_load(sidx)
            sl = ss["sl"]
            sv = ring[:, sl, :].rearrange("p (k n) -> p k n", k=8)
            ps = PS()
            mm(ps, ps.ap(), [(sv[:, kc, i * 128:(i + 1) * 128], hv[:, kc, :]) for kc in range(8)], [ring_r[sl]] + h.regs)
            st["raw"] = Tile()
            A(st["raw"].ap(), ps.ap(), AF.Copy, ps.regs, st["raw"].regs)

        def s1():
            st["sq"] = Tile()
            V_tt(st["sq"].ap(), st["raw"].ap(), st["raw"].ap(), ALU.mult, st["raw"].regs, st["sq"].regs, q=P.pool)

        def s2():
            st["ps2"] = PS()
            mm(st["ps2"], st["ps2"].ap(), [(CM(blk), st["sq"].ap())], st["sq"].regs + [const_r])

        def s3():
            st["rstd"] = Tile()
            rsqrt_from(st["ps2"].ap(), st["rstd"].ap(), st["ps2"].regs, st["rstd"].regs, st["sq"].ap(), st["sq"].regs)

        def s4():
            raw, rstd = st["raw"], st["rstd"]
            if rot is None:
                V_stt(dst.ap(), raw.ap(), gcol_ap, rstd.ap(), ALU.mult, ALU.mult, raw.regs + rstd.regs + [const_r], dst.regs)
                for k_ in ("raw", "sq", "rstd"):
                    st[k_].free()
            else:
                st["xn"] = Tile()
                V_stt(st["xn"].ap(), raw.ap(), gcol_ap, rstd.ap(), ALU.mult, ALU.mult, raw.regs + rstd.regs + [const_r], st["xn"].regs)

        def s5():
            st["ps3"] = PS()
            mm(st["ps3"], st["ps3"].ap(), [(CM(rot[2]), st["xn"].ap())], st["xn"].regs + [const_r])

        def s6():
            st["t1"] = Tile()
            V_tt(st["t1"].ap(), st["ps3"].ap(), rope[:, rot[1], :], ALU.mult, st["ps3"].regs + [const_r], st["t1"].regs)
            V_tt(st["raw"].ap(), st["xn"].ap(), rope[:, rot[0], :], ALU.mult, st["xn"].regs + [const_r], st["raw"].regs, q=P.pool)

        def s7():
            V_tt(dst.ap(), st["t1"].ap(), st["raw"].ap(), ALU.add, st["t1"].regs + st["raw"].regs, dst.regs)
            for k_ in ("raw", "sq", "rstd", "xn", "t1"):
                st[k_].free()

        return [s0, s1, s2, s3, s4] if rot is None else [s0, s1, s2, s3, s4, s5, s6, s7]

    def qk_project(slot_specs, hv, h, blk, gcols, dsts, rot):
        units = []
        ci = 0
        for (sidx, nch) in slot_specs:
            ss = {}
            for i in range(nch):
                units.append(make_qk_unit(sidx, i, ss, hv, h, blk, gcols[ci], dsts[ci], rot))
                ci += 1
        run_pipeline(units, spacing=2)

    def attn_jobs(jobs, LOOK=3):
        items = []
        for J in jobs:
            nk = len(J["keys"])
            per = 512 // J["ncol"]
            i = 0
            while i < nk:
                g = min(per, nk - i)
                items.append((J, i, g))
                i += g
        pend = []

        def do_pv(ent):
            J, i, g, pT = ent
            nk = len(J["keys"])
            ncol = J["ncol"]
            if i == 0:
                J["acc"] = PS("acc")
            acc = J["acc"]
            for u in range(g):
                vap, vregs = J["vwin"](i + u)
                mm(acc, acc.ap()[0:128, 0:ncol], [(vap, pT.ap()[:, u * ncol:(u + 1) * ncol])], vregs + pT.regs,
                   start=(i + u == 0), stop=(i + u == nk - 1))
            pT.free()
            if i + g == nk:
                J["finish"](acc)

        for (J, i, g) in items:
            ncol, qt, qb, q0 = J["ncol"], J["qt"], J["qbase"], J["q0"]
            ps = PS()
            for u in range(g):
                kap, kregs = J["keys"][i + u]
                mm(ps, ps.ap()[:, u * ncol:(u + 1) * ncol], [(kap, qt.ap()[qb:qb + 64, q0:q0 + ncol])], kregs + qt.regs)
            pT = Tile()
            A(pT.ap()[:, 0:g * ncol], ps.ap()[:, 0:g * ncol], AF.Exp, ps.regs, pT.regs, scale=J["scale"])
            pend.append((J, i, g, pT))
            if len(pend) > LOOK:
                do_pv(pend.pop(0))
        while pend:
            do_pv(pend.pop(0))

    def run_pipeline(units, spacing=2):
        n = len(units)
        ns = max(len(u) for u in units)
        for step in range((n - 1) * spacing + ns):
            for u in range(n):
                st_ = step - u * spacing
                if 0 <= st_ < len(units[u]):
                    units[u][st_]()

    def mixer_A_heads(l, QA, segs, keyf, vwinf, mixA):
        jobs = []
        for (q0, ncol, sid) in segs:
            for hh in range(6):
                ch, par, kv = hh // 2, hh % 2, hh // 3
                base = par * 64

                def finish(acc, ch=ch, par=par, q0=q0, ncol=ncol):
                    nb_, db_ = (0, 64) if par == 0 else (64, 0)
                    rd = Tile()
                    V_rcp(rd.ap()[db_:db_ + 64, 0:ncol], acc.ap()[db_:db_ + 64, 0:ncol], acc.regs, rd.regs)
                    V_tt(R(mixA[ch].ap()[nb_:nb_ + 64, q0:q0 + ncol]), acc.ap()[nb_:nb_ + 64, 0:ncol],
                         rd.ap()[db_:db_ + 64, 0:ncol], ALU.mult, acc.regs + rd.regs, mixA[ch].regs)
                    rd.free()

                jobs.append(dict(qt=QA[ch], qbase=base, ncol=ncol, q0=q0, keys=keyf(sid, kv, base), scale=0.125,
                                 vwin=(lambda i, sid=sid, kv=kv, par=par: vwinf(sid, kv, par, i)), finish=finish))
        attn_jobs(jobs)

    def c_out_norm(l, oc, mixCh):
        sq = Tile()
        V_tt(R(sq.ap()[0:96, :]), oc.ap()[0:96, :], oc.ap()[0:96, :], ALU.mult, oc.regs, sq.regs, q=P.pool)
        ps = PS()
        mm(ps, ps.ap()[0:96, :], [(CM(C_BLK96, 96, 96), sq.ap()[0:96, :])], sq.regs + [const_r])
        rstd = Tile()
        A(sq.ap()[0:96, :], ps.ap()[0:96, :], AF.Ln, ps.regs, sq.regs, bias=EPS)
        A(rstd.ap()[0:96, :], sq.ap()[0:96, :], AF.Exp, sq.regs, rstd.regs, scale=-0.5)
        V_stt(R(mixCh.ap()[0:96, :]), oc.ap()[0:96, :], gco[0:96, l:l + 1], rstd.ap()[0:96, :], ALU.mult, ALU.mult,
              oc.regs + rstd.regs + [misc_r], mixCh.regs)
        sq.free()
        rstd.free()
        oc.free()

    def mixer_C_heads(l, QC, heads, segs, keyf, vwinf, mixC):
        jobs = []
        for hi, hh in enumerate(heads):
            st_h = {}
            for si_, (q0, ncol, sid) in enumerate(segs):
                st_s = {}
                for m in range(2):
                    base = m * 64

                    def finish(acc, m=m, ncol=ncol, q0=q0, st_h=st_h, st_s=st_s, hi=hi, last_seg=(si_ == len(segs) - 1)):
                        if "oc" not in st_h:
                            st_h["oc"] = Tile()
                        oc = st_h["oc"]
                        accs = Tile()
                        A(accs.ap()[:, 0:ncol], acc.ap()[:, 0:ncol], AF.Copy, acc.regs, accs.regs)
                        psd = PS()
                        mm(psd, psd.ap()[0:96, 0:ncol], [(CM(C_SEL32, 128, 96), accs.ap()[:, 0:ncol])], accs.regs + [const_r])
                        rd = Tile()
                        V_rcp(rd.ap()[0:96, 0:ncol], psd.ap()[0:96, 0:ncol], psd.regs, rd.regs)
                        om = Tile()
                        st_s[m] = om
                        V_tt(om.ap()[0:96, 0:ncol], accs.ap()[0:96, 0:ncol], rd.ap()[0:96, 0:ncol], ALU.mult,
                             accs.regs + rd.regs, om.regs)
                        rd.free()
                        accs.free()
                        if m == 1:
                            V_stt(oc.ap()[0:96, q0:q0 + ncol], st_s[1].ap()[0:96, 0:ncol], lam[0:96, l, 1:2],
                                  st_s[0].ap()[0:96, 0:ncol], ALU.mult, ALU.add, st_s[0].regs + st_s[1].regs + [misc_r], oc.regs)
                            st_s[0].free()
                            st_s[1].free()
                            if last_seg:
                                c_out_norm(l, oc, mixC[hi])

                    jobs.append(dict(qt=QC[hi], qbase=base, ncol=ncol, q0=q0, keys=keyf(sid, hh, base), scale=48 ** -0.5,
                                     vwin=(lambda i, sid=sid, hh=hh: vwinf(sid, hh, i)), finish=finish))
        attn_jobs(jobs)

    def gla(l, BQ, BK, BG, KTM, VTM, Vpad, segs, sample, st_dst):
        ktm_v = KTM.ap().rearrange("p (b n) -> p b n", b=4)
        vtm_v = VTM.ap().rearrange("p (b n) -> p b n", b=4)
        vpad_v = Vpad.ap().rearrange("p (b h n) -> p b h n", b=4, h=4)
        Lt = Tile(2)
        Lv = Lt.ap().rearrange("p (b n) -> p b n", b=4)
        for tb in range(4):
            ps = PS()
            mm(ps, ps.ap()[:, 0:256], [(BG.ap()[0:32, tb * 128:(tb + 1) * 128], w2[0:32, l, :])], BG.regs + [const_r])
            zb = Tile()
            V_tt(zb.ap()[:, 0:256], ps.ap()[:, 0:256], bgla[:, l, :], ALU.add, ps.regs + [const_r], zb.regs)
            A(zb.ap()[:, 0:256], zb.ap()[:, 0:256], AF.Exp, zb.regs, zb.regs, scale=-1.0)
            A(R(Lv[:, tb, :]), zb.ap()[:, 0:256], AF.Ln, zb.regs, Lt.regs, bias=1.0)
            zb.free()
        osb = [Tile(), Tile()]
        keep = {}
        for (c0, nb, sid) in segs:
            T = nb * 128
            tb0 = c0 // 128
            mid = T // 2 - 1
            psf, psbk = PS(), PS()
            for jb in range(nb):
                mm(psf, psf.ap()[:, jb * 128:T], [(Lv[:, tb0 + jb, 0:128], urow[:, 0, 0:T - jb * 128])], Lt.regs + [const_r],
                   start=(jb == 0), stop=(jb == nb - 1))
            for idx, jb in enumerate(range(nb - 1, -1, -1)):
                mm(psbk, psbk.ap()[:, 0:(jb + 1) * 128], [(Lv[:, tb0 + jb, 128:256], urow[:, 1, (4 - 1 - jb) * 128:512])],
                   Lt.regs + [const_r], start=(idx == 0), stop=(idx == nb - 1))
            V_cp(small[:, 0:1], psf.ap()[:, mid:mid + 1], psf.regs, [small_r])
            V_ts(small[:, 1:2], psf.ap()[:, mid:mid + 1], -1.0, None, ALU.mult, None, psf.regs, [small_r])
            V_cp(small[:, 2:3], psbk.ap()[:, mid:mid + 1], psbk.regs, [small_r])
            V_ts(small[:, 3:4], psbk.ap()[:, mid:mid + 1], -1.0, None, ALU.mult, None, psbk.regs, [small_r])
            Ef, Enf, Eb, Enb = Tile(), Tile(), Tile(), Tile()
            A(Ef.ap()[:, 0:T], psf.ap()[:, 0:T], AF.Exp, psf.regs + [small_r], Ef.regs, bias=small[:, 1:2], scale=1.0)
            A(Enf.ap()[:, 0:T], psf.ap()[:, 0:T], AF.Exp, psf.regs + [small_r], Enf.regs, bias=small[:, 0:1], scale=-1.0)
            A(Eb.ap()[:, 0:T], psbk.ap()[:, 0:T], AF.Exp, psbk.regs + [small_r], Eb.regs, bias=small[:, 3:4], scale=1.0)
            A(Enb.ap()[:, 0:T], psbk.ap()[:, 0:T], AF.Exp, psbk.regs + [small_r], Enb.regs, bias=small[:, 2:3], scale=-1.0)
            if sample:
                A(small[:, 6:7], psf.ap()[:, T - 1:T], AF.Exp, psf.regs, [small_r])
                A(small[:, 7:8], psbk.ap()[:, 0:1], AF.Exp, psbk.regs, [small_r])
                A(small[:, 4:5], small[:, 0:1], AF.Exp, [small_r], [small_r])
                A(small[:, 5:6], small[:, 2:3], AF.Exp, [small_r], [small_r])
                V_ts(small[:, 4:6], small[:, 4:6], float(32 ** -0.5), None, ALU.mult, None, [small_r], [small_r])
                qinf, qinb = Tile(), Tile()
                V_stt(R(qinf.ap()), Ef.ap(), small[:, 4:5], BQ.ap(), ALU.mult, ALU.mult, Ef.regs + BQ.regs + [small_r], qinf.regs)
                V_stt(R(qinb.ap()), Eb.ap(), small[:, 5:6], BQ.ap(), ALU.mult, ALU.mult, Eb.regs + BQ.regs + [small_r], qinb.regs)
                keep["qinf"], keep["qinb"] = qinf, qinb
                d_ap, d_rg = xin_v(1280, 1, 2, nfl=256)
                pool_st(d_ap, small[:, 6:8], [small_r], [d_rg])
            ktf, ktb = Tile(), Tile()
            V_tt(R(ktf.ap()[:, 0:T]), BK.ap()[:, c0:c0 + T], Enf.ap()[:, 0:T], ALU.mult, BK.regs + Enf.regs, ktf.regs)
            V_tt(R(ktb.ap()[:, 0:T]), BK.ap()[:, c0:c0 + T], Enb.ap()[:, 0:T], ALU.mult, BK.regs + Enb.regs, ktb.regs)
            pso = [PS("acc"), PS("acc")]
            def head_unit(hh):
                st_ = {}

                def s0():
                    qf, qb = Tile(), Tile()
                    st_["qf"], st_["qb"] = qf, qb
                    V_stt(R(qf.ap()[:, 0:T]), Ef.ap()[:, 0:T], hmask[:, hh:hh + 1], BQ.ap()[:, c0:c0 + T], ALU.mult, ALU.mult,
                          Ef.regs + BQ.regs + [const_r], qf.regs)
                    V_stt(R(qb.ap()[:, 0:T]), Eb.ap()[:, 0:T], hmask[:, hh:hh + 1], BQ.ap()[:, c0:c0 + T], ALU.mult, ALU.mult,
                          Eb.regs + BQ.regs + [const_r], qb.regs)

                def s1():
                    qf, qb = st_["qf"], st_["qb"]
                    MT = Tile(nb * T // 512)
                    st_["MT"] = MT
                    Mv = MT.ap().rearrange("p (b n) -> p b n", b=nb)
                    for jb in range(nb):
                        pf, pb = PS(), PS()
                        mm(pf, pf.ap()[:, 0:T - jb * 128], [(ktf.ap()[:, jb * 128:(jb + 1) * 128], qf.ap()[:, jb * 128:T])],
                           ktf.regs + qf.regs)
                        mm(pb, pb.ap()[:, 0:(jb + 1) * 128], [(ktb.ap()[:, jb * 128:(jb + 1) * 128], qb.ap()[:, 0:(jb + 1) * 128])],
                           ktb.regs + qb.regs)
                        if jb > 0:
                            A(R(Mv[:, jb, 0:jb * 128]), pb.ap()[:, 0:jb * 128], AF.Copy, pb.regs, MT.regs)
                        if jb < nb - 1:
                            A(R(Mv[:, jb, (jb + 1) * 128:T]), pf.ap()[:, 128:T - jb * 128], AF.Copy, pf.regs, MT.regs)
                        t1 = Tile()
                        V_tt(t1.ap()[:, 0:128], pf.ap()[:, 0:128], CM(C_TRIU), ALU.mult, pf.regs + [const_r], t1.regs)
                        V_tt(t1.ap()[:, 128:256], pb.ap()[:, jb * 128:(jb + 1) * 128], CM(C_TRIL), ALU.mult, pb.regs + [const_r], t1.regs)
                        V_tt(R(Mv[:, jb, jb * 128:(jb + 1) * 128]), t1.ap()[:, 0:128], t1.ap()[:, 128:256], ALU.add, t1.regs, MT.regs,
                             q=P.pool)
                        t1.free()

                def s2():
                    MT = st_["MT"]
                    Mv = MT.ap().rearrange("p (b n) -> p b n", b=nb)
                    for jb in range(nb):
                        first = (hh % 2 == 0 and jb == 0)
                        last = (hh % 2 == 1 and jb == nb - 1)
                        mm(pso[hh // 2], pso[hh // 2].ap()[:, 0:T], [(vpad_v[:, tb0 + jb, hh, :], Mv[:, jb, :])], Vpad.regs + MT.regs,
                           start=first, stop=last)
                    MT.free()
                    st_["qf"].free()
                    st_["qb"].free()

                return [s0, s1, s2]

            if sample:
                for hh in range(4):
                    for f_ in head_unit(hh):
                        f_()
            else:
                run_pipeline([head_unit(hh) for hh in range(4)], spacing=1)
            for c2 in range(2):
                if not sample:
                    V_cp(osb[c2].ap()[:, c0:c0 + T], pso[c2].ap()[:, 0:T], pso[c2].regs, osb[c2].regs)
                else:
                    V_cp(osb[c2].ap()[:, 0:T], pso[c2].ap()[:, 0:T], pso[c2].regs, osb[c2].regs)
            for d_ in range(2):
                kd = Tile()
                kdv = kd.ap().rearrange("p (b n) -> p b n", b=4)
                for jb in range(nb):
                    ps = PS()
                    prs = []
                    if d_ == 0:
                        for j2 in range(jb, nb):
                            lt = CM(C_SL16) if j2 == jb else urow[:, 0, 128:256]
                            prs.append((lt, Lv[:, tb0 + j2, 0:128]))
                    else:
                        for j2 in range(0, jb + 1):
                            lt = CM(C_SU16) if j2 == jb else urow[:, 0, 128:256]
                            prs.append((lt, Lv[:, tb0 + j2, 128:256]))
                    mm(ps, ps.ap()[:, 0:128], prs, Lt.regs + [const_r])
                    ed = Tile()
                    A(ed.ap()[:, 0:128], ps.ap()[:, 0:128], AF.Exp, ps.regs, ed.regs)
                    V_tt(R(kdv[:, jb, :]), ktm_v[:, tb0 + jb, :], ed.ap()[:, 0:128], ALU.mult, KTM.regs + ed.regs, kd.regs)
                    ed.free()
                pst = PS()
                mm(pst, pst.ap()[:, 0:256], [(kdv[:, jb, :], vtm_v[:, tb0 + jb, :]) for jb in range(nb)], kd.regs + VTM.regs)
                kd.free()
                stt = Tile()
                if not sample:
                    for hh in range(4):
                        A(stt.ap()[32 * hh:32 * hh + 32, 0:64], pst.ap()[32 * hh:32 * hh + 32, 64 * hh:64 * hh + 64], AF.Copy,
                          pst.regs, stt.regs)
                    pool_st(st_dst(sid, d_), stt.ap()[:, 0:64], stt.regs)
                else:
                    A(stt.ap()[:, 0:256], pst.ap()[:, 0:256], AF.Copy, pst.regs, stt.regs)
                    r0 = 1152 + 64 * d_
                    d_ap, d_rg = xin_v(r0, 64, 256)
                    pool_st(d_ap, stt.ap()[:, 0:256], stt.regs, [d_rg])
                stt.free()
            for tl_ in (Ef, Enf, Eb, Enb, ktf, ktb):
                tl_.free()
        Lt.free()
        return osb, keep

    def gla_out(l, osb, GR, mixB):
        for c2 in range(2):
            sq = Tile()
            V_tt(R(sq.ap()), osb[c2].ap(), osb[c2].ap(), ALU.mult, osb[c2].regs, sq.regs, q=P.pool)
            ps = PS()
            mm(ps, ps.ap(), [(CM(C_BLK64), sq.ap())], sq.regs + [const_r])
            rstd = Tile()
            rsqrt_from(ps.ap(), rstd.ap(), ps.regs, rstd.regs, sq.ap(), sq.regs)
            V_stt(sq.ap(), osb[c2].ap(), pvec[:, l, 100:101], rstd.ap(), ALU.mult, ALU.mult, osb[c2].regs + rstd.regs + [const_r], sq.regs)
            V_tt(R(mixB[c2].ap()), sq.ap(), GR[c2].ap(), ALU.mult, sq.regs + GR[c2].regs, mixB[c2].regs)
            sq.free()
            rstd.free()

    def w_out(l, t, j, mix):
        gtT, mod_r = gtT2[:, l % 2], mod_rs[l % 2]
        base = l * NSL + 91
        for m in range(8):
            sl = ring_load(base + m, 1152)
            sv = ring[:, sl, 0:1152].rearrange("p (c n) -> p c n", c=9)
            psy = PS()
            prs, rds = [], [ring_r[sl]]
            for c in range(9):
                rows = 128 if c < 5 else 96
                prs.append((sv[0:rows, c, :], mix[c].ap()[0:rows, :]))
                rds += mix[c].regs
            mm(psy, psy.ap(), prs, rds)
            xa = xT[:, m, t * 512:(t + 1) * 512]
            V_stt(xa, psy.ap(), gtT[:, 1, m, j:j + 1], xa, ALU.mult, ALU.add, psy.regs + [mod_r, x_r[m][t]], [x_r[m][t]])

    import os as _os
    stopat = _os.environ.get("STOPAT", "")

    class _Stop(Exception):
        pass

    def chk(name):
        if stopat == name:
            raise _Stop()

    def mixer(l, t):
        try:
            mixer_(l, t)
        except _Stop:
            pass

    def mixer_(l, t):
        sample = (t == 2)
        j = 1 if sample else 0
        wb = l * NSL + 76
        h = Tile(8)
        norm_mod(l, t, 1, j, h)
        hv = h.ap().rearrange("p (c f) -> p c f", c=8)
        mix = [Tile() for _ in range(9)] if not sample else None
        tcol = t * 512
        QA = [Tile() for _ in range(3)]
        KA = [Tile() for _ in range(2)]
        dsts = QA + KA
        rotA = (0, 1, C_PSWA) if sample else None
        qk_project([(wb + 0, 2), (wb + 1, 2), (wb + 2, 1)], hv, h, C_BLK64,
                   [pvec[:, l, 96:97]] * 3 + [pvec[:, l, 97:98]] * 2, dsts, rotA)
        (ps, pv, tbs), = proj_tm(wb + 3, hv, h, 128)
        avt = Tile()
        V_cp(avt.ap(), ps.ap(), ps.regs, avt.regs)
        if not sample:
            pool_st(av_o[l, tcol:tcol + 512, :].rearrange("(b p) c -> p b c", p=128), avt.ap().rearrange("p (b c) -> p b c", b=4), avt.regs)
            for kv in range(2):
                pool_st(ak_o[l, kv, :, tcol:tcol + 512], KA[kv].ap()[0:64, :], KA[kv].regs)
        else:
            for kv in range(2):
                d_ap, d_rg = xin_rows(kv * 64, 64)
                pool_st(d_ap, KA[kv].ap()[0:64, :], KA[kv].regs, [d_rg])
            d_ap, d_rg = xin_v(640, 128, 128)
            pool_st(d_ap.rearrange("(b p) c -> p b c", p=128),
                    avt.ap().rearrange("p (b c) -> p b c", b=4), avt.regs, [d_rg])
            for tl_ in KA:
                tl_.free()
        chk("Aproj")
        VAkv = []
        if not sample:
            VAo = Tile(2)
            vao_v = VAo.ap()[:, 0:768].rearrange("p (b n) -> p b n", b=4)
            fill(VAo, 1.0)
            VAo2 = Tile(2)
            fill(VAo2, 1.0)
            vao2_v = VAo2.ap()[:, 0:768].rearrange("p (b n) -> p b n", b=4)
            avv = avt.ap().rearrange("p (b c) -> p b c", b=4)
            V_cp(R(vao_v[:, :, 64:128]), avv[:, :, 0:64], avt.regs, VAo.regs, q=P.pool)
            V_cp(R(vao2_v[:, :, 64:128]), avv[:, :, 64:128], avt.regs, VAo2.regs, q=P.pool)
            VAkv = [(VAo, vao_v), (VAo2, vao2_v)]

            def keyfA(sid, kv, base):
                return [(KA[kv].ap()[base:base + 64, sid * 256 + kc * 128: sid * 256 + (kc + 1) * 128], KA[kv].regs) for kc in range(2)]

            def vwinA(sid, kv, par, i):
                tl_, vv = VAkv[kv]
                off = 64 if par == 0 else 0
                return (vv[:, sid * 2 + i, off:off + 128], tl_.regs)

            mixer_A_heads(l, QA, [(0, 256, 0), (256, 256, 1)], keyfA, vwinA, mix[0:3])
            VAo2.free()
            VAo.free()
            for tl_ in QA + KA:
                tl_.free()
        avt.free()
        chk("A")
        QC = [Tile() for _ in range(4)]
        KC = [Tile() for _ in range(4)]
        dsts = QC + KC
        rotC = (2, 3, C_PSWC) if sample else None
        qk_project([(wb + 4 + si, 2) for si in range(4)], hv, h, C_BLK48,
                   [pvec[:, l, 98:99]] * 4 + [pvec[:, l, 99:100]] * 4, dsts, rotC)
        cvt = Tile(3)
        cvv = cvt.ap().rearrange("p (b c) -> p b c", b=4)
        for (ps, pv, tbs) in proj_tm(wb + 8, hv, h, 256):
            V_cp(cvv[:, tbs[0]:tbs[0] + 2, 0:256], pv, ps.regs, cvt.regs)
        for (ps, pv, tbs) in proj_tm(wb + 9, hv, h, 128):
            V_cp(cvv[:, :, 256:384], pv, ps.regs, cvt.regs)
        if not sample:
            pool_st(cv_o[l, tcol:tcol + 512, :].rearrange("(b p) c -> p b c", p=128), cvv, cvt.regs)
            for hh in range(4):
                pool_st(ck_o[l, hh, :, tcol:tcol + 512], KC[hh].ap(), KC[hh].regs)
            VCo = [Tile() for _ in range(4)]
            for hh in range(4):
                vv = VCo[hh].ap().rearrange("p (b n) -> p b n", b=4)
                fill(VCo[hh], 1.0)
                V_cp(R(vv[:, :, 0:96]), cvv[:, :, hh * 96:(hh + 1) * 96], cvt.regs, VCo[hh].regs, q=P.pool)

            def keyfC(sid, hh, base):
                return [(KC[hh].ap()[base:base + 64, sid * 256 + kc * 128: sid * 256 + (kc + 1) * 128], KC[hh].regs) for kc in range(2)]

            def vwinC(sid, hh, i):
                vv = VCo[hh].ap().rearrange("p (b n) -> p b n", b=4)
                return (vv[:, sid * 2 + i, :], VCo[hh].regs)

            mixer_C_heads(l, QC, [0, 1, 2, 3], [(0, 256, 0), (256, 256, 1)], keyfC, vwinC, mix[5:9])
            for tl_ in VCo + QC + KC:
                tl_.free()
        else:
            for hh in range(4):
                d_ap, d_rg = xin_rows(128 + hh * 128, 128)
                pool_st(d_ap, KC[hh].ap(), KC[hh].regs, [d_rg])
            d_ap, d_rg = xin_v(768, 384, 384)
            pool_st(d_ap.rearrange("(b p) c -> p b c", p=128), cvv, cvt.regs, [d_rg])
            for b_ in range(2):
                E(P.pool, lambda e, b_=b_: e.collective_compute("AllGather", ALU.bypass, replica_groups=[[0, 1, 2, 3], [4, 5, 6, 7]],
                                                               ins=[xch_in_t[b_].ap().opt()], outs=[xch_out_t[b_].ap().opt()]),
                  [xin_rs[b_]], [xout_rs[b_]], tl=cc_tls[l * 3 + b_])
            for tl_ in KC:
                tl_.free()
        cvt.free()
        chk("C")
        BQ, BK, BR0, BR1, BG = Tile(), Tile(), Tile(), Tile(), Tile()
        raws = [BQ, BK, BR0, BR1, BG]
        ci = 0
        for si, nch in enumerate((2, 2, 1)):
            pss = proj_fm(wb + 10 + si, hv, h, nch)
            for ps in pss:
                dst = raws[ci]
                if ci in (2, 3):
                    A(dst.ap(), ps.ap(), AF.Silu, ps.regs, dst.regs)
                elif ci == 4:
                    A(R(dst.ap()[0:32, :]), ps.ap()[0:32, :], AF.Copy, ps.regs, dst.regs)
                else:
                    A(dst.ap(), ps.ap(), AF.Copy, ps.regs, dst.regs)
                ci += 1
        chk("Braw")
        KTM, VTM, Vpad = Tile(), Tile(2), Tile(4)
        ktm_v = KTM.ap().rearrange("p (b n) -> p b n", b=4)
        vtm_v = VTM.ap().rearrange("p (b n) -> p b n", b=4)
        vpad_v = Vpad.ap().rearrange("p (b h n) -> p b h n", b=4, h=4)
        if _os.environ.get("SKIP", "") != "fill":
            fill(Vpad, 0.0)
        for (ps, pv, tbs) in proj_tm(wb + 13, hv, h, 256):
            b0 = tbs[0]
            for bi in range(2):
                A(R(ktm_v[:, b0 + bi, :]), pv[:, bi, 0:128], AF.Copy, ps.regs, KTM.regs)
                A(R(vtm_v[:, b0 + bi, 0:128]), pv[:, bi, 128:256], AF.Copy, ps.regs, VTM.regs)

        chk("Btm1")
        for (ps, pv, tbs) in proj_tm(wb + 14, hv, h, 128):
            for bi in range(4):
                A(R(vtm_v[:, bi, 128:256]), pv[:, bi, :], AF.Copy, ps.regs, VTM.regs)
        for hh in range(4):
            o_ = (hh % 2) * 64
            V_cp(R(vpad_v[:, :, hh, o_:o_ + 64]), vtm_v[:, :, hh * 64:(hh + 1) * 64], VTM.regs, Vpad.regs, q=P.pool)
        h.free()
        chk("Bproj")
        if not sample:
            segs = [(0, 2, 0), (256, 2, 1)]
            osb, _ = gla(l, BQ, BK, BG, KTM, VTM, Vpad, segs, False,
                         lambda sid, d_: st_o[l, t * 2 + sid, d_, :, :])
            for tl_ in (BQ, BK, BG, KTM, VTM, Vpad):
                tl_.free()
            chk("gla")
            gla_out(l, osb, [BR0, BR1], mix[3:5])
            for tl_ in osb + [BR0, BR1]:
                tl_.free()
            chk("glaout")
            w_out(l, t, j, mix)
            for tl_ in mix:
                tl_.free()
            return
        osb, keep = gla(l, BQ, BK, BG, KTM, VTM, Vpad, [(0, 4, 0)], True, None)
        for tl_ in (BQ, BK, BG, KTM, VTM, Vpad):
            tl_.free()
        for b_ in range(2, 3):
            E(P.pool, lambda e, b_=b_: e.collective_compute("AllGather", ALU.bypass, replica_groups=[[0, 1, 2, 3], [4, 5, 6, 7]],
                                                           ins=[xch_in_t[b_].ap().opt()], outs=[xch_out_t[b_].ap().opt()]),
              [xin_rs[b_]], [xout_rs[b_]], tl=cc_tls[l * 3 + b_])
        mix = [Tile() for _ in range(9)]
        def gla_post():
            Sin = []
            for d_ in range(2):
                S = Tile()
                fill(S, 0.0, 256)
                for hh in range(4):
                    pool_ld(R(S.ap()[32 * hh:32 * hh + 32, 64 * hh:64 * hh + 64]), R(sg_d[l, d_, 32 * hh:32 * hh + 32, :]), S.regs)
                SL = Tile(2)
                slv = SL.ap().rearrange("p (r c) -> p r c", r=4)
                r0 = 1152 + 64 * d_
                for r in range(4):
                    s_ap, s_rg = xout_v(r, r0, 64, 256)
                    pool_ld(R(slv[:, r, :]), R(s_ap), SL.regs, [s_rg])
                AD = Tile()
                adv = AD.ap()[:, 0:8].rearrange("p (r c) -> p r c", r=4)
                for r in range(4):
                    s_ap, s_rg = xout_v(r, 1280, 1, 2, nfl=256)
                    pool_ld(R(adv[:, r, :]), R(s_ap), AD.regs, [s_rg])
                V_ts(AD.ap()[:, 8:16], AD.ap()[:, 0:8], -1.0, None, ALU.add, None, AD.regs, AD.regs)
                am1 = AD.ap()[:, 8:16].rearrange("p (r c) -> p r c", r=4)
                order = range(4) if d_ == 0 else range(3, -1, -1)
                tmp = Tile()
                for r in order:
                    V_stt(tmp.ap()[:, 0:256], S.ap()[:, 0:256], am1[:, r, d_:d_ + 1], slv[:, r, :], ALU.mult, ALU.add,
                          S.regs + AD.regs + SL.regs, tmp.regs)
                    V_stt(S.ap()[:, 0:256], tmp.ap()[:, 0:256], rmask[:, d_ * 4 + r:d_ * 4 + r + 1], S.ap()[:, 0:256], ALU.mult, ALU.add,
                          tmp.regs + S.regs + [const_r], S.regs)
                V_tt(R(tmp.ap()[:, 0:256]), S.ap()[:, 0:256], bdm[:], ALU.mult, S.regs + [const_r], tmp.regs)
                Sin.append(tmp)
                S.free()
                SL.free()
                AD.free()
            qin = [keep["qinf"], keep["qinb"]]
            for c2 in range(2):
                ps = PS("acc")
                mm(ps, ps.ap(), [(Sin[d_].ap()[:, c2 * 128:(c2 + 1) * 128], qin[d_].ap()) for d_ in range(2)],
                   Sin[0].regs + Sin[1].regs + qin[0].regs + qin[1].regs)
                V_tt(osb[c2].ap(), osb[c2].ap(), ps.ap(), ALU.add, osb[c2].regs + ps.regs, osb[c2].regs)
            for tl_ in Sin + qin:
                tl_.free()
            gla_out(l, osb, [BR0, BR1], mix[3:5])
            for tl_ in osb + [BR0, BR1]:
                tl_.free()

        VAll = Tile(8)
        vav = VAll.ap()[:, 0:3840].rearrange("p (k n) -> p k n", k=20)
        fill(VAll, 1.0, q=P.dve)
        for kv in range(2):
            KAll = Tile(5)
            for half in range(2):
                s_ap, s_rg = xo_multi(kv * 64, 64)
                sp_ld(R(KAll.ap()[half * 64:(half + 1) * 64, 0:2048].rearrange("p (r c) -> p r c", r=4)),
                      R(s_ap), KAll.regs, [s_rg])
                sp_ld(R(KAll.ap()[half * 64:(half + 1) * 64, 2048:2560]), R(cak_d[l, kv, :, :]), KAll.regs)
            for r in range(4):
                s_ap, s_rg = xout_v(r, 640, 128, 128)
                src = s_ap.rearrange("(b p) c -> p b c", p=128)
                sp_ld(R(vav[:, r * 4:(r + 1) * 4, 64:128]), R(src[:, :, kv * 64:(kv + 1) * 64]), VAll.regs, [s_rg])
            sp_ld(R(vav[:, 16:20, 64:128]), R(cav_d[l, :, kv * 64:(kv + 1) * 64].rearrange("(b p) c -> p b c", p=128)), VAll.regs)
            jobs = []
            for g in range(3):
                hh = kv * 3 + g
                ch, par = hh // 2, hh % 2
                base = par * 64
                keys = [(KAll.ap()[base:base + 64, kc * 128:(kc + 1) * 128], KAll.regs) for kc in range(20)]

                def finish(acc, ch=ch, par=par):
                    nb_, db_ = (0, 64) if par == 0 else (64, 0)
                    rd = Tile()
                    V_rcp(rd.ap()[db_:db_ + 64, :], acc.ap()[db_:db_ + 64, :], acc.regs, rd.regs)
                    V_tt(R(mix[ch].ap()[nb_:nb_ + 64, :]), acc.ap()[nb_:nb_ + 64, :], rd.ap()[db_:db_ + 64, :], ALU.mult,
                         acc.regs + rd.regs, mix[ch].regs)
                    rd.free()

                off = 64 if par == 0 else 0
                jobs.append(dict(qt=QA[ch], qbase=base, ncol=512, q0=0, keys=keys, scale=0.125,
                                 vwin=(lambda i, off=off: (vav[:, i, off:off + 128], VAll.regs)), finish=finish))
            attn_jobs(jobs)
            KAll.free()
        VAll.free()
        for tl_ in QA:
            tl_.free()
        gla_post()
        KCh = Tile(5)
        VCh = Tile(5)
        vcv = VCh.ap().rearrange("p (k n) -> p k n", k=20)

        def keyfC2(sid, hh, base):
            return [(KCh.ap()[base:base + 64, kc * 128:(kc + 1) * 128], KCh.regs) for kc in range(20)]

        def vwinC2(sid, hh, i):
            return (vcv[:, i, :], VCh.regs)

        fill(VCh, 1.0, q=P.dve)
        for hh in range(4):
            s_ap, s_rg = xo_multi(128 + hh * 128, 128)
            sp_ld(R(KCh.ap()[:, 0:2048].rearrange("p (r c) -> p r c", r=4)), R(s_ap), KCh.regs, [s_rg])
            sp_ld(R(KCh.ap()[:, 2048:2560]), R(cck_d[l, hh, :, :]), KCh.regs)
            for r in range(4):
                s_ap, s_rg = xout_v(r, 768, 384, 384)
                src = s_ap.rearrange("(b p) c -> p b c", p=128)
                sp_ld(R(vcv[:, r * 4:(r + 1) * 4, 0:96]), R(src[:, :, hh * 96:(hh + 1) * 96]), VCh.regs, [s_rg])
            sp_ld(R(vcv[:, 16:20, 0:96]), R(ccv_d[l, :, hh * 96:(hh + 1) * 96].rearrange("(b p) c -> p b c", p=128)), VCh.regs)
            mixer_C_heads(l, [QC[hh]], [hh], [(0, 512, 0)], keyfC2, vwinC2, [mix[5 + hh]])
        KCh.free()
        VCh.free()
        for tl_ in QC:
            tl_.free()
        w_out(l, t, j, mix)
        for tl_ in mix:
            tl_.free()

    step = [0]

    def go():
        step[0] += 1
        return step[0] <= upto

    for l in range(nl):
        if l == 0:
            ada_begin(0)
            ada_end(0)
        if go():
            ffn_multi(l, [0, 1], 0)
        if go():
            mixer(l, 0)
        if go():
            mixer(l, 1)
        if go():
            if l + 1 < nl:
                ada_begin(l + 1)
                ffn_multi(l, [0, 1], 1, hook=lambda l=l: ada_slot(l + 1))
                ada_end(l + 1)
            else:
                ffn_multi(l, [0, 1], 1)
        if go():
            ffn_multi(l, [2], 0)
        if go():
            mixer(l, 2)
        if go():
            ffn_multi(l, [2], 1)
    for t in range(3):
        pool_st(yT_o[:, :, t * 512:(t + 1) * 512], xT[:, :, t * 512:(t + 1) * 512], [x_r[c][t] for c in range(8)])

    for t in P.tls:
        t.sem = es.enter_context(nc.semaphore(t.name))
    final_waits = [(t, t.count) for t in P.dma_tls if t.count > 0]
    with nc.allow_low_precision("float32r PE operands"), nc.Block() as block:
        @block.sync
        def _(e):
            P.replay(P.sp, e)

        @block.tensor
        def _(e):
            P.replay(P.pe, e)

        @block.scalar
        def _(e):
            P.replay(P.act, e)

        @block.vector
        def _(e):
            P.replay(P.dve, e)

        @block.gpsimd
        def _(e):
            P.replay(P.pool, e)
            for t, v in final_waits:
                e.wait_ge(t.sem, v)
    es.close()
    return nc


OFF = dict(a_q=0, a_k=384, a_v=512, b_q=640, b_k=768, b_v=896, b_g=1152, b_r=1184, c_q=1440, c_k=1824, c_v=2208)


def _w_in_cols():
    def pad(lst, n=256):
        return list(lst) + [-1] * (n - len(lst))

    def rng(a, n):
        return list(range(a, a + n))

    def cmap(base, hh):
        out = []
        for m in range(2):
            out += rng(base + hh * 96 + m * 48, 48) + [-1] * 16
        return out

    slots = []
    slots.append(rng(OFF["a_q"], 256))
    slots.append(rng(OFF["a_q"] + 256, 128) + rng(OFF["a_k"], 64) * 2)
    slots.append(pad(rng(OFF["a_k"] + 64, 64) * 2))
    slots.append(pad(rng(OFF["a_v"], 128)))
    for base in (OFF["c_q"], OFF["c_k"]):
        slots.append(cmap(base, 0) + cmap(base, 1))
        slots.append(cmap(base, 2) + cmap(base, 3))
    slots.append(rng(OFF["c_v"], 256))
    slots.append(pad(rng(OFF["c_v"] + 256, 128)))
    slots.append(rng(OFF["b_q"], 128) + rng(OFF["b_k"], 128))
    slots.append(rng(OFF["b_r"], 256))
    slots.append(pad(rng(OFF["b_g"], 32)))
    slots.append(rng(OFF["b_k"], 128) + rng(OFF["b_v"], 128))
    slots.append(pad(rng(OFF["b_v"] + 128, 128)))
    assert len(slots) == 15
    return np.array(slots, dtype=np.int64)


def pack_weights(inp, nl):
    NSL = 135
    wst = np.zeros((nl * NSL, 128, SLOTF), np.float32)
    cols = _w_in_cols()
    for l in range(nl):
        b = l * NSL
        for s in range(2):
            wg = inp["w_ffn_gate"][l, s].reshape(8, 128, 22, 128).transpose(2, 1, 0, 3)
            wu = inp["w_ffn_up"][l, s].reshape(8, 128, 22, 128).transpose(2, 1, 0, 3)
            gu = np.stack([wg, wu], axis=2).reshape(22, 128, SLOTF)
            wd = inp["w_ffn_down"][l, s].reshape(2, 11, 128, 8, 128).transpose(0, 3, 2, 1, 4).reshape(2, 8, 128, 1408)
            for half in range(2):
                o = b + s * 38 + half * 19
                wst[o:o + 11] = gu[half * 11:(half + 1) * 11]
                wst[o + 11:o + 19, :, 0:1408] = wd[half]
        wi = np.concatenate([inp["w_in"][l], np.zeros((D, 1), np.float32)], axis=1)
        for si in range(15):
            wc = wi[:, cols[si]]
            wst[b + 76 + si] = wc.reshape(8, 128, 256).transpose(1, 0, 2).reshape(128, SLOTF)
        wo = inp["w_out"][l]
        wpad = np.zeros((9, 128, D), np.float32)
        for c in range(5):
            wpad[c] = wo[c * 128:(c + 1) * 128]
        for c in range(4):
            wpad[5 + c, 0:96] = wo[640 + c * 96:640 + (c + 1) * 96]
        wst[b + 91:b + 99, :, 0:1152] = wpad.reshape(9, 128, 8, 128).transpose(2, 1, 0, 3).reshape(8, 128, 1152)
        wa = inp["w_ada"][l].reshape(8, 128, 36, 256).transpose(2, 1, 0, 3).reshape(36, 128, SLOTF)
        wst[b + 99:b + 135] = wa
    return wst


def make_consts():
    cm = np.zeros((128, NCM, 128), np.float32)
    p = np.arange(128)
    cm[:, C_ONESM, :] = 1.0 / D
    cm[:, C_BLK64, :] = (p[:, None] // 64 == p[None, :] // 64) / 64.0
    real48 = (p % 64) < 48
    cm[:, C_BLK48, :] = ((p[:, None] // 64 == p[None, :] // 64) & real48[:, None] & real48[None, :]) / 48.0
    cm[0:96, C_BLK96, 0:96] = 1.0 / 96.0
    cm[96:128, C_SEL32, 0:96] = 1.0 / 32.0
    permA = np.zeros(128, np.int64)
    for m in range(128):
        d = m % 64
        permA[m] = m + 16 if (d % 32) < 16 else m - 16
    cm[permA, C_PSWA, p] = 1.0
    permC = np.arange(128)
    for m in range(128):
        d = m % 64
        if d < 48:
            permC[m] = m + 12 if (d % 24) < 12 else m - 12
    for m in range(128):
        if (m % 64) < 48:
            cm[permC[m], C_PSWC, m] = 1.0
    cm[:, C_TRIU, :] = (p[:, None] <= p[None, :])
    cm[:, C_TRIL, :] = (p[:, None] >= p[None, :])
    cm[:, C_SL16, :] = (p[:, None] > p[None, :]).astype(np.float32) / -16.0
    cm[:, C_SU16, :] = (p[:, None] < p[None, :]).astype(np.float32) / -16.0
    ur = np.zeros((128, 2, 512), np.float32)
    ur[:, 0, :] = -1.0 / 16.0
    ur[:, 0, 0:128] = (p[:, None] <= p[None, :]).astype(np.float32) / -16.0
    ur[:, 1, :] = -1.0 / 16.0
    ur[:, 1, 384:512] = (p[:, None] >= p[None, :]).astype(np.float32) / -16.0
    hm = np.zeros((128, 4), np.float32)
    for hh in range(4):
        hm[32 * hh:32 * hh + 32, hh] = 32 ** -0.5
    bd = np.zeros((128, 256), np.float32)
    for hh in range(4):
        bd[32 * hh:32 * hh + 32, 64 * hh:64 * hh + 64] = 1.0
    return cm, ur, hm, bd


def rope_tables(tok0):
    t = np.arange(tok0, tok0 + 512)
    row = (t // 64).astype(np.float32)
    col = (t % 64).astype(np.float32)
    out = np.zeros((128, 4, 512), np.float32)
    out[:, 0, :] = 1.0
    out[:, 2, :] = 1.0
    for p in range(128):
        d = p % 64
        half, dd = d // 32, d % 32
        i = dd % 16
        f = np.float32(10000.0) ** (-np.float32(i) / np.float32(16))
        ang = (row if half == 0 else col) * np.float32(f)
        out[p, 0] = np.cos(ang)
        out[p, 1] = (-np.sin(ang)) if dd < 16 else np.sin(ang)
        if d < 48:
            half, dd = d // 24, d % 24
            i = dd % 12
            f = np.float32(10000.0) ** (-np.float32(i) / np.float32(12))
            ang = (row if half == 0 else col) * np.float32(f)
            out[p, 2] = np.cos(ang)
            out[p, 3] = (-np.sin(ang)) if dd < 12 else np.sin(ang)
        else:
            out[p, 2] = 1.0
            out[p, 3] = 0.0
    return out


def pack_pvec(inp, nl):
    pv = np.zeros((128, nl, NV), np.float32)
    p = np.arange(128)
    for l in range(nl):
        pv[:, l, 0:24] = inp["g_norm"][l].reshape(3, 8, 128).transpose(2, 0, 1).reshape(128, 24)
        pv[:, l, 24:96] = inp["b_ada"][l].reshape(72, 128).T
        pv[:, l, 96] = inp["g_a_q"][l][p % 64]
        pv[:, l, 97] = inp["g_a_k"][l][p % 64]
        for col, key in ((98, "g_c_q"), (99, "g_c_k")):
            g = np.zeros((2, 64), np.float32)
            g[:, 0:48] = inp[key][l]
            pv[:, l, col] = g.reshape(128)
        pv[:, l, 100] = inp["g_gla"][l][p % 64]
        pv[0:96, l, 101] = inp["g_c_out"][l]
    return pv


_NC_CACHE = {}


def kernel(**inp):
    return run(inp, L_FULL)


def run(inp, nl, trace=False):
    inp = {k: np.asarray(v) for k, v in inp.items()}
    if nl not in _NC_CACHE:
        _NC_CACHE[nl] = build(nl)
    nc = _NC_CACHE[nl]
    wst = pack_weights(inp, nl)
    cm, ur, hm, bd = make_consts()
    pv = pack_pvec(inp, nl)
    bgla = np.broadcast_to(inp["b_gla"][:nl].reshape(1, nl, 256), (128, nl, 256)).copy()
    w2 = np.zeros((32, nl, 256), np.float32)
    for l in range(nl):
        w2[0:16, l, 0:128] = inp["w_gla_up"][l, 0]
        w2[16:32, l, 128:256] = inp["w_gla_up"][l, 1]
    lamc = np.broadcast_to(inp["lam_c"][:nl].reshape(1, nl * 4 * 48), (128, nl * 4 * 48)).copy()
    ozc = np.zeros((128, 2, 512), np.float32)
    ozc[:, 1, :] = 1.0
    in_maps = []
    for c in range(8):
        b, r = c // 4, c % 4
        xp = inp["x_prompt"][4 * c:4 * c + 4].reshape(1024, D)
        xs = inp["x_sample"][b, r * 512:(r + 1) * 512]
        xt = np.concatenate([xp, xs], axis=0)
        xin = xt.reshape(NTOK, 8, 128).transpose(2, 1, 0).copy()
        cond = np.stack([inp["c_ctx"], inp["c"][b]], axis=1).reshape(8, 128, 2).transpose(1, 0, 2).copy()
        rm = np.zeros((128, 8), np.float32)
        for rr in range(4):
            rm[:, rr] = 1.0 if rr < r else 0.0
            rm[:, 4 + rr] = 1.0 if rr > r else 0.0
        cak = inp["cache_a_k"][b, :nl].transpose(0, 2, 3, 1).copy()
        cav = inp["cache_a_v"][b, :nl].reshape(nl, 512, 128).copy()
        ck = inp["cache_c_k"][b, :nl].reshape(nl, 512, 4, 2, 48)
        cck = np.zeros((nl, 4, 2, 64, 512), np.float32)
        cck[:, :, :, 0:48, :] = ck.transpose(0, 2, 3, 4, 1)
        cck = cck.reshape(nl, 4, 128, 512)
        ccv = inp["cache_c_v"][b, :nl].reshape(nl, 512, 384).copy()
        sg = inp["state_gla"][b, :nl].reshape(nl, 2, 128, 64).copy()
        in_maps.append(dict(wst=wst, xin=xin, cmat=cm, urow=ur, rope=rope_tables(r * 512), hmask=hm, bdmask=bd, pvec=pv, oz=ozc,
                            bgla=bgla, w2=w2, lamc=lamc, condT=cond, rmask=rm, cakT=cak, cav=cav, cckT=cck, ccv=ccv, sgla=sg))
    if trace:
        res = run_bass_kernel_spmd(nc, in_maps, core_ids=list(range(8)), trace=True)
        print("exec_time_ns", res.exec_time_ns)
    else:
        res = run_bass_kernel_spmd(nc, in_maps, core_ids=list(range(8)))
    return assemble(res.results, nl)


def assemble(results, nl):
    y_prompt = np.zeros((32, 256, D), np.float32)
    y_sample = np.zeros((2, 2048, D), np.float32)
    n_ak = np.zeros((32, nl, 256, 2, 64), np.float32)
    n_av = np.zeros((32, nl, 256, 2, 64), np.float32)
    n_ck = np.zeros((32, nl, 256, 4, 96), np.float32)
    n_cv = np.zeros((32, nl, 256, 4, 96), np.float32)
    n_st = np.zeros((32, nl, 2, 4, 32, 64), np.float32)
    for c in range(8):
        r = results[c]
        b, rk = c // 4, c % 4
        y = np.asarray(r["yT"]).transpose(2, 1, 0).reshape(NTOK, D)
        y_prompt[4 * c:4 * c + 4] = y[0:1024].reshape(4, 256, D)
        y_sample[b, rk * 512:(rk + 1) * 512] = y[1024:]
        ak = np.asarray(r["akT"])
        n_ak[4 * c:4 * c + 4] = ak.reshape(nl, 2, 64, 4, 256).transpose(3, 0, 4, 1, 2)
        av = np.asarray(r["av"])
        n_av[4 * c:4 * c + 4] = av.reshape(nl, 4, 256, 2, 64).transpose(1, 0, 2, 3, 4)
        ck = np.asarray(r["ckT"]).reshape(nl, 4, 2, 64, 4, 256)[:, :, :, 0:48]
        n_ck[4 * c:4 * c + 4] = ck.transpose(4, 0, 5, 1, 2, 3).reshape(4, nl, 256, 4, 96)
        cv = np.asarray(r["cv"])
        n_cv[4 * c:4 * c + 4] = cv.reshape(nl, 4, 256, 4, 96).transpose(1, 0, 2, 3, 4)
        st = np.asarray(r["st"])
        n_st[4 * c:4 * c + 4] = st.reshape(nl, 4, 2, 4, 32, 64).transpose(1, 0, 2, 3, 4, 5)
    return (y_prompt, y_sample, n_ak, n_av, n_ck, n_cv, n_st)
```

```python
import math
from contextlib import ExitStack

import numpy as np
import concourse.bass as bass
import concourse.mybir as mybir
from concourse.bass_utils import run_bass_kernel_spmd

F32 = mybir.dt.float32
F32R = mybir.dt.float32r
AF = mybir.ActivationFunctionType
ALU = mybir.AluOpType
AX = mybir.AxisListType

D = 1024
L_FULL = 4
DFF = 2816
NTOK = 1536
EPS = 1e-6
NSLOT = 4
SLOTF = 2048
NA = 44
XR = 1281
NV = 104

(C_ONESM, C_BLK64, C_BLK48, C_BLK96, C_SEL32, C_PSWA, C_PSWC, C_TRIU, C_TRIL, C_SL16, C_SU16, C_IDENT) = range(12)
NCM = 12


def R(ap):
    return ap if ap.dtype == F32R else ap.bitcast(F32R)


def RO(ap):
    return R(ap) if ap.name == "arena" else ap


class TL:
    def __init__(self, name, step):
        self.name, self.step, self.count, self.sem = name, step, 0, None


class Reg:
    __slots__ = ("name", "w", "r", "excl")

    def __init__(self, name, excl=False):
        self.name, self.w, self.r, self.excl = name, None, {}, excl


class Q:
    def __init__(self, name, no_self=False):
        self.name = name
        self.tl = TL(name, 1)
        self.ops = []
        self.seen = {}
        self.no_self = no_self


class Prog:
    def __init__(self):
        self.pe = Q("pe", no_self=True)
        self.act = Q("act")
        self.dve = Q("dve")
        self.pool = Q("pool")
        self.sp = Q("sp")
        self.tls = [self.pe.tl, self.act.tl, self.dve.tl, self.pool.tl]
        self.dma_tls = []

    def new_dma_tl(self, name):
        t = TL(name, 16)
        self.tls.append(t)
        self.dma_tls.append(t)
        return t

    def emit(self, q, fn, reads=(), writes=(), tl=None):
        dma = tl is not None
        tl = tl or q.tl
        need = {}

        def req(t, v):
            if v > need.get(t, 0):
                need[t] = v

        for r in reads:
            if r.w:
                req(*r.w)
            if r.excl:
                for t, v in r.r.items():
                    if t is not tl:
                        req(t, v)
        for w in writes:
            if w.w:
                req(*w.w)
            for t, v in w.r.items():
                req(t, v)
        if dma and tl.count > 0:
            req(tl, tl.count)
        waits = []
        for t, v in need.items():
            if t is q.tl and q.no_self and not dma:
                continue
            if q.seen.get(t, 0) < v:
                waits.append((t, v))
                q.seen[t] = v
        tl.count += tl.step
        my = tl.count
        q.ops.append((waits, fn, tl))
        for w in writes:
            w.w = (tl, my)
            w.r = {}
        for r in reads:
            r.r[tl] = my

    def replay(self, q, eng):
        for waits, fn, tl in q.ops:
            for t, v in waits:
                eng.wait_ge(t.sem, v)
            ins = fn(eng)
            ins.then_inc(tl.sem, tl.step)


class Rot:
    def __init__(self, items):
        self.items, self.i = list(items), 0

    def next(self):
        it = self.items[self.i % len(self.items)]
        self.i += 1
        return it


class Arena:
    def __init__(self, n):
        self.n = n
        self.free = [True] * n

    def alloc(self, k=1):
        for s in range(self.n - k + 1):
            if all(self.free[s:s + k]):
                for i in range(s, s + k):
                    self.free[i] = False
                return s
        raise RuntimeError(f"arena exhausted (need {k}, free {sum(self.free)})")

    def release(self, s, k=1):
        for i in range(s, s + k):
            assert not self.free[i]
            self.free[i] = True


def build(nl=L_FULL, taps=(), upto=10 ** 9):
    nc = bass.Bass("TRN2", target_bir_lowering=False)
    nc.dge_precook = False
    P = Prog()
    es = ExitStack()

    def din(name, shape):
        return nc.dram_tensor(name, list(shape), F32, kind="ExternalInput").ap()

    def dout(name, shape):
        return nc.dram_tensor(name, list(shape), F32, kind="ExternalOutput").ap()

    NSL = 135
    wst = din("wst", [nl * NSL, 128, SLOTF])
    xin = din("xin", [128, 8, NTOK])
    cmat_d = din("cmat", [128, NCM, 128])
    urow_d = din("urow", [128, 2, 512])
    rope_d = din("rope", [128, 4, 512])
    hmask_d = din("hmask", [128, 4])
    bd_d = din("bdmask", [128, 256])
    pvec_d = din("pvec", [128, nl, NV])
    bgla_d = din("bgla", [128, nl, 256])
    w2_d = din("w2", [32, nl, 256])
    lamc_d = din("lamc", [128, nl * 4 * 48])
    cond_d = din("condT", [128, 8, 2])
    rmask_d = din("rmask", [128, 8])
    oz_d = din("oz", [128, 2, 512])
    cak_d = din("cakT", [nl, 2, 64, 512])
    cav_d = din("cav", [nl, 512, 128])
    cck_d = din("cckT", [nl, 4, 128, 512])
    ccv_d = din("ccv", [nl, 512, 384])
    sg_d = din("sgla", [nl, 2, 128, 64])

    yT_o = dout("yT", [128, 8, NTOK])
    ak_o = dout("akT", [nl, 2, 64, 1024])
    av_o = dout("av", [nl, 1024, 128])
    ck_o = dout("ckT", [nl, 4, 128, 1024])
    cv_o = dout("cv", [nl, 1024, 384])
    st_o = dout("st", [nl, 4, 2, 128, 64])
    tap_o = {name: dout("tap_" + name, shape) for name, shape in taps}

    XB = [512, 512, 264]
    xch_in_t = [nc.dram_tensor(f"xch_in{b}", [XB[b], 512], F32) for b in range(3)]
    xch_out_t = [nc.dram_tensor(f"xch_out{b}", [4 * XB[b], 512], F32) for b in range(3)]

    def xloc(row):
        if row < 512:
            return 0, row
        if row < 640:
            return 1, row - 512
        if row < 768:
            return 2, row - 640
        if row < 1152:
            return 1, row - 768 + 128
        return 2, row - 1152 + 128

    xin_flat = [t_.ap().rearrange("r c -> (r c)") for t_ in xch_in_t]
    xout_flat = [t_.ap().rearrange("r c -> (r c)") for t_ in xch_out_t]

    def xin_rows(row0, nrows):
        b, lr = xloc(row0)
        return xch_in_t[b].ap()[lr:lr + nrows, :], xin_rs[b]

    def xin_v(row0, nrows, c, nfl=None):
        b, lr = xloc(row0)
        n = nrows * 512 if nfl is None else nfl
        return xin_flat[b][lr * 512:lr * 512 + n].rearrange("(t c) -> t c", c=c), xin_rs[b]

    def xo_multi(row0, nrows):
        b, lr = xloc(row0)
        v = xch_out_t[b].ap().rearrange("(r x) c -> r x c", r=4)
        return v[:, lr:lr + nrows, :].rearrange("r p c -> p r c"), xout_rs[b]

    def xout_v(r, row0, nrows, c, nfl=None):
        b, lr = xloc(row0)
        o = (r * XB[b] + lr) * 512
        n = nrows * 512 if nfl is None else nfl
        return xout_flat[b][o:o + n].rearrange("(t c) -> t c", c=c), xout_rs[b]

    def sb(name, shape):
        return es.enter_context(nc.sbuf_tensor(name, list(shape), F32))

    xT = sb("xT", [128, 8, NTOK])
    ring = sb("ring", [128, NSLOT, SLOTF])
    arena = sb("arena", [128, NA, 512])
    cmat = sb("cmat_s", [128, NCM, 128])
    urow = sb("urow_s", [128, 2, 512])
    rope = sb("rope_s", [128, 4, 512])
    hmask = sb("hmask_s", [128, 4])
    bdm = sb("bd_s", [128, 256])
    pvec = sb("pvec_s", [128, nl, NV])
    bgla = sb("bgla_s", [128, nl, 256])
    w2 = sb("w2_s", [32, nl, 256])
    lamc = sb("lamc_s", [128, nl * 4 * 48])
    cond = sb("cond_s", [128, 8, 2])
    scT = sb("scT_s", [128, 8, 2])
    rmask = sb("rmask_s", [128, 8])
    oz = sb("oz_s", [128, 2, 512])
    modT2 = sb("modT_s", [128, 2, 72, 2])
    gsT2 = sb("gs_s", [128, 2, 3, 8, 2])
    gtT2 = sb("gt_s", [128, 2, 3, 8, 2])
    lam = sb("lam_s", [128, 4, 4])
    lamt = sb("lamt_s", [128, nl * 2 * 48])
    gco = sb("gco_s", [128, 4])
    small = sb("small_s", [128, 16])

    psb = [es.enter_context(nc.psum_tensor(f"ps{i}", [128, 512], F32)) for i in range(8)]

    x_r = [[Reg(f"x{c}_{t}") for t in range(3)] for c in range(8)]
    ring_r = [Reg(f"ring{s}") for s in range(NSLOT)]
    ar_r = [Reg(f"ar{i}") for i in range(NA)]
    ps_r = [Reg(f"ps{i}", excl=True) for i in range(8)]
    const_r = Reg("consts")
    mod_rs = [Reg("mod0"), Reg("mod1")]
    misc_r = Reg("misc")
    small_r = Reg("small")
    xin_rs = [Reg(f"xch_in{b}") for b in range(3)]
    xout_rs = [Reg(f"xch_out{b}") for b in range(3)]

    ar = Arena(NA)

    class Tile:
        def __init__(self, k=1):
            self.k = k
            self.s = ar.alloc(k)
            self.regs = ar_r[self.s:self.s + k]

        def ap(self):
            return arena[:, self.s:self.s + self.k, :].rearrange("p k f -> p (k f)") if self.k > 1 else arena[:, self.s, :]

        def free(self):
            ar.release(self.s, self.k)

    ps_tmp = Rot(range(0, 5))
    ps_acc = Rot(range(5, 8))

    class PS:
        def __init__(self, kind="tmp"):
            self.i = (ps_tmp if kind == "tmp" else ps_acc).next()
            self.regs = [ps_r[self.i]]

        def ap(self):
            return psb[self.i][:]

    slot_tl = [P.new_dma_tl(f"slot{s}") for s in range(NSLOT)]
    misc_tl = Rot([P.new_dma_tl(f"md{i}") for i in range(8)])
    out_tl = Rot([P.new_dma_tl(f"od{i}") for i in range(4)])
    cc_tls = []
    for i in range(nl * 3):
        t_ = TL(f"cc{i}", 1)
        P.tls.append(t_)
        cc_tls.append(t_)

    E = P.emit

    def dma(q, out, in_, reads, writes, tl):
        E(q, lambda e, out=out, in_=in_: e.dma_start(out=out, in_=in_), reads, writes, tl=tl)

    def pool_ld(out, in_, writes, reads=()):
        dma(P.pool, out, in_, list(reads), list(writes), misc_tl.next())

    def sp_ld(out, in_, writes, reads=()):
        dma(P.sp, out, in_, list(reads), list(writes), misc_tl.next())

    def pool_st(out, in_, reads, writes=()):
        dma(P.pool, out, in_, list(reads), list(writes), out_tl.next())

    ring_n = [0]

    def ring_load(idx, nfl=SLOTF):
        s = ring_n[0] % NSLOT
        ring_n[0] += 1
        dma(P.sp, R(ring[:, s, 0:nfl]), R(wst[idx, :, 0:nfl]), [], [ring_r[s]], slot_tl[s])
        return s

    def mm(ps, out_ap, pairs, reads, start=True, stop=True):
        n = len(pairs)

        def fn(e, pairs=pairs, out_ap=out_ap, start=start, stop=stop):
            ins = None
            for i, (lt, rh) in enumerate(pairs):
                ins = e.matmul(out_ap, R(lt), R(rh), start=(start and i == 0), stop=(stop and i == n - 1))
            return ins

        E(P.pe, fn, list(reads), ps.regs)

    def A(out, in_, func, reads, writes, bias=0.0, scale=1.0):
        out = RO(out)
        E(P.act, lambda e: e.activation(out, in_, func, bias=bias, scale=scale), list(reads), list(writes))

    def V_tt(out, in0, in1, op, reads, writes, q=None):
        out = RO(out)
        E(q or P.dve, lambda e: e.tensor_tensor(out, in0, in1, op), list(reads), list(writes))

    def V_ts(out, in0, s1, s2, op0, op1, reads, writes, q=None):
        out = RO(out)
        if op1 is None:
            E(q or P.dve, lambda e: e.tensor_scalar(out, in0, s1, None, op0), list(reads), list(writes))
        else:
            E(q or P.dve, lambda e: e.tensor_scalar(out, in0, s1, s2, op0, op1), list(reads), list(writes))

    def V_stt(out, in0, sc, in1, op0, op1, reads, writes, q=None):
        out = RO(out)
        E(q or P.dve, lambda e: e.scalar_tensor_tensor(out, in0, sc, in1, op0, op1), list(reads), list(writes))

    def V_rcp(out, in_, reads, writes):
        out = RO(out)
        E(P.dve, lambda e: e.reciprocal(out, in_), list(reads), list(writes))

    def V_cp(out, in_, reads, writes, q=None):
        out = RO(out)
        E(q or P.dve, lambda e: e.tensor_copy(out, in_), list(reads), list(writes))

    def fill(tl_, val, nfl=None, q=None):
        n = tl_.k * 512 if nfl is None else nfl
        a = tl_.ap()
        for o in range(0, n, 512):
            w = min(512, n - o)
            V_cp(a[:, o:o + w], oz[:, 1 if val == 1.0 else 0, 0:w], [const_r], [tl_.regs[o // 512]], q=(q or P.pool))

    def CM(i, rows=128, cols=128):
        return cmat[0:rows, i, 0:cols]

    def rsqrt_from(ps_ap, out_ap, reads, writes, tmp_ap, tmp_regs):
        A(tmp_ap, ps_ap, AF.Ln, reads, tmp_regs, bias=EPS)
        A(out_ap, tmp_ap, AF.Exp, tmp_regs, writes, scale=-0.5)

    def tap(name, ap, reads):
        if name in tap_o:
            pool_st(tap_o[name], ap, reads)

    pool_ld(R(cmat[:]), R(cmat_d), [const_r])
    pool_ld(R(urow[:]), R(urow_d), [const_r])
    pool_ld(rope[:], rope_d, [const_r])
    pool_ld(hmask[:], hmask_d, [const_r])
    pool_ld(bdm[:], bd_d, [const_r])
    pool_ld(pvec[:], pvec_d, [const_r])
    pool_ld(bgla[:], bgla_d, [const_r])
    pool_ld(R(w2[:]), R(w2_d), [const_r])
    pool_ld(lamc[:], lamc_d, [const_r])
    pool_ld(cond[:], cond_d, [const_r])
    pool_ld(rmask[:], rmask_d, [const_r])
    pool_ld(oz[:], oz_d, [const_r])
    for t in range(3):
        pool_ld(xT[:, :, t * 512:(t + 1) * 512], xin[:, :, t * 512:(t + 1) * 512], [x_r[c][t] for c in range(8)])

    lc = lamc[:].rearrange("p (l a b d) -> p l a b d", l=nl, a=2, b=2, d=48)
    lt_v = lamt[:].rearrange("p (l a d) -> p l a d", l=nl, a=2, d=48)
    V_tt(lt_v, lc[:, :, :, 0, :], lc[:, :, :, 1, :], ALU.mult, [const_r], [misc_r])
    for l in range(nl):
        E(P.dve, lambda e, l=l: e.tensor_reduce(lam[:, l, 2:4], lt_v[:, l, :, :], AX.X, ALU.add), [misc_r], [misc_r])
        A(lam[:, l, 2:4], lam[:, l, 2:4], AF.Exp, [misc_r], [misc_r])
        V_tt(lam[:, l, 0:1], lam[:, l, 2:3], lam[:, l, 3:4], ALU.subtract, [misc_r], [misc_r])
        li = 0.8 - 0.6 * math.exp(-0.3 * l)
        V_ts(lam[:, l, 0:1], lam[:, l, 0:1], float(li), None, ALU.add, None, [misc_r], [misc_r])
        V_ts(lam[:, l, 1:2], lam[:, l, 0:1], -1.0, None, ALU.mult, None, [misc_r], [misc_r])
        V_ts(gco[:, l:l + 1], pvec[:, l, 101:102], float(1.0 - li), None, ALU.mult, None, [const_r, misc_r], [misc_r])
    A(R(scT[:]), cond[:], AF.Silu, [const_r], [misc_r])

    ada_state = {}

    def ada_begin(l):
        ps = PS("acc")
        ada_state[l] = dict(ps=ps, pv=ps.ap()[:, 0:144].rearrange("p (c j) -> p c j", j=2), k=0)

    def ada_slot(l):
        st_ = ada_state[l]
        sl = st_["k"]
        if sl >= 36:
            return
        st_["k"] += 1
        ps, pv = st_["ps"], st_["pv"]
        s = ring_load(l * NSL + 99 + sl)
        sv = ring[:, s, :].rearrange("p (k n) -> p k n", k=8)
        pt = PS()
        mm(pt, pt.ap()[0:2, 0:256], [(scT[:, kc, :], sv[:, kc, :]) for kc in range(8)], [ring_r[s], misc_r])
        tok = Tile()
        A(tok.ap()[0:2, 0:256], pt.ap()[0:2, 0:256], AF.Copy, pt.regs, tok.regs)
        for half in range(2):
            ch = sl * 2 + half
            mm(ps, pv[:, ch, :], [(tok.ap()[0:2, half * 128:(half + 1) * 128], CM(C_IDENT, 2, 2))], tok.regs + [const_r])
        tok.free()

    def ada_end(l):
        st_ = ada_state[l]
        while st_["k"] < 36:
            ada_slot(l)
        ps, pv = st_["ps"], st_["pv"]
        par = l % 2
        modT, gsT, gtT, mod_r = modT2[:, par], gsT2[:, par], gtT2[:, par], mod_rs[par]
        for j in range(2):
            V_tt(modT[:, :, j], pv[:, :, j], pvec[:, l, 24:96], ALU.add, ps.regs + [const_r], [mod_r])
        for s3 in range(3):
            for j in range(2):
                V_stt(gsT[:, s3, :, j], modT[:, (3 * s3 + 1) * 8:(3 * s3 + 2) * 8, j], 1.0,
                      pvec[:, l, s3 * 8:(s3 + 1) * 8], ALU.add, ALU.mult, [mod_r, const_r], [mod_r])
                V_ts(gtT[:, s3, :, j], modT[:, (3 * s3 + 2) * 8:(3 * s3 + 3) * 8, j],
                     (1.0 if s3 == 1 else 0.5), None, ALU.mult, None, [mod_r], [mod_r])

    def norm_mod(l, t, s3, j, h):
        par = l % 2
        modT, gsT, mod_r = modT2[:, par], gsT2[:, par], mod_rs[par]
        ps = PS()
        sqs = [Tile(), Tile(), Tile()]
        for c in range(8):
            sq = sqs[c % 3]
            xa = xT[:, c, t * 512:(t + 1) * 512]
            V_tt(R(sq.ap()), xa, xa, ALU.mult, [x_r[c][t]], sq.regs, q=P.pool)
            mm(ps, ps.ap(), [(CM(C_ONESM), sq.ap())], sq.regs + [const_r], start=(c == 0), stop=(c == 7))
        tmp = Tile()
        rstd = Tile()
        rsqrt_from(ps.ap(), rstd.ap(), ps.regs, rstd.regs, tmp.ap(), tmp.regs)
        hv = h.ap().rearrange("p (c f) -> p c f", c=8)
        for c in range(8):
            t2 = sqs[c % 3]
            V_stt(t2.ap(), xT[:, c, t * 512:(t + 1) * 512], gsT[:, s3, c, j:j + 1], rstd.ap(), ALU.mult, ALU.mult,
                  [x_r[c][t], mod_r] + rstd.regs, t2.regs)
            A(R(hv[:, c, :]), t2.ap(), AF.Identity, t2.regs + [mod_r], [h.regs[c]], bias=modT[:, 3 * s3 * 8 + c, j:j + 1])
        for tl_ in sqs:
            tl_.free()
        tmp.free()
        rstd.free()

    def ffn_multi(l, tiles, s, hook=None):
        s3 = 0 if s == 0 else 2
        gtT, mod_r = gtT2[:, l % 2], mod_rs[l % 2]
        nt = len(tiles)
        js = [1 if t == 2 else 0 for t in tiles]
        hs = [Tile(8) for _ in tiles]
        for ti, t in enumerate(tiles):
            norm_mod(l, t, s3, js[ti], hs[ti])
        hvs = [h.ap().rearrange("p (c f) -> p c f", c=8) for h in hs]
        acts = [Tile(11) for _ in tiles]
        avs = [a_.ap().rearrange("p (c f) -> p c f", c=11) for a_ in acts]
        base = l * NSL + s * 38
        for half in range(2):
            for fcl in range(11):
                sl = ring_load(base + half * 19 + fcl)
                sv = ring[:, sl, :].rearrange("p (g k n) -> p g k n", g=2, k=8)
                for ti in range(nt):
                    psg, psu = PS(), PS()
                    mm(psg, psg.ap(), [(sv[:, 0, kc, :], hvs[ti][:, kc, :]) for kc in range(8)], [ring_r[sl]] + hs[ti].regs)
                    mm(psu, psu.ap(), [(sv[:, 1, kc, :], hvs[ti][:, kc, :]) for kc in range(8)], [ring_r[sl]] + hs[ti].regs)
                    sg = Tile()
                    A(sg.ap(), psg.ap(), AF.Silu, psg.regs, sg.regs)
                    V_tt(R(avs[ti][:, fcl, :]), sg.ap(), psu.ap(), ALU.mult, sg.regs + psu.regs, [acts[ti].regs[fcl]])
                    sg.free()
                if hook is not None:
                    hook()
            for m in range(8):
                sl = ring_load(base + half * 19 + 11 + m, 1408)
                sv = ring[:, sl, 0:1408].rearrange("p (f n) -> p f n", f=11)
                for ti, t in enumerate(tiles):
                    psy = PS()
                    mm(psy, psy.ap(), [(sv[:, f, :], avs[ti][:, f, :]) for f in range(11)], [ring_r[sl]] + acts[ti].regs)
                    xa = xT[:, m, t * 512:(t + 1) * 512]
                    V_stt(xa, psy.ap(), gtT[:, s3, m, js[ti]:js[ti] + 1], xa, ALU.mult, ALU.add,
                          psy.regs + [mod_r, x_r[m][t]], [x_r[m][t]])
                if hook is not None:
                    hook()
        for tl_ in acts + hs:
            tl_.free()

    def proj_fm(sl_idx, hv, h, nchunks):
        sl = ring_load(sl_idx)
        sv = ring[:, sl, :].rearrange("p (k n) -> p k n", k=8)
        out = []
        for i in range(nchunks):
            ps = PS()
            mm(ps, ps.ap(), [(sv[:, kc, i * 128:(i + 1) * 128], hv[:, kc, :]) for kc in range(8)], [ring_r[sl]] + h.regs)
            out.append(ps)
        return out

    def proj_tm(sl_idx, hv, h, ncols):
        sl = ring_load(sl_idx)
        sv = ring[:, sl, :].rearrange("p (k n) -> p k n", k=8)
        per = 512 // ncols
        res = []
        for g in range(0, 4, per):
            ps = PS()
            pv = ps.ap()[:, 0:per * ncols].rearrange("p (b n) -> p b n", b=per)
            for bi in range(per):
                tb = g + bi
                mm(ps, pv[:, bi, :], [(hv[:, kc, tb * 128:(tb + 1) * 128], sv[:, kc, 0:ncols]) for kc in range(8)],
                   [ring_r[sl]] + h.regs)
            res.append((ps, pv, list(range(g, g + per))))
        return res

    def make_qk_unit(sidx, i, ss, hv, h, blk, gcol_ap, dst, rot):
        st = {}

        def s0():
            if "sl" not in ss:
                ss["sl"] = ring_load(sidx)
            sl = ss["sl"]
            sv = ring[:, sl, :].rearrange("p (k n) -> p k n", k=8)
            ps = PS()
            mm(ps, ps.ap(), [(sv[:, kc, i * 128:(i + 1) * 128], hv[:, kc, :]) for kc in range(8)], [ring_r[sl]] + h.regs)
            st["raw"] = Tile()
            A(st["raw"].ap(), ps.ap(), AF.Copy, ps.regs, st["raw"].regs)

        def s1():
            st["sq"] = Tile()
            V_tt(st["sq"].ap(), st["raw"].ap(), st["raw"].ap(), ALU.mult, st["raw"].regs, st["sq"].regs, q=P.pool)

        def s2():
            st["ps2"] = PS()
            mm(st["ps2"], st["ps2"].ap(), [(CM(blk), st["sq"].ap())], st["sq"].regs + [const_r])

        def s3():
            st["rstd"] = Tile()
            rsqrt_from(st["ps2"].ap(), st["rstd"].ap(), st["ps2"].regs, st["rstd"].regs, st["sq"].ap(), st["sq"].regs)

        def s4():
            raw, rstd = st["raw"], st["rstd"]
            if rot is None:
                V_stt(dst.ap(), raw.ap(), gcol_ap, rstd.ap(), ALU.mult, ALU.mult, raw.regs + rstd.regs + [const_r], dst.regs)
                for k_ in ("raw", "sq", "rstd"):
                    st[k_].free()
            else:
                st["xn"] = Tile()
                V_stt(st["xn"].ap(), raw.ap(), gcol_ap, rstd.ap(), ALU.mult, ALU.mult, raw.regs + rstd.regs + [const_r], st["xn"].regs)

        def s5():
            st["ps3"] = PS()
            mm(st["ps3"], st["ps3"].ap(), [(CM(rot[2]), st["xn"].ap())], st["xn"].regs + [const_r])

        def s6():
            st["t1"] = Tile()
            V_tt(st["t1"].ap(), st["ps3"].ap(), rope[:, rot[1], :], ALU.mult, st["ps3"].regs + [const_r], st["t1"].regs)
            V_tt(st["raw"].ap(), st["xn"].ap(), rope[:, rot[0], :], ALU.mult, st["xn"].regs + [const_r], st["raw"].regs, q=P.pool)

        def s7():
            V_tt(dst.ap(), st["t1"].ap(), st["raw"].ap(), ALU.add, st["t1"].regs + st["raw"].regs, dst.regs)
            for k_ in ("raw", "sq", "rstd", "xn", "t1"):
                st[k_].free()

        return [s0, s1, s2, s3, s4] if rot is None else [s0, s1, s2, s3, s4, s5, s6, s7]

    def qk_project(slot_specs, hv, h, blk, gcols, dsts, rot):
        units = []
        ci = 0
        for (sidx, nch) in slot_specs:
            ss = {}
            for i in range(nch):
                units.append(make_qk_unit(sidx, i, ss, hv, h, blk, gcols[ci], dsts[ci], rot))
                ci += 1
        run_pipeline(units, spacing=2)

    def attn_jobs(jobs, LOOK=3):
        items = []
        for J in jobs:
            nk = len(J["keys"])
            per = 512 // J["ncol"]
            i = 0
            while i < nk:
                g = min(per, nk - i)
                items.append((J, i, g))
                i += g
        pend = []

        def do_pv(ent):
            J, i, g, pT = ent
            nk = len(J["keys"])
            ncol = J["ncol"]
            if i == 0:
                J["acc"] = PS("acc")
            acc = J["acc"]
            for u in range(g):
                vap, vregs = J["vwin"](i + u)
                mm(acc, acc.ap()[0:128, 0:ncol], [(vap, pT.ap()[:, u * ncol:(u + 1) * ncol])], vregs + pT.regs,
                   start=(i + u == 0), stop=(i + u == nk - 1))
            pT.free()
            if i + g == nk:
                J["finish"](acc)

        for (J, i, g) in items:
            ncol, qt, qb, q0 = J["ncol"], J["qt"], J["qbase"], J["q0"]
            ps = PS()
            for u in range(g):
                kap, kregs = J["keys"][i + u]
                mm(ps, ps.ap()[:, u * ncol:(u + 1) * ncol], [(kap, qt.ap()[qb:qb + 64, q0:q0 + ncol])], kregs + qt.regs)
            pT = Tile()
            A(pT.ap()[:, 0:g * ncol], ps.ap()[:, 0:g * ncol], AF.Exp, ps.regs, pT.regs, scale=J["scale"])
            pend.append((J, i, g, pT))
            if len(pend) > LOOK:
                do_pv(pend.pop(0))
        while pend:
            do_pv(pend.pop(0))

    def run_pipeline(units, spacing=2):
        n = len(units)
        ns = max(len(u) for u in units)
        for step in range((n - 1) * spacing + ns):
            for u in range(n):
                st_ = step - u * spacing
                if 0 <= st_ < len(units[u]):
                    units[u][st_]()

    def mixer_A_heads(l, QA, segs, keyf, vwinf, mixA):
        jobs = []
        for (q0, ncol, sid) in segs:
            for hh in range(6):
                ch, par, kv = hh // 2, hh % 2, hh // 3
                base = par * 64

                def finish(acc, ch=ch, par=par, q0=q0, ncol=ncol):
                    nb_, db_ = (0, 64) if par == 0 else (64, 0)
                    rd = Tile()
                    V_rcp(rd.ap()[db_:db_ + 64, 0:ncol], acc.ap()[db_:db_ + 64, 0:ncol], acc.regs, rd.regs)
                    V_tt(R(mixA[ch].ap()[nb_:nb_ + 64, q0:q0 + ncol]), acc.ap()[nb_:nb_ + 64, 0:ncol],
                         rd.ap()[db_:db_ + 64, 0:ncol], ALU.mult, acc.regs + rd.regs, mixA[ch].regs)
                    rd.free()

                jobs.append(dict(qt=QA[ch], qbase=base, ncol=ncol, q0=q0, keys=keyf(sid, kv, base), scale=0.125,
                                 vwin=(lambda i, sid=sid, kv=kv, par=par: vwinf(sid, kv, par, i)), finish=finish))
        attn_jobs(jobs)

    def c_out_norm(l, oc, mixCh):
        sq = Tile()
        V_tt(R(sq.ap()[0:96, :]), oc.ap()[0:96, :], oc.ap()[0:96, :], ALU.mult, oc.regs, sq.regs, q=P.pool)
        ps = PS()
        mm(ps, ps.ap()[0:96, :], [(CM(C_BLK96, 96, 96), sq.ap()[0:96, :])], sq.regs + [const_r])
        rstd = Tile()
        A(sq.ap()[0:96, :], ps.ap()[0:96, :], AF.Ln, ps.regs, sq.regs, bias=EPS)
        A(rstd.ap()[0:96, :], sq.ap()[0:96, :], AF.Exp, sq.regs, rstd.regs, scale=-0.5)
        V_stt(R(mixCh.ap()[0:96, :]), oc.ap()[0:96, :], gco[0:96, l:l + 1], rstd.ap()[0:96, :], ALU.mult, ALU.mult,
              oc.regs + rstd.regs + [misc_r], mixCh.regs)
        sq.free()
        rstd.free()
        oc.free()

    def mixer_C_heads(l, QC, heads, segs, keyf, vwinf, mixC):
        jobs = []
        for hi, hh in enumerate(heads):
            st_h = {}
            for si_, (q0, ncol, sid) in enumerate(segs):
                st_s = {}
                for m in range(2):
                    base = m * 64

                    def finish(acc, m=m, ncol=ncol, q0=q0, st_h=st_h, st_s=st_s, hi=hi, last_seg=(si_ == len(segs) - 1)):
                        if "oc" not in st_h:
                            st_h["oc"] = Tile()
                        oc = st_h["oc"]
                        accs = Tile()
                        A(accs.ap()[:, 0:ncol], acc.ap()[:, 0:ncol], AF.Copy, acc.regs, accs.regs)
                        psd = PS()
                        mm(psd, psd.ap()[0:96, 0:ncol], [(CM(C_SEL32, 128, 96), accs.ap()[:, 0:ncol])], accs.regs + [const_r])
                        rd = Tile()
                        V_rcp(rd.ap()[0:96, 0:ncol], psd.ap()[0:96, 0:ncol], psd.regs, rd.regs)
                        om = Tile()
                        st_s[m] = om
                        V_tt(om.ap()[0:96, 0:ncol], accs.ap()[0:96, 0:ncol], rd.ap()[0:96, 0:ncol], ALU.mult,
                             accs.regs + rd.regs, om.regs)
                        rd.free()
                        accs.free()
                        if m == 1:
                            V_stt(oc.ap()[0:96, q0:q0 + ncol], st_s[1].ap()[0:96, 0:ncol], lam[0:96, l, 1:2],
                                  st_s[0].ap()[0:96, 0:ncol], ALU.mult, ALU.add, st_s[0].regs + st_s[1].regs + [misc_r], oc.regs)
                            st_s[0].free()
                            st_s[1].free()
                            if last_seg:
                                c_out_norm(l, oc, mixC[hi])

                    jobs.append(dict(qt=QC[hi], qbase=base, ncol=ncol, q0=q0, keys=keyf(sid, hh, base), scale=48 ** -0.5,
                                     vwin=(lambda i, sid=sid, hh=hh: vwinf(sid, hh, i)), finish=finish))
        attn_jobs(jobs)

    def gla(l, BQ, BK, BG, KTM, VTM, Vpad, segs, sample, st_dst):
        ktm_v = KTM.ap().rearrange("p (b n) -> p b n", b=4)
        vtm_v = VTM.ap().rearrange("p (b n) -> p b n", b=4)
        vpad_v = Vpad.ap().rearrange("p (b h n) -> p b h n", b=4, h=4)
        Lt = Tile(2)
        Lv = Lt.ap().rearrange("p (b n) -> p b n", b=4)
        for tb in range(4):
            ps = PS()
            mm(ps, ps.ap()[:, 0:256], [(BG.ap()[0:32, tb * 128:(tb + 1) * 128], w2[0:32, l, :])], BG.regs + [const_r])
            zb = Tile()
            V_tt(zb.ap()[:, 0:256], ps.ap()[:, 0:256], bgla[:, l, :], ALU.add, ps.regs + [const_r], zb.regs)
            A(zb.ap()[:, 0:256], zb.ap()[:, 0:256], AF.Exp, zb.regs, zb.regs, scale=-1.0)
            A(R(Lv[:, tb, :]), zb.ap()[:, 0:256], AF.Ln, zb.regs, Lt.regs, bias=1.0)
            zb.free()
        osb = [Tile(), Tile()]
        keep = {}
        for (c0, nb, sid) in segs:
            T = nb * 128
            tb0 = c0 // 128
            mid = T // 2 - 1
            psf, psbk = PS(), PS()
            for jb in range(nb):
                mm(psf, psf.ap()[:, jb * 128:T], [(Lv[:, tb0 + jb, 0:128], urow[:, 0, 0:T - jb * 128])], Lt.regs + [const_r],
                   start=(jb == 0), stop=(jb == nb - 1))
            for idx, jb in enumerate(range(nb - 1, -1, -1)):
                mm(psbk, psbk.ap()[:, 0:(jb + 1) * 128], [(Lv[:, tb0 + jb, 128:256], urow[:, 1, (4 - 1 - jb) * 128:512])],
                   Lt.regs + [const_r], start=(idx == 0), stop=(idx == nb - 1))
            V_cp(small[:, 0:1], psf.ap()[:, mid:mid + 1], psf.regs, [small_r])
            V_ts(small[:, 1:2], psf.ap()[:, mid:mid + 1], -1.0, None, ALU.mult, None, psf.regs, [small_r])
            V_cp(small[:, 2:3], psbk.ap()[:, mid:mid + 1], psbk.regs, [small_r])
            V_ts(small[:, 3:4], psbk.ap()[:, mid:mid + 1], -1.0, None, ALU.mult, None, psbk.regs, [small_r])
            Ef, Enf, Eb, Enb = Tile(), Tile(), Tile(), Tile()
            A(Ef.ap()[:, 0:T], psf.ap()[:, 0:T], AF.Exp, psf.regs + [small_r], Ef.regs, bias=small[:, 1:2], scale=1.0)
            A(Enf.ap()[:, 0:T], psf.ap()[:, 0:T], AF.Exp, psf.regs + [small_r], Enf.regs, bias=small[:, 0:1], scale=-1.0)
            A(Eb.ap()[:, 0:T], psbk.ap()[:, 0:T], AF.Exp, psbk.regs + [small_r], Eb.regs, bias=small[:, 3:4], scale=1.0)
            A(Enb.ap()[:, 0:T], psbk.ap()[:, 0:T], AF.Exp, psbk.regs + [small_r], Enb.regs, bias=small[:, 2:3], scale=-1.0)
            if sample:
                A(small[:, 6:7], psf.ap()[:, T - 1:T], AF.Exp, psf.regs, [small_r])
                A(small[:, 7:8], psbk.ap()[:, 0:1], AF.Exp, psbk.regs, [small_r])
                A(small[:, 4:5], small[:, 0:1], AF.Exp, [small_r], [small_r])
                A(small[:, 5:6], small[:, 2:3], AF.Exp, [small_r], [small_r])
                V_ts(small[:, 4:6], small[:, 4:6], float(32 ** -0.5), None, ALU.mult, None, [small_r], [small_r])
                qinf, qinb = Tile(), Tile()
                V_stt(R(qinf.ap()), Ef.ap(), small[:, 4:5], BQ.ap(), ALU.mult, ALU.mult, Ef.regs + BQ.regs + [small_r], qinf.regs)
                V_stt(R(qinb.ap()), Eb.ap(), small[:, 5:6], BQ.ap(), ALU.mult, ALU.mult, Eb.regs + BQ.regs + [small_r], qinb.regs)
                keep["qinf"], keep["qinb"] = qinf, qinb
                d_ap, d_rg = xin_v(1280, 1, 2, nfl=256)
                pool_st(d_ap, small[:, 6:8], [small_r], [d_rg])
            ktf, ktb = Tile(), Tile()
            V_tt(R(ktf.ap()[:, 0:T]), BK.ap()[:, c0:c0 + T], Enf.ap()[:, 0:T], ALU.mult, BK.regs + Enf.regs, ktf.regs)
            V_tt(R(ktb.ap()[:, 0:T]), BK.ap()[:, c0:c0 + T], Enb.ap()[:, 0:T], ALU.mult, BK.regs + Enb.regs, ktb.regs)
            pso = [PS("acc"), PS("acc")]
            def head_unit(hh):
                st_ = {}

                def s0():
                    qf, qb = Tile(), Tile()
                    st_["qf"], st_["qb"] = qf, qb
                    V_stt(R(qf.ap()[:, 0:T]), Ef.ap()[:, 0:T], hmask[:, hh:hh + 1], BQ.ap()[:, c0:c0 + T], ALU.mult, ALU.mult,
                          Ef.regs + BQ.regs + [const_r], qf.regs)
                    V_stt(R(qb.ap()[:, 0:T]), Eb.ap()[:, 0:T], hmask[:, hh:hh + 1], BQ.ap()[:, c0:c0 + T], ALU.mult, ALU.mult,
                          Eb.regs + BQ.regs + [const_r], qb.regs)

                def s1():
                    qf, qb = st_["qf"], st_["qb"]
                    MT = Tile(nb * T // 512)
                    st_["MT"] = MT
                    Mv = MT.ap().rearrange("p (b n) -> p b n", b=nb)
                    for jb in range(nb):
                        pf, pb = PS(), PS()
                        mm(pf, pf.ap()[:, 0:T - jb * 128], [(ktf.ap()[:, jb * 128:(jb + 1) * 128], qf.ap()[:, jb * 128:T])],
                           ktf.regs + qf.regs)
                        mm(pb, pb.ap()[:, 0:(jb + 1) * 128], [(ktb.ap()[:, jb * 128:(jb + 1) * 128], qb.ap()[:, 0:(jb + 1) * 128])],
                           ktb.regs + qb.regs)
                        if jb > 0:
                            A(R(Mv[:, jb, 0:jb * 128]), pb.ap()[:, 0:jb * 128], AF.Copy, pb.regs, MT.regs)
                        if jb < nb - 1:
                            A(R(Mv[:, jb, (jb + 1) * 128:T]), pf.ap()[:, 128:T - jb * 128], AF.Copy, pf.regs, MT.regs)
                        t1 = Tile()
                        V_tt(t1.ap()[:, 0:128], pf.ap()[:, 0:128], CM(C_TRIU), ALU.mult, pf.regs + [const_r], t1.regs)
                        V_tt(t1.ap()[:, 128:256], pb.ap()[:, jb * 128:(jb + 1) * 128], CM(C_TRIL), ALU.mult, pb.regs + [const_r], t1.regs)
                        V_tt(R(Mv[:, jb, jb * 128:(jb + 1) * 128]), t1.ap()[:, 0:128], t1.ap()[:, 128:256], ALU.add, t1.regs, MT.regs,
                             q=P.pool)
                        t1.free()

                def s2():
                    MT = st_["MT"]
                    Mv = MT.ap().rearrange("p (b n) -> p b n", b=nb)
                    for jb in range(nb):
                        first = (hh % 2 == 0 and jb == 0)
                        last = (hh % 2 == 1 and jb == nb - 1)
                        mm(pso[hh // 2], pso[hh // 2].ap()[:, 0:T], [(vpad_v[:, tb0 + jb, hh, :], Mv[:, jb, :])], Vpad.regs + MT.regs,
                           start=first, stop=last)
                    MT.free()
                    st_["qf"].free()
                    st_["qb"].free()

                return [s0, s1, s2]

            if sample:
                for hh in range(4):
                    for f_ in head_unit(hh):
                        f_()
            else:
                run_pipeline([head_unit(hh) for hh in range(4)], spacing=1)
            for c2 in range(2):
                if not sample:
                    V_cp(osb[c2].ap()[:, c0:c0 + T], pso[c2].ap()[:, 0:T], pso[c2].regs, osb[c2].regs)
                else:
                    V_cp(osb[c2].ap()[:, 0:T], pso[c2].ap()[:, 0:T], pso[c2].regs, osb[c2].regs)
            for d_ in range(2):
                kd = Tile()
                kdv = kd.ap().rearrange("p (b n) -> p b n", b=4)
                for jb in range(nb):
                    ps = PS()
                    prs = []
                    if d_ == 0:
                        for j2 in range(jb, nb):
                            lt = CM(C_SL16) if j2 == jb else urow[:, 0, 128:256]
                            prs.append((lt, Lv[:, tb0 + j2, 0:128]))
                    else:
                        for j2 in range(0, jb + 1):
                            lt = CM(C_SU16) if j2 == jb else urow[:, 0, 128:256]
                            prs.append((lt, Lv[:, tb0 + j2, 128:256]))
                    mm(ps, ps.ap()[:, 0:128], prs, Lt.regs + [const_r])
                    ed = Tile()
                    A(ed.ap()[:, 0:128], ps.ap()[:, 0:128], AF.Exp, ps.regs, ed.regs)
                    V_tt(R(kdv[:, jb, :]), ktm_v[:, tb0 + jb, :], ed.ap()[:, 0:128], ALU.mult, KTM.regs + ed.regs, kd.regs)
                    ed.free()
                pst = PS()
                mm(pst, pst.ap()[:, 0:256], [(kdv[:, jb, :], vtm_v[:, tb0 + jb, :]) for jb in range(nb)], kd.regs + VTM.regs)
                kd.free()
                stt = Tile()
                if not sample:
                    for hh in range(4):
                        A(stt.ap()[32 * hh:32 * hh + 32, 0:64], pst.ap()[32 * hh:32 * hh + 32, 64 * hh:64 * hh + 64], AF.Copy,
                          pst.regs, stt.regs)
                    pool_st(st_dst(sid, d_), stt.ap()[:, 0:64], stt.regs)
                else:
                    A(stt.ap()[:, 0:256], pst.ap()[:, 0:256], AF.Copy, pst.regs, stt.regs)
                    r0 = 1152 + 64 * d_
                    d_ap, d_rg = xin_v(r0, 64, 256)
                    pool_st(d_ap, stt.ap()[:, 0:256], stt.regs, [d_rg])
                stt.free()
            for tl_ in (Ef, Enf, Eb, Enb, ktf, ktb):
                tl_.free()
        Lt.free()
        return osb, keep

    def gla_out(l, osb, GR, mixB):
        for c2 in range(2):
            sq = Tile()
            V_tt(R(sq.ap()), osb[c2].ap(), osb[c2].ap(), ALU.mult, osb[c2].regs, sq.regs, q=P.pool)
            ps = PS()
            mm(ps, ps.ap(), [(CM(C_BLK64), sq.ap())], sq.regs + [const_r])
            rstd = Tile()
            rsqrt_from(ps.ap(), rstd.ap(), ps.regs, rstd.regs, sq.ap(), sq.regs)
            V_stt(sq.ap(), osb[c2].ap(), pvec[:, l, 100:101], rstd.ap(), ALU.mult, ALU.mult, osb[c2].regs + rstd.regs + [const_r], sq.regs)
            V_tt(R(mixB[c2].ap()), sq.ap(), GR[c2].ap(), ALU.mult, sq.regs + GR[c2].regs, mixB[c2].regs)
            sq.free()
            rstd.free()

    def w_out(l, t, j, mix):
        gtT, mod_r = gtT2[:, l % 2], mod_rs[l % 2]
        base = l * NSL + 91
        for m in range(8):
            sl = ring_load(base + m, 1152)
            sv = ring[:, sl, 0:1152].rearrange("p (c n) -> p c n", c=9)
            psy = PS()
            prs, rds = [], [ring_r[sl]]
            for c in range(9):
                rows = 128 if c < 5 else 96
                prs.append((sv[0:rows, c, :], mix[c].ap()[0:rows, :]))
                rds += mix[c].regs
            mm(psy, psy.ap(), prs, rds)
            xa = xT[:, m, t * 512:(t + 1) * 512]
            V_stt(xa, psy.ap(), gtT[:, 1, m, j:j + 1], xa, ALU.mult, ALU.add, psy.regs + [mod_r, x_r[m][t]], [x_r[m][t]])

    import os as _os
    stopat = _os.environ.get("STOPAT", "")

    class _Stop(Exception):
        pass

    def chk(name):
        if stopat == name:
            raise _Stop()

    def mixer(l, t):
        try:
            mixer_(l, t)
        except _Stop:
            pass

    def mixer_(l, t):
        sample = (t == 2)
        j = 1 if sample else 0
        wb = l * NSL + 76
        h = Tile(8)
        norm_mod(l, t, 1, j, h)
        hv = h.ap().rearrange("p (c f) -> p c f", c=8)
        mix = [Tile() for _ in range(9)] if not sample else None
        tcol = t * 512
        QA = [Tile() for _ in range(3)]
        KA = [Tile() for _ in range(2)]
        dsts = QA + KA
        rotA = (0, 1, C_PSWA) if sample else None
        qk_project([(wb + 0, 2), (wb + 1, 2), (wb + 2, 1)], hv, h, C_BLK64,
                   [pvec[:, l, 96:97]] * 3 + [pvec[:, l, 97:98]] * 2, dsts, rotA)
        (ps, pv, tbs), = proj_tm(wb + 3, hv, h, 128)
        avt = Tile()
        V_cp(avt.ap(), ps.ap(), ps.regs, avt.regs)
        if not sample:
            pool_st(av_o[l, tcol:tcol + 512, :].rearrange("(b p) c -> p b c", p=128), avt.ap().rearrange("p (b c) -> p b c", b=4), avt.regs)
            for kv in range(2):
                pool_st(ak_o[l, kv, :, tcol:tcol + 512], KA[kv].ap()[0:64, :], KA[kv].regs)
        else:
            for kv in range(2):
                d_ap, d_rg = xin_rows(kv * 64, 64)
                pool_st(d_ap, KA[kv].ap()[0:64, :], KA[kv].regs, [d_rg])
            d_ap, d_rg = xin_v(640, 128, 128)
            pool_st(d_ap.rearrange("(b p) c -> p b c", p=128),
                    avt.ap().rearrange("p (b c) -> p b c", b=4), avt.regs, [d_rg])
            for tl_ in KA:
                tl_.free()
        chk("Aproj")
        VAkv = []
        if not sample:
            VAo = Tile(2)
            vao_v = VAo.ap()[:, 0:768].rearrange("p (b n) -> p b n", b=4)
            fill(VAo, 1.0)
            VAo2 = Tile(2)
            fill(VAo2, 1.0)
            vao2_v = VAo2.ap()[:, 0:768].rearrange("p (b n) -> p b n", b=4)
            avv = avt.ap().rearrange("p (b c) -> p b c", b=4)
            V_cp(R(vao_v[:, :, 64:128]), avv[:, :, 0:64], avt.regs, VAo.regs, q=P.pool)
            V_cp(R(vao2_v[:, :, 64:128]), avv[:, :, 64:128], avt.regs, VAo2.regs, q=P.pool)
            VAkv = [(VAo, vao_v), (VAo2, vao2_v)]

            def keyfA(sid, kv, base):
                return [(KA[kv].ap()[base:base + 64, sid * 256 + kc * 128: sid * 256 + (kc + 1) * 128], KA[kv].regs) for kc in range(2)]

            def vwinA(sid, kv, par, i):
                tl_, vv = VAkv[kv]
                off = 64 if par == 0 else 0
                return (vv[:, sid * 2 + i, off:off + 128], tl_.regs)

            mixer_A_heads(l, QA, [(0, 256, 0), (256, 256, 1)], keyfA, vwinA, mix[0:3])
            VAo2.free()
            VAo.free()
            for tl_ in QA + KA:
                tl_.free()
        avt.free()
        chk("A")
        QC = [Tile() for _ in range(4)]
        KC = [Tile() for _ in range(4)]
        dsts = QC + KC
        rotC = (2, 3, C_PSWC) if sample else None
        qk_project([(wb + 4 + si, 2) for si in range(4)], hv, h, C_BLK48,
                   [pvec[:, l, 98:99]] * 4 + [pvec[:, l, 99:100]] * 4, dsts, rotC)
        cvt = Tile(3)
        cvv = cvt.ap().rearrange("p (b c) -> p b c", b=4)
        for (ps, pv, tbs) in proj_tm(wb + 8, hv, h, 256):
            V_cp(cvv[:, tbs[0]:tbs[0] + 2, 0:256], pv, ps.regs, cvt.regs)
        for (ps, pv, tbs) in proj_tm(wb + 9, hv, h, 128):
            V_cp(cvv[:, :, 256:384], pv, ps.regs, cvt.regs)
        if not sample:
            pool_st(cv_o[l, tcol:tcol + 512, :].rearrange("(b p) c -> p b c", p=128), cvv, cvt.regs)
            for hh in range(4):
                pool_st(ck_o[l, hh, :, tcol:tcol + 512], KC[hh].ap(), KC[hh].regs)
            VCo = [Tile() for _ in range(4)]
            for hh in range(4):
                vv = VCo[hh].ap().rearrange("p (b n) -> p b n", b=4)
                fill(VCo[hh], 1.0)
                V_cp(R(vv[:, :, 0:96]), cvv[:, :, hh * 96:(hh + 1) * 96], cvt.regs, VCo[hh].regs, q=P.pool)

            def keyfC(sid, hh, base):
                return [(KC[hh].ap()[base:base + 64, sid * 256 + kc * 128: sid * 256 + (kc + 1) * 128], KC[hh].regs) for kc in range(2)]

            def vwinC(sid, hh, i):
                vv = VCo[hh].ap().rearrange("p (b n) -> p b n", b=4)
                return (vv[:, sid * 2 + i, :], VCo[hh].regs)

            mixer_C_heads(l, QC, [0, 1, 2, 3], [(0, 256, 0), (256, 256, 1)], keyfC, vwinC, mix[5:9])
            for tl_ in VCo + QC + KC:
                tl_.free()
        else:
            for hh in range(4):
                d_ap, d_rg = xin_rows(128 + hh * 128, 128)
                pool_st(d_ap, KC[hh].ap(), KC[hh].regs, [d_rg])
            d_ap, d_rg = xin_v(768, 384, 384)
            pool_st(d_ap.rearrange("(b p) c -> p b c", p=128), cvv, cvt.regs, [d_rg])
            for b_ in range(2):
                E(P.pool, lambda e, b_=b_: e.collective_compute("AllGather", ALU.bypass, replica_groups=[[0, 1, 2, 3], [4, 5, 6, 7]],
                                                               ins=[xch_in_t[b_].ap().opt()], outs=[xch_out_t[b_].ap().opt()]),
                  [xin_rs[b_]], [xout_rs[b_]], tl=cc_tls[l * 3 + b_])
            for tl_ in KC:
                tl_.free()
        cvt.free()
        chk("C")
        BQ, BK, BR0, BR1, BG = Tile(), Tile(), Tile(), Tile(), Tile()
        raws = [BQ, BK, BR0, BR1, BG]
        ci = 0
        for si, nch in enumerate((2, 2, 1)):
            pss = proj_fm(wb + 10 + si, hv, h, nch)
            for ps in pss:
                dst = raws[ci]
                if ci in (2, 3):
                    A(dst.ap(), ps.ap(), AF.Silu, ps.regs, dst.regs)
                elif ci == 4:
                    A(R(dst.ap()[0:32, :]), ps.ap()[0:32, :], AF.Copy, ps.regs, dst.regs)
                else:
                    A(dst.ap(), ps.ap(), AF.Copy, ps.regs, dst.regs)
                ci += 1
        chk("Braw")
        KTM, VTM, Vpad = Tile(), Tile(2), Tile(4)
        ktm_v = KTM.ap().rearrange("p (b n) -> p b n", b=4)
        vtm_v = VTM.ap().rearrange("p (b n) -> p b n", b=4)
        vpad_v = Vpad.ap().rearrange("p (b h n) -> p b h n", b=4, h=4)
        if _os.environ.get("SKIP", "") != "fill":
            fill(Vpad, 0.0)
        for (ps, pv, tbs) in proj_tm(wb + 13, hv, h, 256):
            b0 = tbs[0]
            for bi in range(2):
                A(R(ktm_v[:, b0 + bi, :]), pv[:, bi, 0:128], AF.Copy, ps.regs, KTM.regs)
                A(R(vtm_v[:, b0 + bi, 0:128]), pv[:, bi, 128:256], AF.Copy, ps.regs, VTM.regs)

        chk("Btm1")
        for (ps, pv, tbs) in proj_tm(wb + 14, hv, h, 128):
            for bi in range(4):
                A(R(vtm_v[:, bi, 128:256]), pv[:, bi, :], AF.Copy, ps.regs, VTM.regs)
        for hh in range(4):
            o_ = (hh % 2) * 64
            V_cp(R(vpad_v[:, :, hh, o_:o_ + 64]), vtm_v[:, :, hh * 64:(hh + 1) * 64], VTM.regs, Vpad.regs, q=P.pool)
        h.free()
        chk("Bproj")
        if not sample:
            segs = [(0, 2, 0), (256, 2, 1)]
            osb, _ = gla(l, BQ, BK, BG, KTM, VTM, Vpad, segs, False,
                         lambda sid, d_: st_o[l, t * 2 + sid, d_, :, :])
            for tl_ in (BQ, BK, BG, KTM, VTM, Vpad):
                tl_.free()
            chk("gla")
            gla_out(l, osb, [BR0, BR1], mix[3:5])
            for tl_ in osb + [BR0, BR1]:
                tl_.free()
            chk("glaout")
            w_out(l, t, j, mix)
            for tl_ in mix:
                tl_.free()
            return
        osb, keep = gla(l, BQ, BK, BG, KTM, VTM, Vpad, [(0, 4, 0)], True, None)
        for tl_ in (BQ, BK, BG, KTM, VTM, Vpad):
            tl_.free()
        for b_ in range(2, 3):
            E(P.pool, lambda e, b_=b_: e.collective_compute("AllGather", ALU.bypass, replica_groups=[[0, 1, 2, 3], [4, 5, 6, 7]],
                                                           ins=[xch_in_t[b_].ap().opt()], outs=[xch_out_t[b_].ap().opt()]),
              [xin_rs[b_]], [xout_rs[b_]], tl=cc_tls[l * 3 + b_])
        mix = [Tile() for _ in range(9)]
        def gla_post():
            Sin = []
            for d_ in range(2):
                S = Tile()
                fill(S, 0.0, 256)
                for hh in range(4):
                    pool_ld(R(S.ap()[32 * hh:32 * hh + 32, 64 * hh:64 * hh + 64]), R(sg_d[l, d_, 32 * hh:32 * hh + 32, :]), S.regs)
                SL = Tile(2)
                slv = SL.ap().rearrange("p (r c) -> p r c", r=4)
                r0 = 1152 + 64 * d_
                for r in range(4):
                    s_ap, s_rg = xout_v(r, r0, 64, 256)
                    pool_ld(R(slv[:, r, :]), R(s_ap), SL.regs, [s_rg])
                AD = Tile()
                adv = AD.ap()[:, 0:8].rearrange("p (r c) -> p r c", r=4)
                for r in range(4):
                    s_ap, s_rg = xout_v(r, 1280, 1, 2, nfl=256)
                    pool_ld(R(adv[:, r, :]), R(s_ap), AD.regs, [s_rg])
                V_ts(AD.ap()[:, 8:16], AD.ap()[:, 0:8], -1.0, None, ALU.add, None, AD.regs, AD.regs)
                am1 = AD.ap()[:, 8:16].rearrange("p (r c) -> p r c", r=4)
                order = range(4) if d_ == 0 else range(3, -1, -1)
                tmp = Tile()
                for r in order:
                    V_stt(tmp.ap()[:, 0:256], S.ap()[:, 0:256], am1[:, r, d_:d_ + 1], slv[:, r, :], ALU.mult, ALU.add,
                          S.regs + AD.regs + SL.regs, tmp.regs)
                    V_stt(S.ap()[:, 0:256], tmp.ap()[:, 0:256], rmask[:, d_ * 4 + r:d_ * 4 + r + 1], S.ap()[:, 0:256], ALU.mult, ALU.add,
                          tmp.regs + S.regs + [const_r], S.regs)
                V_tt(R(tmp.ap()[:, 0:256]), S.ap()[:, 0:256], bdm[:], ALU.mult, S.regs + [const_r], tmp.regs)
                Sin.append(tmp)
                S.free()
                SL.free()
                AD.free()
            qin = [keep["qinf"], keep["qinb"]]
            for c2 in range(2):
                ps = PS("acc")
                mm(ps, ps.ap(), [(Sin[d_].ap()[:, c2 * 128:(c2 + 1) * 128], qin[d_].ap()) for d_ in range(2)],
                   Sin[0].regs + Sin[1].regs + qin[0].regs + qin[1].regs)
                V_tt(osb[c2].ap(), osb[c2].ap(), ps.ap(), ALU.add, osb[c2].regs + ps.regs, osb[c2].regs)
            for tl_ in Sin + qin:
                tl_.free()
            gla_out(l, osb, [BR0, BR1], mix[3:5])
            for tl_ in osb + [BR0, BR1]:
                tl_.free()

        VAll = Tile(8)
        vav = VAll.ap()[:, 0:3840].rearrange("p (k n) -> p k n", k=20)
        fill(VAll, 1.0, q=P.dve)
        for kv in range(2):
            KAll = Tile(5)
            for half in range(2):
                s_ap, s_rg = xo_multi(kv * 64, 64)
                sp_ld(R(KAll.ap()[half * 64:(half + 1) * 64, 0:2048].rearrange("p (r c) -> p r c", r=4)),
                      R(s_ap), KAll.regs, [s_rg])
                sp_ld(R(KAll.ap()[half * 64:(half + 1) * 64, 2048:2560]), R(cak_d[l, kv, :, :]), KAll.regs)
            for r in range(4):
                s_ap, s_rg = xout_v(r, 640, 128, 128)
                src = s_ap.rearrange("(b p) c -> p b c", p=128)
                sp_ld(R(vav[:, r * 4:(r + 1) * 4, 64:128]), R(src[:, :, kv * 64:(kv + 1) * 64]), VAll.regs, [s_rg])
            sp_ld(R(vav[:, 16:20, 64:128]), R(cav_d[l, :, kv * 64:(kv + 1) * 64].rearrange("(b p) c -> p b c", p=128)), VAll.regs)
            jobs = []
            for g in range(3):
                hh = kv * 3 + g
                ch, par = hh // 2, hh % 2
                base = par * 64
                keys = [(KAll.ap()[base:base + 64, kc * 128:(kc + 1) * 128], KAll.regs) for kc in range(20)]

                def finish(acc, ch=ch, par=par):
                    nb_, db_ = (0, 64) if par == 0 else (64, 0)
                    rd = Tile()
                    V_rcp(rd.ap()[db_:db_ + 64, :], acc.ap()[db_:db_ + 64, :], acc.regs, rd.regs)
                    V_tt(R(mix[ch].ap()[nb_:nb_ + 64, :]), acc.ap()[nb_:nb_ + 64, :], rd.ap()[db_:db_ + 64, :], ALU.mult,
                         acc.regs + rd.regs, mix[ch].regs)
                    rd.free()

                off = 64 if par == 0 else 0
                jobs.append(dict(qt=QA[ch], qbase=base, ncol=512, q0=0, keys=keys, scale=0.125,
                                 vwin=(lambda i, off=off: (vav[:, i, off:off + 128], VAll.regs)), finish=finish))
            attn_jobs(jobs)
            KAll.free()
        VAll.free()
        for tl_ in QA:
            tl_.free()
        gla_post()
        KCh = Tile(5)
        VCh = Tile(5)
        vcv = VCh.ap().rearrange("p (k n) -> p k n", k=20)

        def keyfC2(sid, hh, base):
            return [(KCh.ap()[base:base + 64, kc * 128:(kc + 1) * 128], KCh.regs) for kc in range(20)]

        def vwinC2(sid, hh, i):
            return (vcv[:, i, :], VCh.regs)

        fill(VCh, 1.0, q=P.dve)
        for hh in range(4):
            s_ap, s_rg = xo_multi(128 + hh * 128, 128)
            sp_ld(R(KCh.ap()[:, 0:2048].rearrange("p (r c) -> p r c", r=4)), R(s_ap), KCh.regs, [s_rg])
            sp_ld(R(KCh.ap()[:, 2048:2560]), R(cck_d[l, hh, :, :]), KCh.regs)
            for r in range(4):
                s_ap, s_rg = xout_v(r, 768, 384, 384)
                src = s_ap.rearrange("(b p) c -> p b c", p=128)
                sp_ld(R(vcv[:, r * 4:(r + 1) * 4, 0:96]), R(src[:, :, hh * 96:(hh + 1) * 96]), VCh.regs, [s_rg])
            sp_ld(R(vcv[:, 16:20, 0:96]), R(ccv_d[l, :, hh * 96:(hh + 1) * 96].rearrange("(b p) c -> p b c", p=128)), VCh.regs)
            mixer_C_heads(l, [QC[hh]], [hh], [(0, 512, 0)], keyfC2, vwinC2, [mix[5 + hh]])
        KCh.free()
        VCh.free()
        for tl_ in QC:
            tl_.free()
        w_out(l, t, j, mix)
        for tl_ in mix:
            tl_.free()

    step = [0]

    def go():
        step[0] += 1
        return step[0] <= upto

    for l in range(nl):
        if l == 0:
            ada_begin(0)
            ada_end(0)
        if go():
            ffn_multi(l, [0, 1], 0)
        if go():
            mixer(l, 0)
        if go():
            mixer(l, 1)
        if go():
            if l + 1 < nl:
                ada_begin(l + 1)
                ffn_multi(l, [0, 1], 1, hook=lambda l=l: ada_slot(l + 1))
                ada_end(l + 1)
            else:
                ffn_multi(l, [0, 1], 1)
        if go():
            ffn_multi(l, [2], 0)
        if go():
            mixer(l, 2)
        if go():
            ffn_multi(l, [2], 1)
    for t in range(3):
        pool_st(yT_o[:, :, t * 512:(t + 1) * 512], xT[:, :, t * 512:(t + 1) * 512], [x_r[c][t] for c in range(8)])

    for t in P.tls:
        t.sem = es.enter_context(nc.semaphore(t.name))
    final_waits = [(t, t.count) for t in P.dma_tls if t.count > 0]
    with nc.allow_low_precision("float32r PE operands"), nc.Block() as block:
        @block.sync
        def _(e):
            P.replay(P.sp, e)

        @block.tensor
        def _(e):
            P.replay(P.pe, e)

        @block.scalar
        def _(e):
            P.replay(P.act, e)

        @block.vector
        def _(e):
            P.replay(P.dve, e)

        @block.gpsimd
        def _(e):
            P.replay(P.pool, e)
            for t, v in final_waits:
                e.wait_ge(t.sem, v)
    es.close()
    return nc


OFF = dict(a_q=0, a_k=384, a_v=512, b_q=640, b_k=768, b_v=896, b_g=1152, b_r=1184, c_q=1440, c_k=1824, c_v=2208)


def _w_in_cols():
    def pad(lst, n=256):
        return list(lst) + [-1] * (n - len(lst))

    def rng(a, n):
        return list(range(a, a + n))

    def cmap(base, hh):
        out = []
        for m in range(2):
            out += rng(base + hh * 96 + m * 48, 48) + [-1] * 16
        return out

    slots = []
    slots.append(rng(OFF["a_q"], 256))
    slots.append(rng(OFF["a_q"] + 256, 128) + rng(OFF["a_k"], 64) * 2)
    slots.append(pad(rng(OFF["a_k"] + 64, 64) * 2))
    slots.append(pad(rng(OFF["a_v"], 128)))
    for base in (OFF["c_q"], OFF["c_k"]):
        slots.append(cmap(base, 0) + cmap(base, 1))
        slots.append(cmap(base, 2) + cmap(base, 3))
    slots.append(rng(OFF["c_v"], 256))
    slots.append(pad(rng(OFF["c_v"] + 256, 128)))
    slots.append(rng(OFF["b_q"], 128) + rng(OFF["b_k"], 128))
    slots.append(rng(OFF["b_r"], 256))
    slots.append(pad(rng(OFF["b_g"], 32)))
    slots.append(rng(OFF["b_k"], 128) + rng(OFF["b_v"], 128))
    slots.append(pad(rng(OFF["b_v"] + 128, 128)))
    assert len(slots) == 15
    return np.array(slots, dtype=np.int64)


def pack_weights(inp, nl):
    NSL = 135
    wst = np.zeros((nl * NSL, 128, SLOTF), np.float32)
    cols = _w_in_cols()
    for l in range(nl):
        b = l * NSL
        for s in range(2):
            wg = inp["w_ffn_gate"][l, s].reshape(8, 128, 22, 128).transpose(2, 1, 0, 3)
            wu = inp["w_ffn_up"][l, s].reshape(8, 128, 22, 128).transpose(2, 1, 0, 3)
            gu = np.stack([wg, wu], axis=2).reshape(22, 128, SLOTF)
            wd = inp["w_ffn_down"][l, s].reshape(2, 11, 128, 8, 128).transpose(0, 3, 2, 1, 4).reshape(2, 8, 128, 1408)
            for half in range(2):
                o = b + s * 38 + half * 19
                wst[o:o + 11] = gu[half * 11:(half + 1) * 11]
                wst[o + 11:o + 19, :, 0:1408] = wd[half]
        wi = np.concatenate([inp["w_in"][l], np.zeros((D, 1), np.float32)], axis=1)
        for si in range(15):
            wc = wi[:, cols[si]]
            wst[b + 76 + si] = wc.reshape(8, 128, 256).transpose(1, 0, 2).reshape(128, SLOTF)
        wo = inp["w_out"][l]
        wpad = np.zeros((9, 128, D), np.float32)
        for c in range(5):
            wpad[c] = wo[c * 128:(c + 1) * 128]
        for c in range(4):
            wpad[5 + c, 0:96] = wo[640 + c * 96:640 + (c + 1) * 96]
        wst[b + 91:b + 99, :, 0:1152] = wpad.reshape(9, 128, 8, 128).transpose(2, 1, 0, 3).reshape(8, 128, 1152)
        wa = inp["w_ada"][l].reshape(8, 128, 36, 256).transpose(2, 1, 0, 3).reshape(36, 128, SLOTF)
        wst[b + 99:b + 135] = wa
    return wst


def make_consts():
    cm = np.zeros((128, NCM, 128), np.float32)
    p = np.arange(128)
    cm[:, C_ONESM, :] = 1.0 / D
    cm[:, C_BLK64, :] = (p[:, None] // 64 == p[None, :] // 64) / 64.0
    real48 = (p % 64) < 48
    cm[:, C_BLK48, :] = ((p[:, None] // 64 == p[None, :] // 64) & real48[:, None] & real48[None, :]) / 48.0
    cm[0:96, C_BLK96, 0:96] = 1.0 / 96.0
    cm[96:128, C_SEL32, 0:96] = 1.0 / 32.0
    permA = np.zeros(128, np.int64)
    for m in range(128):
        d = m % 64
        permA[m] = m + 16 if (d % 32) < 16 else m - 16
    cm[permA, C_PSWA, p] = 1.0
    permC = np.arange(128)
    for m in range(128):
        d = m % 64
        if d < 48:
            permC[m] = m + 12 if (d % 24) < 12 else m - 12
    for m in range(128):
        if (m % 64) < 48:
            cm[permC[m], C_PSWC, m] = 1.0
    cm[:, C_IDENT, :] = np.eye(128, dtype=np.float32)
    cm[:, C_TRIU, :] = (p[:, None] <= p[None, :])
    cm[:, C_TRIL, :] = (p[:, None] >= p[None, :])
    cm[:, C_SL16, :] = (p[:, None] > p[None, :]).astype(np.float32) / -16.0
    cm[:, C_SU16, :] = (p[:, None] < p[None, :]).astype(np.float32) / -16.0
    ur = np.zeros((128, 2, 512), np.float32)
    ur[:, 0, :] = -1.0 / 16.0
    ur[:, 0, 0:128] = (p[:, None] <= p[None, :]).astype(np.float32) / -16.0
    ur[:, 1, :] = -1.0 / 16.0
    ur[:, 1, 384:512] = (p[:, None] >= p[None, :]).astype(np.float32) / -16.0
    hm = np.zeros((128, 4), np.float32)
    for hh in range(4):
        hm[32 * hh:32 * hh + 32, hh] = 32 ** -0.5
    bd = np.zeros((128, 256), np.float32)
    for hh in range(4):
        bd[32 * hh:32 * hh + 32, 64 * hh:64 * hh + 64] = 1.0
    return cm, ur, hm, bd


def rope_tables(tok0):
    t = np.arange(tok0, tok0 + 512)
    row = (t // 64).astype(np.float32)
    col = (t % 64).astype(np.float32)
    out = np.zeros((128, 4, 512), np.float32)
    out[:, 0, :] = 1.0
    out[:, 2, :] = 1.0
    for p in range(128):
        d = p % 64
        half, dd = d // 32, d % 32
        i = dd % 16
        f = np.float32(10000.0) ** (-np.float32(i) / np.float32(16))
        ang = (row if half == 0 else col) * np.float32(f)
        out[p, 0] = np.cos(ang)
        out[p, 1] = (-np.sin(ang)) if dd < 16 else np.sin(ang)
        if d < 48:
            half, dd = d // 24, d % 24
            i = dd % 12
            f = np.float32(10000.0) ** (-np.float32(i) / np.float32(12))
            ang = (row if half == 0 else col) * np.float32(f)
            out[p, 2] = np.cos(ang)
            out[p, 3] = (-np.sin(ang)) if dd < 12 else np.sin(ang)
        else:
            out[p, 2] = 1.0
            out[p, 3] = 0.0
    return out


def pack_pvec(inp, nl):
    pv = np.zeros((128, nl, NV), np.float32)
    p = np.arange(128)
    for l in range(nl):
        pv[:, l, 0:24] = inp["g_norm"][l].reshape(3, 8, 128).transpose(2, 0, 1).reshape(128, 24)
        pv[:, l, 24:96] = inp["b_ada"][l].reshape(72, 128).T
        pv[:, l, 96] = inp["g_a_q"][l][p % 64]
        pv[:, l, 97] = inp["g_a_k"][l][p % 64]
        for col, key in ((98, "g_c_q"), (99, "g_c_k")):
            g = np.zeros((2, 64), np.float32)
            g[:, 0:48] = inp[key][l]
            pv[:, l, col] = g.reshape(128)
        pv[:, l, 100] = inp["g_gla"][l][p % 64]
        pv[0:96, l, 101] = inp["g_c_out"][l]
    return pv


_NC_CACHE = {}


def kernel(**inp):
    return run(inp, L_FULL)


def run(inp, nl, trace=False):
    inp = {k: np.asarray(v) for k, v in inp.items()}
    if nl not in _NC_CACHE:
        _NC_CACHE[nl] = build(nl)
    nc = _NC_CACHE[nl]
    wst = pack_weights(inp, nl)
    cm, ur, hm, bd = make_consts()
    pv = pack_pvec(inp, nl)
    bgla = np.broadcast_to(inp["b_gla"][:nl].reshape(1, nl, 256), (128, nl, 256)).copy()
    w2 = np.zeros((32, nl, 256), np.float32)
    for l in range(nl):
        w2[0:16, l, 0:128] = inp["w_gla_up"][l, 0]
        w2[16:32, l, 128:256] = inp["w_gla_up"][l, 1]
    lamc = np.broadcast_to(inp["lam_c"][:nl].reshape(1, nl * 4 * 48), (128, nl * 4 * 48)).copy()
    ozc = np.zeros((128, 2, 512), np.float32)
    ozc[:, 1, :] = 1.0
    in_maps = []
    for c in range(8):
        b, r = c // 4, c % 4
        xp = inp["x_prompt"][4 * c:4 * c + 4].reshape(1024, D)
        xs = inp["x_sample"][b, r * 512:(r + 1) * 512]
        xt = np.concatenate([xp, xs], axis=0)
        xin = xt.reshape(NTOK, 8, 128).transpose(2, 1, 0).copy()
        cond = np.stack([inp["c_ctx"], inp["c"][b]], axis=1).reshape(8, 128, 2).transpose(1, 0, 2).copy()
        rm = np.zeros((128, 8), np.float32)
        for rr in range(4):
            rm[:, rr] = 1.0 if rr < r else 0.0
            rm[:, 4 + rr] = 1.0 if rr > r else 0.0
        cak = inp["cache_a_k"][b, :nl].transpose(0, 2, 3, 1).copy()
        cav = inp["cache_a_v"][b, :nl].reshape(nl, 512, 128).copy()
        ck = inp["cache_c_k"][b, :nl].reshape(nl, 512, 4, 2, 48)
        cck = np.zeros((nl, 4, 2, 64, 512), np.float32)
        cck[:, :, :, 0:48, :] = ck.transpose(0, 2, 3, 4, 1)
        cck = cck.reshape(nl, 4, 128, 512)
        ccv = inp["cache_c_v"][b, :nl].reshape(nl, 512, 384).copy()
        sg = inp["state_gla"][b, :nl].reshape(nl, 2, 128, 64).copy()
        in_maps.append(dict(wst=wst, xin=xin, cmat=cm, urow=ur, rope=rope_tables(r * 512), hmask=hm, bdmask=bd, pvec=pv, oz=ozc,
                            bgla=bgla, w2=w2, lamc=lamc, condT=cond, rmask=rm, cakT=cak, cav=cav, cckT=cck, ccv=ccv, sgla=sg))
    if trace:
        res = run_bass_kernel_spmd(nc, in_maps, core_ids=list(range(8)), trace=True)
        print("exec_time_ns", res.exec_time_ns)
    else:
        res = run_bass_kernel_spmd(nc, in_maps, core_ids=list(range(8)))
    return assemble(res.results, nl)


def assemble(results, nl):
    y_prompt = np.zeros((32, 256, D), np.float32)
    y_sample = np.zeros((2, 2048, D), np.float32)
    n_ak = np.zeros((32, nl, 256, 2, 64), np.float32)
    n_av = np.zeros((32, nl, 256, 2, 64), np.float32)
    n_ck = np.zeros((32, nl, 256, 4, 96), np.float32)
    n_cv = np.zeros((32, nl, 256, 4, 96), np.float32)
    n_st = np.zeros((32, nl, 2, 4, 32, 64), np.float32)
    for c in range(8):
        r = results[c]
        b, rk = c // 4, c % 4
        y = np.asarray(r["yT"]).transpose(2, 1, 0).reshape(NTOK, D)
        y_prompt[4 * c:4 * c + 4] = y[0:1024].reshape(4, 256, D)
        y_sample[b, rk * 512:(rk + 1) * 512] = y[1024:]
        ak = np.asarray(r["akT"])
        n_ak[4 * c:4 * c + 4] = ak.reshape(nl, 2, 64, 4, 256).transpose(3, 0, 4, 1, 2)
        av = np.asarray(r["av"])
        n_av[4 * c:4 * c + 4] = av.reshape(nl, 4, 256, 2, 64).transpose(1, 0, 2, 3, 4)
        ck = np.asarray(r["ckT"]).reshape(nl, 4, 2, 64, 4, 256)[:, :, :, 0:48]
        n_ck[4 * c:4 * c + 4] = ck.transpose(4, 0, 5, 1, 2, 3).reshape(4, nl, 256, 4, 96)
        cv = np.asarray(r["cv"])
        n_cv[4 * c:4 * c + 4] = cv.reshape(nl, 4, 256, 4, 96).transpose(1, 0, 2, 3, 4)
        st = np.asarray(r["st"])
        n_st[4 * c:4 * c + 4] = st.reshape(nl, 4, 2, 4, 32, 64).transpose(1, 0, 2, 3, 4, 5)
    return (y_prompt, y_sample, n_ak, n_av, n_ck, n_cv, n_st)
```

```python
import math
from contextlib import ExitStack

import numpy as np
import concourse.bass as bass
import concourse.mybir as mybir
from concourse.bass_utils import run_bass_kernel_spmd

F32 = mybir.dt.float32
F32R = mybir.dt.float32r
AF = mybir.ActivationFunctionType
ALU = mybir.AluOpType
AX = mybir.AxisListType

D = 1024
L_FULL = 4
DFF = 2816
NTOK = 1536
EPS = 1e-6
NSLOT = 4
SLOTF = 2048
NA = 44
XR = 1281
NV = 104

(C_ONESM, C_BLK64, C_BLK48, C_BLK96, C_SEL32, C_PSWA, C_PSWC, C_TRIU, C_TRIL, C_SL16, C_SU16, C_IDENT) = range(12)
NCM = 12


def R(ap):
    return ap if ap.dtype == F32R else ap.bitcast(F32R)


def RO(ap):
    return R(ap) if ap.name == "arena" else ap


class TL:
    def __init__(self, name, step):
        self.name, self.step, self.count, self.sem = name, step, 0, None


class Reg:
    __slots__ = ("name", "w", "r", "excl")

    def __init__(self, name, excl=False):
        self.name, self.w, self.r, self.excl = name, None, {}, excl


class Q:
    def __init__(self, name, no_self=False):
        self.name = name
        self.tl = TL(name, 1)
        self.ops = []
        self.seen = {}
        self.no_self = no_self


class Prog:
    def __init__(self):
        self.pe = Q("pe", no_self=True)
        self.act = Q("act")
        self.dve = Q("dve")
        self.pool = Q("pool")
        self.sp = Q("sp")
        self.tls = [self.pe.tl, self.act.tl, self.dve.tl, self.pool.tl]
        self.dma_tls = []

    def new_dma_tl(self, name):
        t = TL(name, 16)
        self.tls.append(t)
        self.dma_tls.append(t)
        return t

    def emit(self, q, fn, reads=(), writes=(), tl=None):
        dma = tl is not None
        tl = tl or q.tl
        need = {}

        def req(t, v):
            if v > need.get(t, 0):
                need[t] = v

        for r in reads:
            if r.w:
                req(*r.w)
            if r.excl:
                for t, v in r.r.items():
                    if t is not tl:
                        req(t, v)
        for w in writes:
            if w.w:
                req(*w.w)
            for t, v in w.r.items():
                req(t, v)
        if dma and tl.count > 0:
            req(tl, tl.count)
        waits = []
        for t, v in need.items():
            if t is q.tl and q.no_self and not dma:
                continue
            if q.seen.get(t, 0) < v:
                waits.append((t, v))
                q.seen[t] = v
        tl.count += tl.step
        my = tl.count
        q.ops.append((waits, fn, tl))
        for w in writes:
            w.w = (tl, my)
            w.r = {}
        for r in reads:
            r.r[tl] = my

    def replay(self, q, eng):
        for waits, fn, tl in q.ops:
            for t, v in waits:
                eng.wait_ge(t.sem, v)
            ins = fn(eng)
            ins.then_inc(tl.sem, tl.step)


class Rot:
    def __init__(self, items):
        self.items, self.i = list(items), 0

    def next(self):
        it = self.items[self.i % len(self.items)]
        self.i += 1
        return it


class Arena:
    def __init__(self, n):
        self.n = n
        self.free = [True] * n

    def alloc(self, k=1):
        for s in range(self.n - k + 1):
            if all(self.free[s:s + k]):
                for i in range(s, s + k):
                    self.free[i] = False
                return s
        raise RuntimeError(f"arena exhausted (need {k}, free {sum(self.free)})")

    def release(self, s, k=1):
        for i in range(s, s + k):
            assert not self.free[i]
            self.free[i] = True


def build(nl=L_FULL, taps=(), upto=10 ** 9):
    nc = bass.Bass("TRN2", target_bir_lowering=False)
    nc.dge_precook = False
    P = Prog()
    es = ExitStack()

    def din(name, shape):
        return nc.dram_tensor(name, list(shape), F32, kind="ExternalInput").ap()

    def dout(name, shape):
        return nc.dram_tensor(name, list(shape), F32, kind="ExternalOutput").ap()

    NSL = 135
    wst = din("wst", [nl * NSL, 128, SLOTF])
    xin = din("xin", [128, 8, NTOK])
    cmat_d = din("cmat", [128, NCM, 128])
    urow_d = din("urow", [128, 2, 512])
    rope_d = din("rope", [128, 4, 512])
    hmask_d = din("hmask", [128, 4])
    bd_d = din("bdmask", [128, 256])
    pvec_d = din("pvec", [128, nl, NV])
    bgla_d = din("bgla", [128, nl, 256])
    w2_d = din("w2", [32, nl, 256])
    lamc_d = din("lamc", [128, nl * 4 * 48])
    cond_d = din("condT", [128, 8, 2])
    rmask_d = din("rmask", [128, 8])
    oz_d = din("oz", [128, 2, 512])
    cak_d = din("cakT", [nl, 2, 64, 512])
    cav_d = din("cav", [nl, 512, 128])
    cck_d = din("cckT", [nl, 4, 128, 512])
    ccv_d = din("ccv", [nl, 512, 384])
    sg_d = din("sgla", [nl, 2, 128, 64])

    yT_o = dout("yT", [128, 8, NTOK])
    ak_o = dout("akT", [nl, 2, 64, 1024])
    av_o = dout("av", [nl, 1024, 128])
    ck_o = dout("ckT", [nl, 4, 128, 1024])
    cv_o = dout("cv", [nl, 1024, 384])
    st_o = dout("st", [nl, 4, 2, 128, 64])
    tap_o = {name: dout("tap_" + name, shape) for name, shape in taps}

    XB = [512, 512, 136, 128]
    xch_in_t = [nc.dram_tensor(f"xch_in{b}", [XB[b], 512], F32) for b in range(4)]
    xch_out_t = [nc.dram_tensor(f"xch_out{b}", [4 * XB[b], 512], F32) for b in range(4)]

    def xloc(row):
        if row < 512:
            return 0, row
        if row < 640:
            return 1, row - 512
        if row < 768:
            return 3, row - 640
        if row < 1152:
            return 1, row - 768 + 128
        return 2, row - 1152

    xin_flat = [t_.ap().rearrange("r c -> (r c)") for t_ in xch_in_t]
    xout_flat = [t_.ap().rearrange("r c -> (r c)") for t_ in xch_out_t]

    def xin_rows(row0, nrows):
        b, lr = xloc(row0)
        return xch_in_t[b].ap()[lr:lr + nrows, :], xin_rs[b]

    def xin_v(row0, nrows, c, nfl=None):
        b, lr = xloc(row0)
        n = nrows * 512 if nfl is None else nfl
        return xin_flat[b][lr * 512:lr * 512 + n].rearrange("(t c) -> t c", c=c), xin_rs[b]

    def xo_multi(row0, nrows):
        b, lr = xloc(row0)
        v = xch_out_t[b].ap().rearrange("(r x) c -> r x c", r=4)
        return v[:, lr:lr + nrows, :].rearrange("r p c -> p r c"), xout_rs[b]

    def xout_v(r, row0, nrows, c, nfl=None):
        b, lr = xloc(row0)
        o = (r * XB[b] + lr) * 512
        n = nrows * 512 if nfl is None else nfl
        return xout_flat[b][o:o + n].rearrange("(t c) -> t c", c=c), xout_rs[b]

    def sb(name, shape):
        return es.enter_context(nc.sbuf_tensor(name, list(shape), F32))

    xT = sb("xT", [128, 8, NTOK])
    ring = sb("ring", [128, NSLOT, SLOTF])
    arena = sb("arena", [128, NA, 512])
    cmat = sb("cmat_s", [128, NCM, 128])
    urow = sb("urow_s", [128, 2, 512])
    rope = sb("rope_s", [128, 4, 512])
    hmask = sb("hmask_s", [128, 4])
    bdm = sb("bd_s", [128, 256])
    pvec = sb("pvec_s", [128, nl, NV])
    bgla = sb("bgla_s", [128, nl, 256])
    w2 = sb("w2_s", [32, nl, 256])
    lamc = sb("lamc_s", [128, nl * 4 * 48])
    cond = sb("cond_s", [128, 8, 2])
    scT = sb("scT_s", [128, 8, 2])
    rmask = sb("rmask_s", [128, 8])
    oz = sb("oz_s", [128, 2, 512])
    modT2 = sb("modT_s", [128, 2, 72, 2])
    gsT2 = sb("gs_s", [128, 2, 3, 8, 2])
    gtT2 = sb("gt_s", [128, 2, 3, 8, 2])
    lam = sb("lam_s", [128, 4, 4])
    lamt = sb("lamt_s", [128, nl * 2 * 48])
    gco = sb("gco_s", [128, 4])
    small = sb("small_s", [128, 16])

    psb = [es.enter_context(nc.psum_tensor(f"ps{i}", [128, 512], F32)) for i in range(8)]

    x_r = [[Reg(f"x{c}_{t}") for t in range(3)] for c in range(8)]
    ring_r = [Reg(f"ring{s}") for s in range(NSLOT)]
    ar_r = [Reg(f"ar{i}") for i in range(NA)]
    ps_r = [Reg(f"ps{i}", excl=True) for i in range(8)]
    const_r = Reg("consts")
    mod_rs = [Reg("mod0"), Reg("mod1")]
    misc_r = Reg("misc")
    small_r = Reg("small")
    xin_rs = [Reg(f"xch_in{b}") for b in range(4)]
    xout_rs = [Reg(f"xch_out{b}") for b in range(4)]

    ar = Arena(NA)

    class Tile:
        def __init__(self, k=1):
            self.k = k
            self.s = ar.alloc(k)
            self.regs = ar_r[self.s:self.s + k]

        def ap(self):
            return arena[:, self.s:self.s + self.k, :].rearrange("p k f -> p (k f)") if self.k > 1 else arena[:, self.s, :]

        def free(self):
            ar.release(self.s, self.k)

    ps_tmp = Rot(range(0, 5))
    ps_acc = Rot(range(5, 8))

    class PS:
        def __init__(self, kind="tmp"):
            self.i = (ps_tmp if kind == "tmp" else ps_acc).next()
            self.regs = [ps_r[self.i]]

        def ap(self):
            return psb[self.i][:]

    slot_tl = [P.new_dma_tl(f"slot{s}") for s in range(NSLOT)]
    misc_tl = Rot([P.new_dma_tl(f"md{i}") for i in range(8)])
    out_tl = Rot([P.new_dma_tl(f"od{i}") for i in range(4)])
    cc_tls = []
    for i in range(nl * 4):
        t_ = TL(f"cc{i}", 1)
        P.tls.append(t_)
        cc_tls.append(t_)

    E = P.emit

    def dma(q, out, in_, reads, writes, tl):
        E(q, lambda e, out=out, in_=in_: e.dma_start(out=out, in_=in_), reads, writes, tl=tl)

    def pool_ld(out, in_, writes, reads=()):
        dma(P.pool, out, in_, list(reads), list(writes), misc_tl.next())

    def sp_ld(out, in_, writes, reads=()):
        dma(P.sp, out, in_, list(reads), list(writes), misc_tl.next())

    def pool_st(out, in_, reads, writes=()):
        dma(P.pool, out, in_, list(reads), list(writes), out_tl.next())

    ring_n = [0]

    def ring_load(idx, nfl=SLOTF):
        s = ring_n[0] % NSLOT
        ring_n[0] += 1
        dma(P.sp, R(ring[:, s, 0:nfl]), R(wst[idx, :, 0:nfl]), [], [ring_r[s]], slot_tl[s])
        return s

    def mm(ps, out_ap, pairs, reads, start=True, stop=True):
        n = len(pairs)

        def fn(e, pairs=pairs, out_ap=out_ap, start=start, stop=stop):
            ins = None
            for i, (lt, rh) in enumerate(pairs):
                ins = e.matmul(out_ap, R(lt), R(rh), start=(start and i == 0), stop=(stop and i == n - 1))
            return ins

        E(P.pe, fn, list(reads), ps.regs)

    def A(out, in_, func, reads, writes, bias=0.0, scale=1.0):
        out = RO(out)
        E(P.act, lambda e: e.activation(out, in_, func, bias=bias, scale=scale), list(reads), list(writes))

    def V_tt(out, in0, in1, op, reads, writes, q=None):
        out = RO(out)
        E(q or P.dve, lambda e: e.tensor_tensor(out, in0, in1, op), list(reads), list(writes))

    def V_ts(out, in0, s1, s2, op0, op1, reads, writes, q=None):
        out = RO(out)
        if op1 is None:
            E(q or P.dve, lambda e: e.tensor_scalar(out, in0, s1, None, op0), list(reads), list(writes))
        else:
            E(q or P.dve, lambda e: e.tensor_scalar(out, in0, s1, s2, op0, op1), list(reads), list(writes))

    def V_stt(out, in0, sc, in1, op0, op1, reads, writes, q=None):
        out = RO(out)
        E(q or P.dve, lambda e: e.scalar_tensor_tensor(out, in0, sc, in1, op0, op1), list(reads), list(writes))

    def V_rcp(out, in_, reads, writes):
        out = RO(out)
        E(P.dve, lambda e: e.reciprocal(out, in_), list(reads), list(writes))

    def V_cp(out, in_, reads, writes, q=None):
        out = RO(out)
        E(q or P.dve, lambda e: e.tensor_copy(out, in_), list(reads), list(writes))

    def fill(tl_, val, nfl=None, q=None):
        n = tl_.k * 512 if nfl is None else nfl
        a = tl_.ap()
        for o in range(0, n, 512):
            w = min(512, n - o)
            V_cp(a[:, o:o + w], oz[:, 1 if val == 1.0 else 0, 0:w], [const_r], [tl_.regs[o // 512]], q=(q or P.pool))

    def CM(i, rows=128, cols=128):
        return cmat[0:rows, i, 0:cols]

    def rsqrt_from(ps_ap, out_ap, reads, writes, tmp_ap, tmp_regs):
        A(tmp_ap, ps_ap, AF.Ln, reads, tmp_regs, bias=EPS)
        A(out_ap, tmp_ap, AF.Exp, tmp_regs, writes, scale=-0.5)

    def tap(name, ap, reads):
        if name in tap_o:
            pool_st(tap_o[name], ap, reads)

    pool_ld(R(cmat[:]), R(cmat_d), [const_r])
    pool_ld(R(urow[:]), R(urow_d), [const_r])
    pool_ld(rope[:], rope_d, [const_r])
    pool_ld(hmask[:], hmask_d, [const_r])
    pool_ld(bdm[:], bd_d, [const_r])
    pool_ld(pvec[:], pvec_d, [const_r])
    pool_ld(bgla[:], bgla_d, [const_r])
    pool_ld(R(w2[:]), R(w2_d), [const_r])
    pool_ld(lamc[:], lamc_d, [const_r])
    pool_ld(cond[:], cond_d, [const_r])
    pool_ld(rmask[:], rmask_d, [const_r])
    pool_ld(oz[:], oz_d, [const_r])
    for t in range(3):
        pool_ld(xT[:, :, t * 512:(t + 1) * 512], xin[:, :, t * 512:(t + 1) * 512], [x_r[c][t] for c in range(8)])

    lc = lamc[:].rearrange("p (l a b d) -> p l a b d", l=nl, a=2, b=2, d=48)
    lt_v = lamt[:].rearrange("p (l a d) -> p l a d", l=nl, a=2, d=48)
    V_tt(lt_v, lc[:, :, :, 0, :], lc[:, :, :, 1, :], ALU.mult, [const_r], [misc_r])
    for l in range(nl):
        E(P.dve, lambda e, l=l: e.tensor_reduce(lam[:, l, 2:4], lt_v[:, l, :, :], AX.X, ALU.add), [misc_r], [misc_r])
        A(lam[:, l, 2:4], lam[:, l, 2:4], AF.Exp, [misc_r], [misc_r])
        V_tt(lam[:, l, 0:1], lam[:, l, 2:3], lam[:, l, 3:4], ALU.subtract, [misc_r], [misc_r])
        li = 0.8 - 0.6 * math.exp(-0.3 * l)
        V_ts(lam[:, l, 0:1], lam[:, l, 0:1], float(li), None, ALU.add, None, [misc_r], [misc_r])
        V_ts(lam[:, l, 1:2], lam[:, l, 0:1], -1.0, None, ALU.mult, None, [misc_r], [misc_r])
        V_ts(gco[:, l:l + 1], pvec[:, l, 101:102], float(1.0 - li), None, ALU.mult, None, [const_r, misc_r], [misc_r])
    A(R(scT[:]), cond[:], AF.Silu, [const_r], [misc_r])

    ada_state = {}

    def ada_begin(l):
        ps = PS("acc")
        ada_state[l] = dict(ps=ps, pv=ps.ap()[:, 0:144].rearrange("p (c j) -> p c j", j=2), k=0)

    def ada_slot(l):
        st_ = ada_state[l]
        sl = st_["k"]
        if sl >= 36:
            return
        st_["k"] += 1
        ps, pv = st_["ps"], st_["pv"]
        s = ring_load(l * NSL + 99 + sl)
        sv = ring[:, s, :].rearrange("p (k n) -> p k n", k=8)
        pt = PS()
        mm(pt, pt.ap()[0:2, 0:256], [(scT[:, kc, :], sv[:, kc, :]) for kc in range(8)], [ring_r[s], misc_r])
        tok = Tile()
        A(tok.ap()[0:2, 0:256], pt.ap()[0:2, 0:256], AF.Copy, pt.regs, tok.regs)
        for half in range(2):
            ch = sl * 2 + half
            mm(ps, pv[:, ch, :], [(tok.ap()[0:2, half * 128:(half + 1) * 128], CM(C_IDENT, 2, 2))], tok.regs + [const_r])
        tok.free()

    def ada_end(l):
        st_ = ada_state[l]
        while st_["k"] < 36:
            ada_slot(l)
        ps, pv = st_["ps"], st_["pv"]
        par = l % 2
        modT, gsT, gtT, mod_r = modT2[:, par], gsT2[:, par], gtT2[:, par], mod_rs[par]
        for j in range(2):
            V_tt(modT[:, :, j], pv[:, :, j], pvec[:, l, 24:96], ALU.add, ps.regs + [const_r], [mod_r])
        for s3 in range(3):
            for j in range(2):
                V_stt(gsT[:, s3, :, j], modT[:, (3 * s3 + 1) * 8:(3 * s3 + 2) * 8, j], 1.0,
                      pvec[:, l, s3 * 8:(s3 + 1) * 8], ALU.add, ALU.mult, [mod_r, const_r], [mod_r])
                V_ts(gtT[:, s3, :, j], modT[:, (3 * s3 + 2) * 8:(3 * s3 + 3) * 8, j],
                     (1.0 if s3 == 1 else 0.5), None, ALU.mult, None, [mod_r], [mod_r])

    def norm_mod(l, t, s3, j, h):
        par = l % 2
        modT, gsT, mod_r = modT2[:, par], gsT2[:, par], mod_rs[par]
        ps = PS()
        sqs = [Tile(), Tile(), Tile()]
        for c in range(8):
            sq = sqs[c % 3]
            xa = xT[:, c, t * 512:(t + 1) * 512]
            V_tt(R(sq.ap()), xa, xa, ALU.mult, [x_r[c][t]], sq.regs, q=P.pool)
            mm(ps, ps.ap(), [(CM(C_ONESM), sq.ap())], sq.regs + [const_r], start=(c == 0), stop=(c == 7))
        tmp = Tile()
        rstd = Tile()
        rsqrt_from(ps.ap(), rstd.ap(), ps.regs, rstd.regs, tmp.ap(), tmp.regs)
        hv = h.ap().rearrange("p (c f) -> p c f", c=8)
        for c in range(8):
            t2 = sqs[c % 3]
            V_stt(t2.ap(), xT[:, c, t * 512:(t + 1) * 512], gsT[:, s3, c, j:j + 1], rstd.ap(), ALU.mult, ALU.mult,
                  [x_r[c][t], mod_r] + rstd.regs, t2.regs)
            A(R(hv[:, c, :]), t2.ap(), AF.Identity, t2.regs + [mod_r], [h.regs[c]], bias=modT[:, 3 * s3 * 8 + c, j:j + 1])
        for tl_ in sqs:
            tl_.free()
        tmp.free()
        rstd.free()

    def ffn_multi(l, tiles, s, hook=None):
        s3 = 0 if s == 0 else 2
        gtT, mod_r = gtT2[:, l % 2], mod_rs[l % 2]
        nt = len(tiles)
        js = [1 if t == 2 else 0 for t in tiles]
        hs = [Tile(8) for _ in tiles]
        for ti, t in enumerate(tiles):
            norm_mod(l, t, s3, js[ti], hs[ti])
        hvs = [h.ap().rearrange("p (c f) -> p c f", c=8) for h in hs]
        acts = [Tile(11) for _ in tiles]
        avs = [a_.ap().rearrange("p (c f) -> p c f", c=11) for a_ in acts]
        base = l * NSL + s * 38
        for half in range(2):
            for fcl in range(11):
                sl = ring_load(base + half * 19 + fcl)
                sv = ring[:, sl, :].rearrange("p (g k n) -> p g k n", g=2, k=8)
                for ti in range(nt):
                    psg, psu = PS(), PS()
                    mm(psg, psg.ap(), [(sv[:, 0, kc, :], hvs[ti][:, kc, :]) for kc in range(8)], [ring_r[sl]] + hs[ti].regs)
                    mm(psu, psu.ap(), [(sv[:, 1, kc, :], hvs[ti][:, kc, :]) for kc in range(8)], [ring_r[sl]] + hs[ti].regs)
                    sg = Tile()
                    A(sg.ap(), psg.ap(), AF.Silu, psg.regs, sg.regs)
                    V_tt(R(avs[ti][:, fcl, :]), sg.ap(), psu.ap(), ALU.mult, sg.regs + psu.regs, [acts[ti].regs[fcl]])
                    sg.free()
                if hook is not None:
                    hook()
            for m in range(8):
                sl = ring_load(base + half * 19 + 11 + m, 1408)
                sv = ring[:, sl, 0:1408].rearrange("p (f n) -> p f n", f=11)
                for ti, t in enumerate(tiles):
                    psy = PS()
                    mm(psy, psy.ap(), [(sv[:, f, :], avs[ti][:, f, :]) for f in range(11)], [ring_r[sl]] + acts[ti].regs)
                    xa = xT[:, m, t * 512:(t + 1) * 512]
                    V_stt(xa, psy.ap(), gtT[:, s3, m, js[ti]:js[ti] + 1], xa, ALU.mult, ALU.add,
                          psy.regs + [mod_r, x_r[m][t]], [x_r[m][t]])
                if hook is not None:
                    hook()
        for tl_ in acts + hs:
            tl_.free()

    def proj_fm(sl_idx, hv, h, nchunks):
        sl = ring_load(sl_idx)
        sv = ring[:, sl, :].rearrange("p (k n) -> p k n", k=8)
        out = []
        for i in range(nchunks):
            ps = PS()
            mm(ps, ps.ap(), [(sv[:, kc, i * 128:(i + 1) * 128], hv[:, kc, :]) for kc in range(8)], [ring_r[sl]] + h.regs)
            out.append(ps)
        return out

    def proj_tm(sl_idx, hv, h, ncols):
        sl = ring_load(sl_idx)
        sv = ring[:, sl, :].rearrange("p (k n) -> p k n", k=8)
        per = 512 // ncols
        res = []
        for g in range(0, 4, per):
            ps = PS()
            pv = ps.ap()[:, 0:per * ncols].rearrange("p (b n) -> p b n", b=per)
            for bi in range(per):
                tb = g + bi
                mm(ps, pv[:, bi, :], [(hv[:, kc, tb * 128:(tb + 1) * 128], sv[:, kc, 0:ncols]) for kc in range(8)],
                   [ring_r[sl]] + h.regs)
            res.append((ps, pv, list(range(g, g + per))))
        return res

    def make_qk_unit(sidx, i, ss, hv, h, blk, gcol_ap, dst, rot):
        st = {}

        def s0():
            if "sl" not in ss:
                ss["sl"] = ring_load(sidx)
            sl = ss["sl"]
            sv = ring[:, sl, :].rearrange("p (k n) -> p k n", k=8)
            ps = PS()
            mm(ps, ps.ap(), [(sv[:, kc, i * 128:(i + 1) * 128], hv[:, kc, :]) for kc in range(8)], [ring_r[sl]] + h.regs)
            st["raw"] = Tile()
            A(st["raw"].ap(), ps.ap(), AF.Copy, ps.regs, st["raw"].regs)

        def s1():
            st["sq"] = Tile()
            V_tt(st["sq"].ap(), st["raw"].ap(), st["raw"].ap(), ALU.mult, st["raw"].regs, st["sq"].regs, q=P.pool)

        def s2():
            st["ps2"] = PS()
            mm(st["ps2"], st["ps2"].ap(), [(CM(blk), st["sq"].ap())], st["sq"].regs + [const_r])

        def s3():
            st["rstd"] = Tile()
            rsqrt_from(st["ps2"].ap(), st["rstd"].ap(), st["ps2"].regs, st["rstd"].regs, st["sq"].ap(), st["sq"].regs)

        def s4():
            raw, rstd = st["raw"], st["rstd"]
            if rot is None:
                V_stt(dst.ap(), raw.ap(), gcol_ap, rstd.ap(), ALU.mult, ALU.mult, raw.regs + rstd.regs + [const_r], dst.regs)
                for k_ in ("raw", "sq", "rstd"):
                    st[k_].free()
            else:
                st["xn"] = Tile()
                V_stt(st["xn"].ap(), raw.ap(), gcol_ap, rstd.ap(), ALU.mult, ALU.mult, raw.regs + rstd.regs + [const_r], st["xn"].regs)

        def s5():
            st["ps3"] = PS()
            mm(st["ps3"], st["ps3"].ap(), [(CM(rot[2]), st["xn"].ap())], st["xn"].regs + [const_r])

        def s6():
            st["t1"] = Tile()
            V_tt(st["t1"].ap(), st["ps3"].ap(), rope[:, rot[1], :], ALU.mult, st["ps3"].regs + [const_r], st["t1"].regs)
            V_tt(st["raw"].ap(), st["xn"].ap(), rope[:, rot[0], :], ALU.mult, st["xn"].regs + [const_r], st["raw"].regs, q=P.pool)

        def s7():
            V_tt(dst.ap(), st["t1"].ap(), st["raw"].ap(), ALU.add, st["t1"].regs + st["raw"].regs, dst.regs)
            for k_ in ("raw", "sq", "rstd", "xn", "t1"):
                st[k_].free()

        return [s0, s1, s2, s3, s4] if rot is None else [s0, s1, s2, s3, s4, s5, s6, s7]

    def qk_project(slot_specs, hv, h, blk, gcols, dsts, rot):
        units = []
        ci = 0
        for (sidx, nch) in slot_specs:
            ss = {}
            for i in range(nch):
                units.append(make_qk_unit(sidx, i, ss, hv, h, blk, gcols[ci], dsts[ci], rot))
                ci += 1
        run_pipeline(units, spacing=2)

    def attn_jobs(jobs, LOOK=3):
        items = []
        for J in jobs:
            nk = len(J["keys"])
            per = 512 // J["ncol"]
            i = 0
            while i < nk:
                g = min(per, nk - i)
                items.append((J, i, g))
                i += g
        pend = []

        def do_pv(ent):
            J, i, g, pT = ent
            nk = len(J["keys"])
            ncol = J["ncol"]
            if i == 0:
                J["acc"] = PS("acc")
            acc = J["acc"]
            for u in range(g):
                vap, vregs = J["vwin"](i + u)
                mm(acc, acc.ap()[0:128, 0:ncol], [(vap, pT.ap()[:, u * ncol:(u + 1) * ncol])], vregs + pT.regs,
                   start=(i + u == 0), stop=(i + u == nk - 1))
            pT.free()
            if i + g == nk:
                J["finish"](acc)

        for (J, i, g) in items:
            ncol, qt, qb, q0 = J["ncol"], J["qt"], J["qbase"], J["q0"]
            ps = PS()
            for u in range(g):
                kap, kregs = J["keys"][i + u]
                mm(ps, ps.ap()[:, u * ncol:(u + 1) * ncol], [(kap, qt.ap()[qb:qb + 64, q0:q0 + ncol])], kregs + qt.regs)
            pT = Tile()
            A(pT.ap()[:, 0:g * ncol], ps.ap()[:, 0:g * ncol], AF.Exp, ps.regs, pT.regs, scale=J["scale"])
            pend.append((J, i, g, pT))
            if len(pend) > LOOK:
                do_pv(pend.pop(0))
        while pend:
            do_pv(pend.pop(0))

    def run_pipeline(units, spacing=2):
        n = len(units)
        ns = max(len(u) for u in units)
        for step in range((n - 1) * spacing + ns):
            for u in range(n):
                st_ = step - u * spacing
                if 0 <= st_ < len(units[u]):
                    units[u][st_]()

    def mixer_A_heads(l, QA, segs, keyf, vwinf, mixA):
        jobs = []
        for (q0, ncol, sid) in segs:
            for hh in range(6):
                ch, par, kv = hh // 2, hh % 2, hh // 3
                base = par * 64

                def finish(acc, ch=ch, par=par, q0=q0, ncol=ncol):
                    nb_, db_ = (0, 64) if par == 0 else (64, 0)
                    rd = Tile()
                    V_rcp(rd.ap()[db_:db_ + 64, 0:ncol], acc.ap()[db_:db_ + 64, 0:ncol], acc.regs, rd.regs)
                    V_tt(R(mixA[ch].ap()[nb_:nb_ + 64, q0:q0 + ncol]), acc.ap()[nb_:nb_ + 64, 0:ncol],
                         rd.ap()[db_:db_ + 64, 0:ncol], ALU.mult, acc.regs + rd.regs, mixA[ch].regs)
                    rd.free()

                jobs.append(dict(qt=QA[ch], qbase=base, ncol=ncol, q0=q0, keys=keyf(sid, kv, base), scale=0.125,
                                 vwin=(lambda i, sid=sid, kv=kv, par=par: vwinf(sid, kv, par, i)), finish=finish))
        attn_jobs(jobs)

    def c_out_norm(l, oc, mixCh):
        sq = Tile()
        V_tt(R(sq.ap()[0:96, :]), oc.ap()[0:96, :], oc.ap()[0:96, :], ALU.mult, oc.regs, sq.regs, q=P.pool)
        ps = PS()
        mm(ps, ps.ap()[0:96, :], [(CM(C_BLK96, 96, 96), sq.ap()[0:96, :])], sq.regs + [const_r])
        rstd = Tile()
        A(sq.ap()[0:96, :], ps.ap()[0:96, :], AF.Ln, ps.regs, sq.regs, bias=EPS)
        A(rstd.ap()[0:96, :], sq.ap()[0:96, :], AF.Exp, sq.regs, rstd.regs, scale=-0.5)
        V_stt(R(mixCh.ap()[0:96, :]), oc.ap()[0:96, :], gco[0:96, l:l + 1], rstd.ap()[0:96, :], ALU.mult, ALU.mult,
              oc.regs + rstd.regs + [misc_r], mixCh.regs)
        sq.free()
        rstd.free()
        oc.free()

    def mixer_C_heads(l, QC, heads, segs, keyf, vwinf, mixC):
        jobs = []
        for hi, hh in enumerate(heads):
            st_h = {}
            for si_, (q0, ncol, sid) in enumerate(segs):
                st_s = {}
                for m in range(2):
                    base = m * 64

                    def finish(acc, m=m, ncol=ncol, q0=q0, st_h=st_h, st_s=st_s, hi=hi, last_seg=(si_ == len(segs) - 1)):
                        if "oc" not in st_h:
                            st_h["oc"] = Tile()
                        oc = st_h["oc"]
                        accs = Tile()
                        A(accs.ap()[:, 0:ncol], acc.ap()[:, 0:ncol], AF.Copy, acc.regs, accs.regs)
                        psd = PS()
                        mm(psd, psd.ap()[0:96, 0:ncol], [(CM(C_SEL32, 128, 96), accs.ap()[:, 0:ncol])], accs.regs + [const_r])
                        rd = Tile()
                        V_rcp(rd.ap()[0:96, 0:ncol], psd.ap()[0:96, 0:ncol], psd.regs, rd.regs)
                        om = Tile()
                        st_s[m] = om
                        V_tt(om.ap()[0:96, 0:ncol], accs.ap()[0:96, 0:ncol], rd.ap()[0:96, 0:ncol], ALU.mult,
                             accs.regs + rd.regs, om.regs)
                        rd.free()
                        accs.free()
                        if m == 1:
                            V_stt(oc.ap()[0:96, q0:q0 + ncol], st_s[1].ap()[0:96, 0:ncol], lam[0:96, l, 1:2],
                                  st_s[0].ap()[0:96, 0:ncol], ALU.mult, ALU.add, st_s[0].regs + st_s[1].regs + [misc_r], oc.regs)
                            st_s[0].free()
                            st_s[1].free()
                            if last_seg:
                                c_out_norm(l, oc, mixC[hi])

                    jobs.append(dict(qt=QC[hi], qbase=base, ncol=ncol, q0=q0, keys=keyf(sid, hh, base), scale=48 ** -0.5,
                                     vwin=(lambda i, sid=sid, hh=hh: vwinf(sid, hh, i)), finish=finish))
        attn_jobs(jobs)

    def gla(l, BQ, BK, BG, KTM, VTM, Vpad, segs, sample, st_dst):
        ktm_v = KTM.ap().rearrange("p (b n) -> p b n", b=4)
        vtm_v = VTM.ap().rearrange("p (b n) -> p b n", b=4)
        vpad_v = Vpad.ap().rearrange("p (b h n) -> p b h n", b=4, h=4)
        Lt = Tile(2)
        Lv = Lt.ap().rearrange("p (b n) -> p b n", b=4)
        for tb in range(4):
            ps = PS()
            mm(ps, ps.ap()[:, 0:256], [(BG.ap()[0:32, tb * 128:(tb + 1) * 128], w2[0:32, l, :])], BG.regs + [const_r])
            zb = Tile()
            V_tt(zb.ap()[:, 0:256], ps.ap()[:, 0:256], bgla[:, l, :], ALU.add, ps.regs + [const_r], zb.regs)
            A(zb.ap()[:, 0:256], zb.ap()[:, 0:256], AF.Exp, zb.regs, zb.regs, scale=-1.0)
            A(R(Lv[:, tb, :]), zb.ap()[:, 0:256], AF.Ln, zb.regs, Lt.regs, bias=1.0)
            zb.free()
        osb = [Tile(), Tile()]
        keep = {}
        for (c0, nb, sid) in segs:
            T = nb * 128
            tb0 = c0 // 128
            mid = T // 2 - 1
            psf, psbk = PS(), PS()
            for jb in range(nb):
                mm(psf, psf.ap()[:, jb * 128:T], [(Lv[:, tb0 + jb, 0:128], urow[:, 0, 0:T - jb * 128])], Lt.regs + [const_r],
                   start=(jb == 0), stop=(jb == nb - 1))
            for idx, jb in enumerate(range(nb - 1, -1, -1)):
                mm(psbk, psbk.ap()[:, 0:(jb + 1) * 128], [(Lv[:, tb0 + jb, 128:256], urow[:, 1, (4 - 1 - jb) * 128:512])],
                   Lt.regs + [const_r], start=(idx == 0), stop=(idx == nb - 1))
            V_cp(small[:, 0:1], psf.ap()[:, mid:mid + 1], psf.regs, [small_r])
            V_ts(small[:, 1:2], psf.ap()[:, mid:mid + 1], -1.0, None, ALU.mult, None, psf.regs, [small_r])
            V_cp(small[:, 2:3], psbk.ap()[:, mid:mid + 1], psbk.regs, [small_r])
            V_ts(small[:, 3:4], psbk.ap()[:, mid:mid + 1], -1.0, None, ALU.mult, None, psbk.regs, [small_r])
            Ef, Enf, Eb, Enb = Tile(), Tile(), Tile(), Tile()
            A(Ef.ap()[:, 0:T], psf.ap()[:, 0:T], AF.Exp, psf.regs + [small_r], Ef.regs, bias=small[:, 1:2], scale=1.0)
            A(Enf.ap()[:, 0:T], psf.ap()[:, 0:T], AF.Exp, psf.regs + [small_r], Enf.regs, bias=small[:, 0:1], scale=-1.0)
            A(Eb.ap()[:, 0:T], psbk.ap()[:, 0:T], AF.Exp, psbk.regs + [small_r], Eb.regs, bias=small[:, 3:4], scale=1.0)
            A(Enb.ap()[:, 0:T], psbk.ap()[:, 0:T], AF.Exp, psbk.regs + [small_r], Enb.regs, bias=small[:, 2:3], scale=-1.0)
            if sample:
                A(small[:, 6:7], psf.ap()[:, T - 1:T], AF.Exp, psf.regs, [small_r])
                A(small[:, 7:8], psbk.ap()[:, 0:1], AF.Exp, psbk.regs, [small_r])
                A(small[:, 4:5], small[:, 0:1], AF.Exp, [small_r], [small_r])
                A(small[:, 5:6], small[:, 2:3], AF.Exp, [small_r], [small_r])
                V_ts(small[:, 4:6], small[:, 4:6], float(32 ** -0.5), None, ALU.mult, None, [small_r], [small_r])
                qinf, qinb = Tile(), Tile()
                V_stt(R(qinf.ap()), Ef.ap(), small[:, 4:5], BQ.ap(), ALU.mult, ALU.mult, Ef.regs + BQ.regs + [small_r], qinf.regs)
                V_stt(R(qinb.ap()), Eb.ap(), small[:, 5:6], BQ.ap(), ALU.mult, ALU.mult, Eb.regs + BQ.regs + [small_r], qinb.regs)
                keep["qinf"], keep["qinb"] = qinf, qinb
                d_ap, d_rg = xin_v(1280, 1, 2, nfl=256)
                pool_st(d_ap, small[:, 6:8], [small_r], [d_rg])
            ktf, ktb = Tile(), Tile()
            V_tt(R(ktf.ap()[:, 0:T]), BK.ap()[:, c0:c0 + T], Enf.ap()[:, 0:T], ALU.mult, BK.regs + Enf.regs, ktf.regs)
            V_tt(R(ktb.ap()[:, 0:T]), BK.ap()[:, c0:c0 + T], Enb.ap()[:, 0:T], ALU.mult, BK.regs + Enb.regs, ktb.regs)
            pso = [PS("acc"), PS("acc")]
            def head_unit(hh):
                st_ = {}

                def s0():
                    qf, qb = Tile(), Tile()
                    st_["qf"], st_["qb"] = qf, qb
                    V_stt(R(qf.ap()[:, 0:T]), Ef.ap()[:, 0:T], hmask[:, hh:hh + 1], BQ.ap()[:, c0:c0 + T], ALU.mult, ALU.mult,
                          Ef.regs + BQ.regs + [const_r], qf.regs)
                    V_stt(R(qb.ap()[:, 0:T]), Eb.ap()[:, 0:T], hmask[:, hh:hh + 1], BQ.ap()[:, c0:c0 + T], ALU.mult, ALU.mult,
                          Eb.regs + BQ.regs + [const_r], qb.regs)

                def s1():
                    qf, qb = st_["qf"], st_["qb"]
                    MT = Tile(nb * T // 512)
                    st_["MT"] = MT
                    Mv = MT.ap().rearrange("p (b n) -> p b n", b=nb)
                    for jb in range(nb):
                        pf, pb = PS(), PS()
                        mm(pf, pf.ap()[:, 0:T - jb * 128], [(ktf.ap()[:, jb * 128:(jb + 1) * 128], qf.ap()[:, jb * 128:T])],
                           ktf.regs + qf.regs)
                        mm(pb, pb.ap()[:, 0:(jb + 1) * 128], [(ktb.ap()[:, jb * 128:(jb + 1) * 128], qb.ap()[:, 0:(jb + 1) * 128])],
                           ktb.regs + qb.regs)
                        if jb > 0:
                            A(R(Mv[:, jb, 0:jb * 128]), pb.ap()[:, 0:jb * 128], AF.Copy, pb.regs, MT.regs)
                        if jb < nb - 1:
                            A(R(Mv[:, jb, (jb + 1) * 128:T]), pf.ap()[:, 128:T - jb * 128], AF.Copy, pf.regs, MT.regs)
                        t1 = Tile()
                        V_tt(t1.ap()[:, 0:128], pf.ap()[:, 0:128], CM(C_TRIU), ALU.mult, pf.regs + [const_r], t1.regs)
                        V_tt(t1.ap()[:, 128:256], pb.ap()[:, jb * 128:(jb + 1) * 128], CM(C_TRIL), ALU.mult, pb.regs + [const_r], t1.regs)
                        V_tt(R(Mv[:, jb, jb * 128:(jb + 1) * 128]), t1.ap()[:, 0:128], t1.ap()[:, 128:256], ALU.add, t1.regs, MT.regs,
                             q=P.pool)
                        t1.free()

                def s2():
                    MT = st_["MT"]
                    Mv = MT.ap().rearrange("p (b n) -> p b n", b=nb)
                    for jb in range(nb):
                        first = (hh % 2 == 0 and jb == 0)
                        last = (hh % 2 == 1 and jb == nb - 1)
                        mm(pso[hh // 2], pso[hh // 2].ap()[:, 0:T], [(vpad_v[:, tb0 + jb, hh, :], Mv[:, jb, :])], Vpad.regs + MT.regs,
                           start=first, stop=last)
                    MT.free()
                    st_["qf"].free()
                    st_["qb"].free()

                return [s0, s1, s2]

            if sample:
                for hh in range(4):
                    for f_ in head_unit(hh):
                        f_()
            else:
                run_pipeline([head_unit(hh) for hh in range(4)], spacing=1)
            for c2 in range(2):
                if not sample:
                    V_cp(osb[c2].ap()[:, c0:c0 + T], pso[c2].ap()[:, 0:T], pso[c2].regs, osb[c2].regs)
                else:
                    V_cp(osb[c2].ap()[:, 0:T], pso[c2].ap()[:, 0:T], pso[c2].regs, osb[c2].regs)
            for d_ in range(2):
                kd = Tile()
                kdv = kd.ap().rearrange("p (b n) -> p b n", b=4)
                for jb in range(nb):
                    ps = PS()
                    prs = []
                    if d_ == 0:
                        for j2 in range(jb, nb):
                            lt = CM(C_SL16) if j2 == jb else urow[:, 0, 128:256]
                            prs.append((lt, Lv[:, tb0 + j2, 0:128]))
                    else:
                        for j2 in range(0, jb + 1):
                            lt = CM(C_SU16) if j2 == jb else urow[:, 0, 128:256]
                            prs.append((lt, Lv[:, tb0 + j2, 128:256]))
                    mm(ps, ps.ap()[:, 0:128], prs, Lt.regs + [const_r])
                    ed = Tile()
                    A(ed.ap()[:, 0:128], ps.ap()[:, 0:128], AF.Exp, ps.regs, ed.regs)
                    V_tt(R(kdv[:, jb, :]), ktm_v[:, tb0 + jb, :], ed.ap()[:, 0:128], ALU.mult, KTM.regs + ed.regs, kd.regs)
                    ed.free()
                pst = PS()
                mm(pst, pst.ap()[:, 0:256], [(kdv[:, jb, :], vtm_v[:, tb0 + jb, :]) for jb in range(nb)], kd.regs + VTM.regs)
                kd.free()
                stt = Tile()
                if not sample:
                    for hh in range(4):
                        A(stt.ap()[32 * hh:32 * hh + 32, 0:64], pst.ap()[32 * hh:32 * hh + 32, 64 * hh:64 * hh + 64], AF.Copy,
                          pst.regs, stt.regs)
                    pool_st(st_dst(sid, d_), stt.ap()[:, 0:64], stt.regs)
                else:
                    A(stt.ap()[:, 0:256], pst.ap()[:, 0:256], AF.Copy, pst.regs, stt.regs)
                    r0 = 1152 + 64 * d_
                    d_ap, d_rg = xin_v(r0, 64, 256)
                    pool_st(d_ap, stt.ap()[:, 0:256], stt.regs, [d_rg])
                stt.free()
            for tl_ in (Ef, Enf, Eb, Enb, ktf, ktb):
                tl_.free()
        Lt.free()
        return osb, keep

    def gla_out(l, osb, GR, mixB):
        for c2 in range(2):
            sq = Tile()
            V_tt(R(sq.ap()), osb[c2].ap(), osb[c2].ap(), ALU.mult, osb[c2].regs, sq.regs, q=P.pool)
            ps = PS()
            mm(ps, ps.ap(), [(CM(C_BLK64), sq.ap())], sq.regs + [const_r])
            rstd = Tile()
            rsqrt_from(ps.ap(), rstd.ap(), ps.regs, rstd.regs, sq.ap(), sq.regs)
            V_stt(sq.ap(), osb[c2].ap(), pvec[:, l, 100:101], rstd.ap(), ALU.mult, ALU.mult, osb[c2].regs + rstd.regs + [const_r], sq.regs)
            V_tt(R(mixB[c2].ap()), sq.ap(), GR[c2].ap(), ALU.mult, sq.regs + GR[c2].regs, mixB[c2].regs)
            sq.free()
            rstd.free()

    def w_out(l, t, j, mix):
        gtT, mod_r = gtT2[:, l % 2], mod_rs[l % 2]
        base = l * NSL + 91
        for m in range(8):
            sl = ring_load(base + m, 1152)
            sv = ring[:, sl, 0:1152].rearrange("p (c n) -> p c n", c=9)
            psy = PS()
            prs, rds = [], [ring_r[sl]]
            for c in range(9):
                rows = 128 if c < 5 else 96
                prs.append((sv[0:rows, c, :], mix[c].ap()[0:rows, :]))
                rds += mix[c].regs
            mm(psy, psy.ap(), prs, rds)
            xa = xT[:, m, t * 512:(t + 1) * 512]
            V_stt(xa, psy.ap(), gtT[:, 1, m, j:j + 1], xa, ALU.mult, ALU.add, psy.regs + [mod_r, x_r[m][t]], [x_r[m][t]])

    import os as _os
    stopat = _os.environ.get("STOPAT", "")

    class _Stop(Exception):
        pass

    def chk(name):
        if stopat == name:
            raise _Stop()

    def emit_cc(l, b_):
        E(P.pool, lambda e, b_=b_: e.collective_compute("AllGather", ALU.bypass, replica_groups=[[0, 1, 2, 3], [4, 5, 6, 7]],
                                                       ins=[xch_in_t[b_].ap().opt()], outs=[xch_out_t[b_].ap().opt()]),
          [xin_rs[b_]], [xout_rs[b_]], tl=cc_tls[l * 4 + b_])

    def mixer(l, t):
        try:
            mixer_(l, t)
        except _Stop:
            pass

    def mixer_(l, t):
        sample = (t == 2)
        j = 1 if sample else 0
        wb = l * NSL + 76
        h = Tile(8)
        norm_mod(l, t, 1, j, h)
        hv = h.ap().rearrange("p (c f) -> p c f", c=8)
        mix = [Tile() for _ in range(9)] if not sample else None
        tcol = t * 512
        QA = [Tile() for _ in range(3)]
        KA = [Tile() for _ in range(2)]
        dsts = QA + KA
        rotA = (0, 1, C_PSWA) if sample else None
        qk_project([(wb + 0, 2), (wb + 1, 2), (wb + 2, 1)], hv, h, C_BLK64,
                   [pvec[:, l, 96:97]] * 3 + [pvec[:, l, 97:98]] * 2, dsts, rotA)
        (ps, pv, tbs), = proj_tm(wb + 3, hv, h, 128)
        avt = Tile()
        V_cp(avt.ap(), ps.ap(), ps.regs, avt.regs)
        if not sample:
            pool_st(av_o[l, tcol:tcol + 512, :].rearrange("(b p) c -> p b c", p=128), avt.ap().rearrange("p (b c) -> p b c", b=4), avt.regs)
            for kv in range(2):
                pool_st(ak_o[l, kv, :, tcol:tcol + 512], KA[kv].ap()[0:64, :], KA[kv].regs)
        else:
            for kv in range(2):
                d_ap, d_rg = xin_rows(kv * 64, 64)
                pool_st(d_ap, KA[kv].ap()[0:64, :], KA[kv].regs, [d_rg])
            d_ap, d_rg = xin_v(640, 128, 128)
            pool_st(d_ap.rearrange("(b p) c -> p b c", p=128),
                    avt.ap().rearrange("p (b c) -> p b c", b=4), avt.regs, [d_rg])
            emit_cc(l, 3)
            for tl_ in KA:
                tl_.free()
        chk("Aproj")
        VAkv = []
        if not sample:
            VAo = Tile(2)
            vao_v = VAo.ap()[:, 0:768].rearrange("p (b n) -> p b n", b=4)
            fill(VAo, 1.0)
            VAo2 = Tile(2)
            fill(VAo2, 1.0)
            vao2_v = VAo2.ap()[:, 0:768].rearrange("p (b n) -> p b n", b=4)
            avv = avt.ap().rearrange("p (b c) -> p b c", b=4)
            V_cp(R(vao_v[:, :, 64:128]), avv[:, :, 0:64], avt.regs, VAo.regs, q=P.pool)
            V_cp(R(vao2_v[:, :, 64:128]), avv[:, :, 64:128], avt.regs, VAo2.regs, q=P.pool)
            VAkv = [(VAo, vao_v), (VAo2, vao2_v)]

            def keyfA(sid, kv, base):
                return [(KA[kv].ap()[base:base + 64, sid * 256 + kc * 128: sid * 256 + (kc + 1) * 128], KA[kv].regs) for kc in range(2)]

            def vwinA(sid, kv, par, i):
                tl_, vv = VAkv[kv]
                off = 64 if par == 0 else 0
                return (vv[:, sid * 2 + i, off:off + 128], tl_.regs)

            mixer_A_heads(l, QA, [(0, 256, 0), (256, 256, 1)], keyfA, vwinA, mix[0:3])
            VAo2.free()
            VAo.free()
            for tl_ in QA + KA:
                tl_.free()
        avt.free()
        chk("A")
        QC = [Tile() for _ in range(4)]
        KC = [Tile() for _ in range(4)]
        dsts = QC + KC
        rotC = (2, 3, C_PSWC) if sample else None
        qk_project([(wb + 4 + si, 2) for si in range(4)], hv, h, C_BLK48,
                   [pvec[:, l, 98:99]] * 4 + [pvec[:, l, 99:100]] * 4, dsts, rotC)
        cvt = Tile(3)
        cvv = cvt.ap().rearrange("p (b c) -> p b c", b=4)
        for (ps, pv, tbs) in proj_tm(wb + 8, hv, h, 256):
            V_cp(cvv[:, tbs[0]:tbs[0] + 2, 0:256], pv, ps.regs, cvt.regs)
        for (ps, pv, tbs) in proj_tm(wb + 9, hv, h, 128):
            V_cp(cvv[:, :, 256:384], pv, ps.regs, cvt.regs)
        if not sample:
            pool_st(cv_o[l, tcol:tcol + 512, :].rearrange("(b p) c -> p b c", p=128), cvv, cvt.regs)
            for hh in range(4):
                pool_st(ck_o[l, hh, :, tcol:tcol + 512], KC[hh].ap(), KC[hh].regs)
            VCo = [Tile() for _ in range(4)]
            for hh in range(4):
                vv = VCo[hh].ap().rearrange("p (b n) -> p b n", b=4)
                fill(VCo[hh], 1.0)
                V_cp(R(vv[:, :, 0:96]), cvv[:, :, hh * 96:(hh + 1) * 96], cvt.regs, VCo[hh].regs, q=P.pool)

            def keyfC(sid, hh, base):
                return [(KC[hh].ap()[base:base + 64, sid * 256 + kc * 128: sid * 256 + (kc + 1) * 128], KC[hh].regs) for kc in range(2)]

            def vwinC(sid, hh, i):
                vv = VCo[hh].ap().rearrange("p (b n) -> p b n", b=4)
                return (vv[:, sid * 2 + i, :], VCo[hh].regs)

            mixer_C_heads(l, QC, [0, 1, 2, 3], [(0, 256, 0), (256, 256, 1)], keyfC, vwinC, mix[5:9])
            for tl_ in VCo + QC + KC:
                tl_.free()
        else:
            for hh in range(4):
                d_ap, d_rg = xin_rows(128 + hh * 128, 128)
                pool_st(d_ap, KC[hh].ap(), KC[hh].regs, [d_rg])
            d_ap, d_rg = xin_v(768, 384, 384)
            pool_st(d_ap.rearrange("(b p) c -> p b c", p=128), cvv, cvt.regs, [d_rg])
            for b_ in range(2):
                E(P.pool, lambda e, b_=b_: e.collective_compute("AllGather", ALU.bypass, replica_groups=[[0, 1, 2, 3], [4, 5, 6, 7]],
                                                               ins=[xch_in_t[b_].ap().opt()], outs=[xch_out_t[b_].ap().opt()]),
                  [xin_rs[b_]], [xout_rs[b_]], tl=cc_tls[l * 4 + b_])
            for tl_ in KC:
                tl_.free()
        cvt.free()
        chk("C")
        BQ, BK, BR0, BR1, BG = Tile(), Tile(), Tile(), Tile(), Tile()
        raws = [BQ, BK, BR0, BR1, BG]
        ci = 0
        for si, nch in enumerate((2, 2, 1)):
            pss = proj_fm(wb + 10 + si, hv, h, nch)
            for ps in pss:
                dst = raws[ci]
                if ci in (2, 3):
                    A(dst.ap(), ps.ap(), AF.Silu, ps.regs, dst.regs)
                elif ci == 4:
                    A(R(dst.ap()[0:32, :]), ps.ap()[0:32, :], AF.Copy, ps.regs, dst.regs)
                else:
                    A(dst.ap(), ps.ap(), AF.Copy, ps.regs, dst.regs)
                ci += 1
        chk("Braw")
        KTM, VTM, Vpad = Tile(), Tile(2), Tile(4)
        ktm_v = KTM.ap().rearrange("p (b n) -> p b n", b=4)
        vtm_v = VTM.ap().rearrange("p (b n) -> p b n", b=4)
        vpad_v = Vpad.ap().rearrange("p (b h n) -> p b h n", b=4, h=4)
        if _os.environ.get("SKIP", "") != "fill":
            fill(Vpad, 0.0)
        for (ps, pv, tbs) in proj_tm(wb + 13, hv, h, 256):
            b0 = tbs[0]
            for bi in range(2):
                A(R(ktm_v[:, b0 + bi, :]), pv[:, bi, 0:128], AF.Copy, ps.regs, KTM.regs)
                A(R(vtm_v[:, b0 + bi, 0:128]), pv[:, bi, 128:256], AF.Copy, ps.regs, VTM.regs)

        chk("Btm1")
        for (ps, pv, tbs) in proj_tm(wb + 14, hv, h, 128):
            for bi in range(4):
                A(R(vtm_v[:, bi, 128:256]), pv[:, bi, :], AF.Copy, ps.regs, VTM.regs)
        for hh in range(4):
            o_ = (hh % 2) * 64
            V_cp(R(vpad_v[:, :, hh, o_:o_ + 64]), vtm_v[:, :, hh * 64:(hh + 1) * 64], VTM.regs, Vpad.regs, q=P.pool)
        h.free()
        chk("Bproj")
        if not sample:
            segs = [(0, 2, 0), (256, 2, 1)]
            osb, _ = gla(l, BQ, BK, BG, KTM, VTM, Vpad, segs, False,
                         lambda sid, d_: st_o[l, t * 2 + sid, d_, :, :])
            for tl_ in (BQ, BK, BG, KTM, VTM, Vpad):
                tl_.free()
            chk("gla")
            gla_out(l, osb, [BR0, BR1], mix[3:5])
            for tl_ in osb + [BR0, BR1]:
                tl_.free()
            chk("glaout")
            w_out(l, t, j, mix)
            for tl_ in mix:
                tl_.free()
            return
        osb, keep = gla(l, BQ, BK, BG, KTM, VTM, Vpad, [(0, 4, 0)], True, None)
        for tl_ in (BQ, BK, BG, KTM, VTM, Vpad):
            tl_.free()
        for b_ in range(2, 3):
            E(P.pool, lambda e, b_=b_: e.collective_compute("AllGather", ALU.bypass, replica_groups=[[0, 1, 2, 3], [4, 5, 6, 7]],
                                                           ins=[xch_in_t[b_].ap().opt()], outs=[xch_out_t[b_].ap().opt()]),
              [xin_rs[b_]], [xout_rs[b_]], tl=cc_tls[l * 4 + b_])
        mix = [Tile() for _ in range(9)]
        def gla_post():
            Sin = []
            for d_ in range(2):
                S = Tile()
                fill(S, 0.0, 256)
                for hh in range(4):
                    pool_ld(R(S.ap()[32 * hh:32 * hh + 32, 64 * hh:64 * hh + 64]), R(sg_d[l, d_, 32 * hh:32 * hh + 32, :]), S.regs)
                SL = Tile(2)
                slv = SL.ap().rearrange("p (r c) -> p r c", r=4)
                r0 = 1152 + 64 * d_
                for r in range(4):
                    s_ap, s_rg = xout_v(r, r0, 64, 256)
                    pool_ld(R(slv[:, r, :]), R(s_ap), SL.regs, [s_rg])
                AD = Tile()
                adv = AD.ap()[:, 0:8].rearrange("p (r c) -> p r c", r=4)
                for r in range(4):
                    s_ap, s_rg = xout_v(r, 1280, 1, 2, nfl=256)
                    pool_ld(R(adv[:, r, :]), R(s_ap), AD.regs, [s_rg])
                V_ts(AD.ap()[:, 8:16], AD.ap()[:, 0:8], -1.0, None, ALU.add, None, AD.regs, AD.regs)
                am1 = AD.ap()[:, 8:16].rearrange("p (r c) -> p r c", r=4)
                order = range(4) if d_ == 0 else range(3, -1, -1)
                tmp = Tile()
                for r in order:
                    V_stt(tmp.ap()[:, 0:256], S.ap()[:, 0:256], am1[:, r, d_:d_ + 1], slv[:, r, :], ALU.mult, ALU.add,
                          S.regs + AD.regs + SL.regs, tmp.regs)
                    V_stt(S.ap()[:, 0:256], tmp.ap()[:, 0:256], rmask[:, d_ * 4 + r:d_ * 4 + r + 1], S.ap()[:, 0:256], ALU.mult, ALU.add,
                          tmp.regs + S.regs + [const_r], S.regs)
                V_tt(R(tmp.ap()[:, 0:256]), S.ap()[:, 0:256], bdm[:], ALU.mult, S.regs + [const_r], tmp.regs)
                Sin.append(tmp)
                S.free()
                SL.free()
                AD.free()
            qin = [keep["qinf"], keep["qinb"]]
            for c2 in range(2):
                ps = PS("acc")
                mm(ps, ps.ap(), [(Sin[d_].ap()[:, c2 * 128:(c2 + 1) * 128], qin[d_].ap()) for d_ in range(2)],
                   Sin[0].regs + Sin[1].regs + qin[0].regs + qin[1].regs)
                V_tt(osb[c2].ap(), osb[c2].ap(), ps.ap(), ALU.add, osb[c2].regs + ps.regs, osb[c2].regs)
            for tl_ in Sin + qin:
                tl_.free()
            gla_out(l, osb, [BR0, BR1], mix[3:5])
            for tl_ in osb + [BR0, BR1]:
                tl_.free()

        VAll = Tile(8)
        vav = VAll.ap()[:, 0:3840].rearrange("p (k n) -> p k n", k=20)
        fill(VAll, 1.0, q=P.dve)
        for kv in range(2):
            KAll = Tile(5)
            for half in range(2):
                s_ap, s_rg = xo_multi(kv * 64, 64)
                sp_ld(R(KAll.ap()[half * 64:(half + 1) * 64, 0:2048].rearrange("p (r c) -> p r c", r=4)),
                      R(s_ap), KAll.regs, [s_rg])
                sp_ld(R(KAll.ap()[half * 64:(half + 1) * 64, 2048:2560]), R(cak_d[l, kv, :, :]), KAll.regs)
            for r in range(4):
                s_ap, s_rg = xout_v(r, 640, 128, 128)
                src = s_ap.rearrange("(b p) c -> p b c", p=128)
                sp_ld(R(vav[:, r * 4:(r + 1) * 4, 64:128]), R(src[:, :, kv * 64:(kv + 1) * 64]), VAll.regs, [s_rg])
            sp_ld(R(vav[:, 16:20, 64:128]), R(cav_d[l, :, kv * 64:(kv + 1) * 64].rearrange("(b p) c -> p b c", p=128)), VAll.regs)
            jobs = []
            for g in range(3):
                hh = kv * 3 + g
                ch, par = hh // 2, hh % 2
                base = par * 64
                keys = [(KAll.ap()[base:base + 64, kc * 128:(kc + 1) * 128], KAll.regs) for kc in range(20)]

                def finish(acc, ch=ch, par=par):
                    nb_, db_ = (0, 64) if par == 0 else (64, 0)
                    rd = Tile()
                    V_rcp(rd.ap()[db_:db_ + 64, :], acc.ap()[db_:db_ + 64, :], acc.regs, rd.regs)
                    V_tt(R(mix[ch].ap()[nb_:nb_ + 64, :]), acc.ap()[nb_:nb_ + 64, :], rd.ap()[db_:db_ + 64, :], ALU.mult,
                         acc.regs + rd.regs, mix[ch].regs)
                    rd.free()

                off = 64 if par == 0 else 0
                jobs.append(dict(qt=QA[ch], qbase=base, ncol=512, q0=0, keys=keys, scale=0.125,
                                 vwin=(lambda i, off=off: (vav[:, i, off:off + 128], VAll.regs)), finish=finish))
            attn_jobs(jobs)
            KAll.free()
        VAll.free()
        for tl_ in QA:
            tl_.free()
        gla_post()
        KCh = Tile(5)
        VCh = Tile(5)
        vcv = VCh.ap().rearrange("p (k n) -> p k n", k=20)

        def keyfC2(sid, hh, base):
            return [(KCh.ap()[base:base + 64, kc * 128:(kc + 1) * 128], KCh.regs) for kc in range(20)]

        def vwinC2(sid, hh, i):
            return (vcv[:, i, :], VCh.regs)

        fill(VCh, 1.0, q=P.dve)
        for hh in range(4):
            s_ap, s_rg = xo_multi(128 + hh * 128, 128)
            sp_ld(R(KCh.ap()[:, 0:2048].rearrange("p (r c) -> p r c", r=4)), R(s_ap), KCh.regs, [s_rg])
            sp_ld(R(KCh.ap()[:, 2048:2560]), R(cck_d[l, hh, :, :]), KCh.regs)
            for r in range(4):
                s_ap, s_rg = xout_v(r, 768, 384, 384)
                src = s_ap.rearrange("(b p) c -> p b c", p=128)
                sp_ld(R(vcv[:, r * 4:(r + 1) * 4, 0:96]), R(src[:, :, hh * 96:(hh + 1) * 96]), VCh.regs, [s_rg])
            sp_ld(R(vcv[:, 16:20, 0:96]), R(ccv_d[l, :, hh * 96:(hh + 1) * 96].rearrange("(b p) c -> p b c", p=128)), VCh.regs)
            mixer_C_heads(l, [QC[hh]], [hh], [(0, 512, 0)], keyfC2, vwinC2, [mix[5 + hh]])
        KCh.free()
        VCh.free()
        for tl_ in QC:
            tl_.free()
        w_out(l, t, j, mix)
        for tl_ in mix:
            tl_.free()

    step = [0]

    def go():
        step[0] += 1
        return step[0] <= upto

    for l in range(nl):
        if l == 0:
            ada_begin(0)
            ada_end(0)
        if go():
            ffn_multi(l, [0, 1], 0)
        if go():
            mixer(l, 0)
        if go():
            mixer(l, 1)
        if go():
            if l + 1 < nl:
                ada_begin(l + 1)
                ffn_multi(l, [0, 1], 1, hook=lambda l=l: ada_slot(l + 1))
                ada_end(l + 1)
            else:
                ffn_multi(l, [0, 1], 1)
                for t in range(2):
                    pool_st(yT_o[:, :, t * 512:(t + 1) * 512], xT[:, :, t * 512:(t + 1) * 512], [x_r[c][t] for c in range(8)])
        if go():
            ffn_multi(l, [2], 0)
        if go():
            mixer(l, 2)
        if go():
            ffn_multi(l, [2], 1)
    for t in range(2, 3):
        pool_st(yT_o[:, :, t * 512:(t + 1) * 512], xT[:, :, t * 512:(t + 1) * 512], [x_r[c][t] for c in range(8)])

    for t in P.tls:
        t.sem = es.enter_context(nc.semaphore(t.name))
    final_waits = [(t, t.count) for t in P.dma_tls if t.count > 0]
    with nc.allow_low_precision("float32r PE operands"), nc.Block() as block:
        @block.sync
        def _(e):
            P.replay(P.sp, e)

        @block.tensor
        def _(e):
            P.replay(P.pe, e)

        @block.scalar
        def _(e):
            P.replay(P.act, e)

        @block.vector
        def _(e):
            P.replay(P.dve, e)

        @block.gpsimd
        def _(e):
            P.replay(P.pool, e)
            for t, v in final_waits:
                e.wait_ge(t.sem, v)
    es.close()
    return nc


OFF = dict(a_q=0, a_k=384, a_v=512, b_q=640, b_k=768, b_v=896, b_g=1152, b_r=1184, c_q=1440, c_k=1824, c_v=2208)


def _w_in_cols():
    def pad(lst, n=256):
        return list(lst) + [-1] * (n - len(lst))

    def rng(a, n):
        return list(range(a, a + n))

    def cmap(base, hh):
        out = []
        for m in range(2):
            out += rng(base + hh * 96 + m * 48, 48) + [-1] * 16
        return out

    slots = []
    slots.append(rng(OFF["a_q"], 256))
    slots.append(rng(OFF["a_q"] + 256, 128) + rng(OFF["a_k"], 64) * 2)
    slots.append(pad(rng(OFF["a_k"] + 64, 64) * 2))
    slots.append(pad(rng(OFF["a_v"], 128)))
    for base in (OFF["c_q"], OFF["c_k"]):
        slots.append(cmap(base, 0) + cmap(base, 1))
        slots.append(cmap(base, 2) + cmap(base, 3))
    slots.append(rng(OFF["c_v"], 256))
    slots.append(pad(rng(OFF["c_v"] + 256, 128)))
    slots.append(rng(OFF["b_q"], 128) + rng(OFF["b_k"], 128))
    slots.append(rng(OFF["b_r"], 256))
    slots.append(pad(rng(OFF["b_g"], 32)))
    slots.append(rng(OFF["b_k"], 128) + rng(OFF["b_v"], 128))
    slots.append(pad(rng(OFF["b_v"] + 128, 128)))
    assert len(slots) == 15
    return np.array(slots, dtype=np.int64)


def pack_weights(inp, nl):
    NSL = 135
    wst = np.zeros((nl * NSL, 128, SLOTF), np.float32)
    cols = _w_in_cols()
    for l in range(nl):
        b = l * NSL
        for s in range(2):
            wg = inp["w_ffn_gate"][l, s].reshape(8, 128, 22, 128).transpose(2, 1, 0, 3)
            wu = inp["w_ffn_up"][l, s].reshape(8, 128, 22, 128).transpose(2, 1, 0, 3)
            gu = np.stack([wg, wu], axis=2).reshape(22, 128, SLOTF)
            wd = inp["w_ffn_down"][l, s].reshape(2, 11, 128, 8, 128).transpose(0, 3, 2, 1, 4).reshape(2, 8, 128, 1408)
            for half in range(2):
                o = b + s * 38 + half * 19
                wst[o:o + 11] = gu[half * 11:(half + 1) * 11]
                wst[o + 11:o + 19, :, 0:1408] = wd[half]
        wi = np.concatenate([inp["w_in"][l], np.zeros((D, 1), np.float32)], axis=1)
        for si in range(15):
            wc = wi[:, cols[si]]
            wst[b + 76 + si] = wc.reshape(8, 128, 256).transpose(1, 0, 2).reshape(128, SLOTF)
        wo = inp["w_out"][l]
        wpad = np.zeros((9, 128, D), np.float32)
        for c in range(5):
            wpad[c] = wo[c * 128:(c + 1) * 128]
        for c in range(4):
            wpad[5 + c, 0:96] = wo[640 + c * 96:640 + (c + 1) * 96]
        wst[b + 91:b + 99, :, 0:1152] = wpad.reshape(9, 128, 8, 128).transpose(2, 1, 0, 3).reshape(8, 128, 1152)
        wa = inp["w_ada"][l].reshape(8, 128, 36, 256).transpose(2, 1, 0, 3).reshape(36, 128, SLOTF)
        wst[b + 99:b + 135] = wa
    return wst


def make_consts():
    cm = np.zeros((128, NCM, 128), np.float32)
    p = np.arange(128)
    cm[:, C_ONESM, :] = 1.0 / D
    cm[:, C_BLK64, :] = (p[:, None] // 64 == p[None, :] // 64) / 64.0
    real48 = (p % 64) < 48
    cm[:, C_BLK48, :] = ((p[:, None] // 64 == p[None, :] // 64) & real48[:, None] & real48[None, :]) / 48.0
    cm[0:96, C_BLK96, 0:96] = 1.0 / 96.0
    cm[96:128, C_SEL32, 0:96] = 1.0 / 32.0
    permA = np.zeros(128, np.int64)
    for m in range(128):
        d = m % 64
        permA[m] = m + 16 if (d % 32) < 16 else m - 16
    cm[permA, C_PSWA, p] = 1.0
    permC = np.arange(128)
    for m in range(128):
        d = m % 64
        if d < 48:
            permC[m] = m + 12 if (d % 24) < 12 else m - 12
    for m in range(128):
        if (m % 64) < 48:
            cm[permC[m], C_PSWC, m] = 1.0
    cm[:, C_IDENT, :] = np.eye(128, dtype=np.float32)
    cm[:, C_TRIU, :] = (p[:, None] <= p[None, :])
    cm[:, C_TRIL, :] = (p[:, None] >= p[None, :])
    cm[:, C_SL16, :] = (p[:, None] > p[None, :]).astype(np.float32) / -16.0
    cm[:, C_SU16, :] = (p[:, None] < p[None, :]).astype(np.float32) / -16.0
    ur = np.zeros((128, 2, 512), np.float32)
    ur[:, 0, :] = -1.0 / 16.0
    ur[:, 0, 0:128] = (p[:, None] <= p[None, :]).astype(np.float32) / -16.0
    ur[:, 1, :] = -1.0 / 16.0
    ur[:, 1, 384:512] = (p[:, None] >= p[None, :]).astype(np.float32) / -16.0
    hm = np.zeros((128, 4), np.float32)
    for hh in range(4):
        hm[32 * hh:32 * hh + 32, hh] = 32 ** -0.5
    bd = np.zeros((128, 256), np.float32)
    for hh in range(4):
        bd[32 * hh:32 * hh + 32, 64 * hh:64 * hh + 64] = 1.0
    return cm, ur, hm, bd


def rope_tables(tok0):
    t = np.arange(tok0, tok0 + 512)
    row = (t // 64).astype(np.float32)
    col = (t % 64).astype(np.float32)
    out = np.zeros((128, 4, 512), np.float32)
    out[:, 0, :] = 1.0
    out[:, 2, :] = 1.0
    for p in range(128):
        d = p % 64
        half, dd = d // 32, d % 32
        i = dd % 16
        f = np.float32(10000.0) ** (-np.float32(i) / np.float32(16))
        ang = (row if half == 0 else col) * np.float32(f)
        out[p, 0] = np.cos(ang)
        out[p, 1] = (-np.sin(ang)) if dd < 16 else np.sin(ang)
        if d < 48:
            half, dd = d // 24, d % 24
            i = dd % 12
            f = np.float32(10000.0) ** (-np.float32(i) / np.float32(12))
            ang = (row if half == 0 else col) * np.float32(f)
            out[p, 2] = np.cos(ang)
            out[p, 3] = (-np.sin(ang)) if dd < 12 else np.sin(ang)
        else:
            out[p, 2] = 1.0
            out[p, 3] = 0.0
    return out


def pack_pvec(inp, nl):
    pv = np.zeros((128, nl, NV), np.float32)
    p = np.arange(128)
    for l in range(nl):
        pv[:, l, 0:24] = inp["g_norm"][l].reshape(3, 8, 128).transpose(2, 0, 1).reshape(128, 24)
        pv[:, l, 24:96] = inp["b_ada"][l].reshape(72, 128).T
        pv[:, l, 96] = inp["g_a_q"][l][p % 64]
        pv[:, l, 97] = inp["g_a_k"][l][p % 64]
        for col, key in ((98, "g_c_q"), (99, "g_c_k")):
            g = np.zeros((2, 64), np.float32)
            g[:, 0:48] = inp[key][l]
            pv[:, l, col] = g.reshape(128)
        pv[:, l, 100] = inp["g_gla"][l][p % 64]
        pv[0:96, l, 101] = inp["g_c_out"][l]
    return pv


_NC_CACHE = {}


def kernel(**inp):
    return run(inp, L_FULL)


def run(inp, nl, trace=False):
    inp = {k: np.asarray(v) for k, v in inp.items()}
    if nl not in _NC_CACHE:
        _NC_CACHE[nl] = build(nl)
    nc = _NC_CACHE[nl]
    wst = pack_weights(inp, nl)
    cm, ur, hm, bd = make_consts()
    pv = pack_pvec(inp, nl)
    bgla = np.broadcast_to(inp["b_gla"][:nl].reshape(1, nl, 256), (128, nl, 256)).copy()
    w2 = np.zeros((32, nl, 256), np.float32)
    for l in range(nl):
        w2[0:16, l, 0:128] = inp["w_gla_up"][l, 0]
        w2[16:32, l, 128:256] = inp["w_gla_up"][l, 1]
    lamc = np.broadcast_to(inp["lam_c"][:nl].reshape(1, nl * 4 * 48), (128, nl * 4 * 48)).copy()
    ozc = np.zeros((128, 2, 512), np.float32)
    ozc[:, 1, :] = 1.0
    in_maps = []
    for c in range(8):
        b, r = c // 4, c % 4
        xp = inp["x_prompt"][4 * c:4 * c + 4].reshape(1024, D)
        xs = inp["x_sample"][b, r * 512:(r + 1) * 512]
        xt = np.concatenate([xp, xs], axis=0)
        xin = xt.reshape(NTOK, 8, 128).transpose(2, 1, 0).copy()
        cond = np.stack([inp["c_ctx"], inp["c"][b]], axis=1).reshape(8, 128, 2).transpose(1, 0, 2).copy()
        rm = np.zeros((128, 8), np.float32)
        for rr in range(4):
            rm[:, rr] = 1.0 if rr < r else 0.0
            rm[:, 4 + rr] = 1.0 if rr > r else 0.0
        cak = inp["cache_a_k"][b, :nl].transpose(0, 2, 3, 1).copy()
        cav = inp["cache_a_v"][b, :nl].reshape(nl, 512, 128).copy()
        ck = inp["cache_c_k"][b, :nl].reshape(nl, 512, 4, 2, 48)
        cck = np.zeros((nl, 4, 2, 64, 512), np.float32)
        cck[:, :, :, 0:48, :] = ck.transpose(0, 2, 3, 4, 1)
        cck = cck.reshape(nl, 4, 128, 512)
        ccv = inp["cache_c_v"][b, :nl].reshape(nl, 512, 384).copy()
        sg = inp["state_gla"][b, :nl].reshape(nl, 2, 128, 64).copy()
        in_maps.append(dict(wst=wst, xin=xin, cmat=cm, urow=ur, rope=rope_tables(r * 512), hmask=hm, bdmask=bd, pvec=pv, oz=ozc,
                            bgla=bgla, w2=w2, lamc=lamc, condT=cond, rmask=rm, cakT=cak, cav=cav, cckT=cck, ccv=ccv, sgla=sg))
    if trace:
        res = run_bass_kernel_spmd(nc, in_maps, core_ids=list(range(8)), trace=True)
        print("exec_time_ns", res.exec_time_ns)
    else:
        res = run_bass_kernel_spmd(nc, in_maps, core_ids=list(range(8)))
    return assemble(res.results, nl)


def assemble(results, nl):
    y_prompt = np.zeros((32, 256, D), np.float32)
    y_sample = np.zeros((2, 2048, D), np.float32)
    n_ak = np.zeros((32, nl, 256, 2, 64), np.float32)
    n_av = np.zeros((32, nl, 256, 2, 64), np.float32)
    n_ck = np.zeros((32, nl, 256, 4, 96), np.float32)
    n_cv = np.zeros((32, nl, 256, 4, 96), np.float32)
    n_st = np.zeros((32, nl, 2, 4, 32, 64), np.float32)
    for c in range(8):
        r = results[c]
        b, rk = c // 4, c % 4
        y = np.asarray(r["yT"]).transpose(2, 1, 0).reshape(NTOK, D)
        y_prompt[4 * c:4 * c + 4] = y[0:1024].reshape(4, 256, D)
        y_sample[b, rk * 512:(rk + 1) * 512] = y[1024:]
        ak = np.asarray(r["akT"])
        n_ak[4 * c:4 * c + 4] = ak.reshape(nl, 2, 64, 4, 256).transpose(3, 0, 4, 1, 2)
        av = np.asarray(r["av"])
        n_av[4 * c:4 * c + 4] = av.reshape(nl, 4, 256, 2, 64).transpose(1, 0, 2, 3, 4)
        ck = np.asarray(r["ckT"]).reshape(nl, 4, 2, 64, 4, 256)[:, :, :, 0:48]
        n_ck[4 * c:4 * c + 4] = ck.transpose(4, 0, 5, 1, 2, 3).reshape(4, nl, 256, 4, 96)
        cv = np.asarray(r["cv"])
        n_cv[4 * c:4 * c + 4] = cv.reshape(nl, 4, 256, 4, 96).transpose(1, 0, 2, 3, 4)
        st = np.asarray(r["st"])
        n_st[4 * c:4 * c + 4] = st.reshape(nl, 4, 2, 4, 32, 64).transpose(1, 0, 2, 3, 4, 5)
    return (y_prompt, y_sample, n_ak, n_av, n_ck, n_cv, n_st)
```
